# Optimizing a Trainium2 kernel written in Bass

```python
import math
import jax, jax.numpy as jnp
from jax import lax
import numpy as np

D_MODEL = 2048
BATCH = 4
SEQ = 2048
DEPTH = 4
DEC_BATCH = 8
DEC_SEQ = 4
PAST_LEN = 16384
PAGE_SIZE = 128

N_MIXERS = 3
N_A_LAYERS = (DEPTH + 2) // 3
N_B_LAYERS = (DEPTH + 1) // 3
N_C_LAYERS = DEPTH // 3
SSM_GROUP = 16
SSM_GROUPS = D_MODEL // SSM_GROUP
SSM_STATE = 64
DT_MIN = 1e-3
DT_MAX = 1e-1
CONV_W = 3
N_HEADS = 16
HEAD_DIM = D_MODEL // N_HEADS
Q_BLOCK = 128
SB_BIAS_INIT = -6.0
D_FF = 4 * D_MODEL
EPS = 1e-6

kernel_name = "hybrid_s5_shortconv_stickbreaking_step"


def rms_norm(x, g):
    x32 = x.astype(jnp.float32)
    y = x32 * lax.rsqrt(jnp.mean(x32 * x32, axis=-1, keepdims=True) + EPS)
    return (y * g.astype(jnp.float32)).astype(x.dtype)


def _scan_combine(e1, e2):
    a1, b1 = e1
    a2, b2 = e2
    return a1 * a2, a2 * b1 + b2


def s5_mixer(u, h0_re, h0_im, lam_re, lam_im, log_dt, b_re, b_im, c_re, c_im, d, w_glu):
    f32 = jnp.float32
    n, t, _ = u.shape
    lam = lax.complex(lam_re.astype(f32), lam_im.astype(f32))
    dt = jnp.exp(log_dt.astype(f32))[:, None]
    a_bar = jnp.exp(lam * dt)
    b = lax.complex(b_re.astype(f32), b_im.astype(f32))
    b_bar = ((a_bar - 1.0) / lam)[..., None] * b
    u32 = u.astype(f32)
    ug = u32.reshape(n, t, SSM_GROUPS, SSM_GROUP).astype(jnp.complex64)
    bu = jnp.einsum("ntgc,gpc->ntgp", ug, b_bar)
    a_elems = jnp.broadcast_to(a_bar[None, None], (1, t) + a_bar.shape)
    a_cum, h = lax.associative_scan(_scan_combine, (a_elems, bu), axis=1)
    h0 = lax.complex(h0_re.astype(f32), h0_im.astype(f32))[:, None]
    h = h + a_cum * h0
    c = lax.complex(c_re.astype(f32), c_im.astype(f32))
    y = jnp.real(jnp.einsum("ntgp,gcp->ntgc", h, c)).reshape(n, t, D_MODEL)
    y = y + d.astype(f32) * u32
    z = jax.nn.gelu(y).astype(u.dtype)
    a_half, g_half = jnp.split(z @ w_glu, 2, axis=-1)
    out = a_half * jax.nn.sigmoid(g_half)
    h_last = h[:, -1]
    return out, jnp.real(h_last), jnp.imag(h_last)


def short_conv_mixer(x, buf, w_in, w_dw, w_out):
    t = x.shape[1]
    gate_b, gate_c, v = jnp.split(x @ w_in, 3, axis=-1)
    cv = gate_c * v
    full = jnp.concatenate([buf.astype(cv.dtype), cv], axis=1)
    conv = sum(full[:, k:k + t] * w_dw[k] for k in range(CONV_W))
    y = (gate_b * conv) @ w_out
    return y, full[:, -(CONV_W - 1):]


def qkv_heads(x, w_qkv, q_gain, k_gain):
    n, t, _ = x.shape
    qkv = (x @ w_qkv).reshape(n, t, 3, N_HEADS, HEAD_DIM)
    q = rms_norm(qkv[:, :, 0], q_gain)
    k = rms_norm(qkv[:, :, 1], k_gain)
    return q, k, qkv[:, :, 2]


def stick_breaking_weights(z, mask):
    z = z.astype(jnp.float32)
    log_1m = jnp.where(mask, jax.nn.log_sigmoid(-z), 0.0)
    suffix = lax.cumsum(log_1m, axis=z.ndim - 1, reverse=True) - log_1m
    return jnp.where(mask, jnp.exp(jax.nn.log_sigmoid(z) + suffix), 0.0)


def sb_attention_prompt(x, w_qkv, q_gain, k_gain, sb_bias, w_o):
    n, t, _ = x.shape
    q, k, v = qkv_heads(x, w_qkv, q_gain, k_gain)
    scale = HEAD_DIM ** -0.5
    bias = sb_bias.astype(jnp.float32)[None, :, None, None]
    n_blk = t // Q_BLOCK
    q_blocks = q.reshape(n, n_blk, Q_BLOCK, N_HEADS, HEAD_DIM).transpose(1, 0, 2, 3, 4)
    key_pos = jnp.arange(t)

    def one_block(args):
        qb, blk = args
        q_pos = blk * Q_BLOCK + jnp.arange(Q_BLOCK)
        z = jnp.einsum("nqhd,nkhd->nhqk", qb, k).astype(jnp.float32) * scale + bias
        mask = key_pos[None, :] < q_pos[:, None]
        a = stick_breaking_weights(z, mask).astype(v.dtype)
        return jnp.einsum("nhqk,nkhd->nqhd", a, v)

    o = lax.map(one_block, (q_blocks, jnp.arange(n_blk)))
    o = o.transpose(1, 0, 2, 3, 4).reshape(n, t, D_MODEL)
    return o @ w_o, k, v


def sb_attention_sample(x, k_past, v_past, w_qkv, q_gain, k_gain, sb_bias, w_o):
    n, t, _ = x.shape
    p = k_past.shape[1]
    q, k, v = qkv_heads(x, w_qkv, q_gain, k_gain)
    scale = HEAD_DIM ** -0.5
    bias = sb_bias.astype(jnp.float32)[None, :, None, None]
    z = jnp.concatenate([jnp.einsum("nqhd,nkhd->nhqk", q, k_past.astype(q.dtype)),
                         jnp.einsum("nqhd,nkhd->nhqk", q, k)], axis=-1)
    z = z.astype(jnp.float32) * scale + bias
    pos = jnp.arange(t)
    mask = jnp.concatenate([jnp.ones((t, p), dtype=bool), pos[None, :] < pos[:, None]], axis=1)
    a = stick_breaking_weights(z, mask).astype(v.dtype)
    o = (jnp.einsum("nhqk,nkhd->nqhd", a[..., :p], v_past.astype(v.dtype))
         + jnp.einsum("nhqk,nkhd->nqhd", a[..., p:], v))
    return o.reshape(n, t, D_MODEL) @ w_o, k, v


def sq_relu_mlp(x, w_up, w_down):
    return jnp.square(jax.nn.relu(x @ w_up)) @ w_down


def setup_inputs(seed: int = 0) -> dict:
    key = jax.random.key(seed)
    ks = jax.random.split(key, 32)
    f32 = jnp.float32
    n_pages = PAST_LEN // PAGE_SIZE
    n_pool = (DEC_BATCH * n_pages * 5) // 4
    nrm = lambda k, s, sc: jax.random.normal(k, s, f32) * sc

    x_prompt = nrm(ks[0], (BATCH, SEQ, D_MODEL), 1.0)
    x_sample = nrm(ks[1], (DEC_BATCH, DEC_SEQ, D_MODEL), 1.0)
    state_ssm_re = nrm(ks[2], (N_A_LAYERS, DEC_BATCH, SSM_GROUPS, SSM_STATE), 0.3)
    state_ssm_im = nrm(ks[3], (N_A_LAYERS, DEC_BATCH, SSM_GROUPS, SSM_STATE), 0.3)
    state_conv = nrm(ks[4], (N_B_LAYERS, DEC_BATCH, CONV_W - 1, D_MODEL), 1.0)
    cache_k = nrm(ks[5], (N_C_LAYERS, n_pool, PAGE_SIZE, N_HEADS, HEAD_DIM), 1.0)
    cache_v = nrm(ks[6], (N_C_LAYERS, n_pool, PAGE_SIZE, N_HEADS, HEAD_DIM), 1.0)
    page_table = jax.random.permutation(ks[7], n_pool)[: DEC_BATCH * n_pages]
    page_table = page_table.reshape(DEC_BATCH, n_pages).astype(jnp.int32)

    norm_mix = 1.0 + nrm(ks[8], (DEPTH, D_MODEL), 0.02)
    norm_mlp = 1.0 + nrm(ks[9], (DEPTH, D_MODEL), 0.02)

    ssm_lambda_re = -0.5 + nrm(ks[10], (N_A_LAYERS, SSM_GROUPS, SSM_STATE), 0.01)
    ssm_lambda_im = (jnp.pi * jnp.arange(SSM_STATE, dtype=f32)
                     + nrm(ks[11], (N_A_LAYERS, SSM_GROUPS, SSM_STATE), 0.01))
    ssm_log_dt = jax.random.uniform(ks[12], (N_A_LAYERS, SSM_GROUPS), f32,
                                    minval=math.log(DT_MIN), maxval=math.log(DT_MAX))
    ssm_b_re = nrm(ks[13], (N_A_LAYERS, SSM_GROUPS, SSM_STATE, SSM_GROUP), (2 * SSM_GROUP) ** -0.5)
    ssm_b_im = nrm(ks[14], (N_A_LAYERS, SSM_GROUPS, SSM_STATE, SSM_GROUP), (2 * SSM_GROUP) ** -0.5)
    ssm_c_re = nrm(ks[15], (N_A_LAYERS, SSM_GROUPS, SSM_GROUP, SSM_STATE), (2 * SSM_STATE) ** -0.5)
    ssm_c_im = nrm(ks[16], (N_A_LAYERS, SSM_GROUPS, SSM_GROUP, SSM_STATE), (2 * SSM_STATE) ** -0.5)
    ssm_d = nrm(ks[17], (N_A_LAYERS, D_MODEL), 1.0)
    ssm_w_glu = nrm(ks[18], (N_A_LAYERS, D_MODEL, 2 * D_MODEL), D_MODEL ** -0.5)

    conv_w_in = nrm(ks[19], (N_B_LAYERS, D_MODEL, 3 * D_MODEL), D_MODEL ** -0.5)
    conv_w_dw = nrm(ks[20], (N_B_LAYERS, CONV_W, D_MODEL), CONV_W ** -0.5)
    conv_w_out = nrm(ks[21], (N_B_LAYERS, D_MODEL, D_MODEL), D_MODEL ** -0.5)

    attn_w_qkv = nrm(ks[22], (N_C_LAYERS, D_MODEL, 3 * D_MODEL), D_MODEL ** -0.5)
    attn_q_norm = 1.0 + nrm(ks[23], (N_C_LAYERS, HEAD_DIM), 0.02)
    attn_k_norm = 1.0 + nrm(ks[24], (N_C_LAYERS, HEAD_DIM), 0.02)
    attn_sb_bias = SB_BIAS_INIT + nrm(ks[28], (N_C_LAYERS, N_HEADS), 0.1)
    attn_w_o = nrm(ks[25], (N_C_LAYERS, D_MODEL, D_MODEL), D_MODEL ** -0.5)

    mlp_w_up = nrm(ks[26], (DEPTH, D_MODEL, D_FF), D_MODEL ** -0.5)
    mlp_w_down = nrm(ks[27], (DEPTH, D_FF, D_MODEL), D_FF ** -0.5)

    return {
        "x_prompt": x_prompt, "x_sample": x_sample,
        "state_ssm_re": state_ssm_re, "state_ssm_im": state_ssm_im,
        "state_conv": state_conv, "cache_k": cache_k, "cache_v": cache_v,
        "page_table": page_table,
        "norm_mix": norm_mix, "norm_mlp": norm_mlp,
        "ssm_lambda_re": ssm_lambda_re, "ssm_lambda_im": ssm_lambda_im,
        "ssm_log_dt": ssm_log_dt, "ssm_b_re": ssm_b_re, "ssm_b_im": ssm_b_im,
        "ssm_c_re": ssm_c_re, "ssm_c_im": ssm_c_im, "ssm_d": ssm_d, "ssm_w_glu": ssm_w_glu,
        "conv_w_in": conv_w_in, "conv_w_dw": conv_w_dw, "conv_w_out": conv_w_out,
        "attn_w_qkv": attn_w_qkv, "attn_q_norm": attn_q_norm, "attn_k_norm": attn_k_norm,
        "attn_sb_bias": attn_sb_bias, "attn_w_o": attn_w_o,
        "mlp_w_up": mlp_w_up, "mlp_w_down": mlp_w_down,
    }


def reference(x_prompt, x_sample, state_ssm_re, state_ssm_im, state_conv, cache_k, cache_v,
              page_table, norm_mix, norm_mlp,
              ssm_lambda_re, ssm_lambda_im, ssm_log_dt, ssm_b_re, ssm_b_im, ssm_c_re, ssm_c_im,
              ssm_d, ssm_w_glu, conv_w_in, conv_w_dw, conv_w_out,
              attn_w_qkv, attn_q_norm, attn_k_norm, attn_sb_bias, attn_w_o,
              mlp_w_up, mlp_w_down):
    n_p = x_prompt.shape[0]
    n_s = x_sample.shape[0]
    y_p, y_s = x_prompt, x_sample
    ssm_re_p, ssm_im_p, ssm_re_s, ssm_im_s = [], [], [], []
    conv_p, conv_s = [], []
    k_p, v_p, k_s, v_s = [], [], [], []
    for i in range(DEPTH):
        kind = i % N_MIXERS
        j = i // N_MIXERS
        hp = rms_norm(y_p, norm_mix[i])
        hs = rms_norm(y_s, norm_mix[i])
        if kind == 0:
            ssm_w = (ssm_lambda_re[j], ssm_lambda_im[j], ssm_log_dt[j], ssm_b_re[j], ssm_b_im[j],
                     ssm_c_re[j], ssm_c_im[j], ssm_d[j], ssm_w_glu[j])
            zeros = jnp.zeros((n_p, SSM_GROUPS, SSM_STATE), jnp.float32)
            mp, re_p, im_p = s5_mixer(hp, zeros, zeros, *ssm_w)
            ms, re_s, im_s = s5_mixer(hs, state_ssm_re[j], state_ssm_im[j], *ssm_w)
            ssm_re_p.append(re_p); ssm_im_p.append(im_p)
            ssm_re_s.append(re_s); ssm_im_s.append(im_s)
        elif kind == 1:
            zbuf = jnp.zeros((n_p, CONV_W - 1, D_MODEL), hp.dtype)
            mp, bp = short_conv_mixer(hp, zbuf, conv_w_in[j], conv_w_dw[j], conv_w_out[j])
            ms, bs = short_conv_mixer(hs, state_conv[j], conv_w_in[j], conv_w_dw[j], conv_w_out[j])
            conv_p.append(bp); conv_s.append(bs)
        else:
            mp, kp, vp = sb_attention_prompt(hp, attn_w_qkv[j], attn_q_norm[j], attn_k_norm[j],
                                             attn_sb_bias[j], attn_w_o[j])
            k_past = cache_k[j][page_table].reshape(n_s, -1, N_HEADS, HEAD_DIM)
            v_past = cache_v[j][page_table].reshape(n_s, -1, N_HEADS, HEAD_DIM)
            ms, ks_new, vs_new = sb_attention_sample(hs, k_past, v_past, attn_w_qkv[j],
                                                     attn_q_norm[j], attn_k_norm[j],
                                                     attn_sb_bias[j], attn_w_o[j])
            k_p.append(kp); v_p.append(vp); k_s.append(ks_new); v_s.append(vs_new)
        y_p = y_p + mp
        y_s = y_s + ms
        y_p = y_p + sq_relu_mlp(rms_norm(y_p, norm_mlp[i]), mlp_w_up[i], mlp_w_down[i])
        y_s = y_s + sq_relu_mlp(rms_norm(y_s, norm_mlp[i]), mlp_w_up[i], mlp_w_down[i])
    return (y_p, y_s,
            jnp.stack(ssm_re_p), jnp.stack(ssm_im_p), jnp.stack(ssm_re_s), jnp.stack(ssm_im_s),
            jnp.stack(conv_p), jnp.stack(conv_s),
            jnp.stack(k_p), jnp.stack(v_p), jnp.stack(k_s), jnp.stack(v_s))
```

```python
import numpy as np
import ml_dtypes
from contextlib import ExitStack
import concourse.bass as bass
import concourse.mybir as mybir
from concourse.bass_utils import run_bass_kernel_spmd

F32 = mybir.dt.float32
BF16 = mybir.dt.bfloat16
I32 = mybir.dt.int32
AF = mybir.ActivationFunctionType
ALU = mybir.AluOpType
AX = mybir.AxisListType

D = 2048
DC = 16
SEQ = 2048
TT = 512
NTILES = SEQ // TT
NS = 4
NT = TT + NS
FF = 8192
EPS = 1e-6
DEPTH = 4
NH = 16
PAST = 16384
NPAGES = 128
LCH = 8
NCH = TT // LCH
SW = 256


class DSem:
    def __init__(self, h):
        self.h = h
        self.count = 0


class Sched:
    ENG = ("pe", "act", "dve", "pool", "sp")

    def __init__(self, nc, es):
        self.nc = nc
        self.es = es
        self.ops = {e: [] for e in self.ENG}
        self.sem = {e: es.enter_context(nc.semaphore("c_" + e)) for e in self.ENG}
        self.cnt = {e: 0 for e in self.ENG}
        self.waited = {e: {} for e in self.ENG}
        self.lastw = {}
        self.readers = {}
        self.nsem = 0

    def dsem(self, name=None):
        self.nsem += 1
        return DSem(self.es.enter_context(self.nc.semaphore(name or ("d%d" % self.nsem))))

    def _deps(self, eng, r, w):
        need = {}

        def add(sv):
            if sv is None:
                return
            s, v = sv
            k = s.name
            if k not in need or need[k][1] < v:
                need[k] = (s, v)
        for t in r:
            add(self.lastw.get(t))
        for t in w:
            add(self.lastw.get(t))
            for sv in self.readers.get(t, ()):
                add(sv)
        out = []
        wd = self.waited[eng]
        for k, (s, v) in need.items():
            if wd.get(k, 0) < v:
                wd[k] = v
                out.append((s, v))
        return out

    def _mark(self, r, w, sv):
        for t in w:
            self.lastw[t] = sv
            self.readers[t] = []
        for t in r:
            self.readers.setdefault(t, []).append(sv)
            if len(self.readers[t]) > 12:
                best = {}
                for s, v in self.readers[t]:
                    if s.name not in best or best[s.name][1] < v:
                        best[s.name] = (s, v)
                self.readers[t] = list(best.values())

    def op(self, eng, fn, r=(), w=()):
        waits = self._deps(eng, r, w)
        self.cnt[eng] += 1
        sem = self.sem[eng]
        sv = (sem, self.cnt[eng])

        def emit(e, fn=fn, waits=waits, sem=sem):
            for s, v in waits:
                e.wait_ge(s, v)
            fn(e).then_inc(sem, 1)
        self.ops[eng].append(emit)
        self._mark(r, w, sv)

    def group(self, eng, fns, r=(), w=()):
        waits = self._deps(eng, r, w)
        self.cnt[eng] += 1
        sem = self.sem[eng]
        sv = (sem, self.cnt[eng])

        def emit(e, fns=fns, waits=waits, sem=sem):
            for s, v in waits:
                e.wait_ge(s, v)
            for f in fns[:-1]:
                f(e)
            fns[-1](e).then_inc(sem, 1)
        self.ops[eng].append(emit)
        self._mark(r, w, sv)

    def dma(self, q, fn, ds, r=(), w=()):
        waits = self._deps(q, r, w)
        ds.count += 16
        sv = (ds.h, ds.count)

        def emit(e, fn=fn, waits=waits, h=ds.h):
            for s, v in waits:
                e.wait_ge(s, v)
            fn(e).then_inc(h, 16)
        self.ops[q].append(emit)
        self._mark(r, w, sv)

    def final_wait(self, eng, dsems):
        def emit(e, dsems=dsems):
            for d in dsems:
                if d.count:
                    e.wait_ge(d.h, d.count)
        self.ops[eng].append(emit)

    def flush(self):
        with self.nc.Block() as block:
            ops = self.ops

            @block.tensor
            def _(e):
                for f in ops["pe"]:
                    f(e)

            @block.scalar
            def _(e):
                for f in ops["act"]:
                    f(e)

            @block.vector
            def _(e):
                for f in ops["dve"]:
                    f(e)

            @block.gpsimd
            def _(e):
                for f in ops["pool"]:
                    f(e)

            @block.sync
            def _(e):
                for f in ops["sp"]:
                    f(e)


class Builder:
    def __init__(self, stage=99, dump=None):
        self.stage = stage
        self.dump = dump
        self.nc = bass.Bass("TRN2", target_bir_lowering=False)
        self.es = ExitStack()
        self.S = None
        self.ins = {}
        self.outs = {}
        self.out_sems = []

    def din(self, name, shape, dt=F32):
        ap = self.nc.dram_tensor(name, list(shape), dt, kind="ExternalInput").ap()
        self.ins[name] = ap
        return ap

    def dout(self, name, shape, dt=F32):
        ap = self.nc.dram_tensor(name, list(shape), dt, kind="ExternalOutput").ap()
        self.outs[name] = ap
        return ap

    def sb(self, name, shape, dt):
        return self.es.enter_context(self.nc.sbuf_tensor(name, list(shape), dt))

    def ps(self, name, shape, dt=F32):
        return self.es.enter_context(self.nc.psum_tensor(name, list(shape), dt))

    def bank(self):
        k = self.bank_i % len(self.banks)
        self.bank_i += 1
        return k, self.banks[k], ("ps", k)

    def load_slab(self, wap, r0, c0, ncols=SW):
        S = self.S
        k = self.w_i % len(self.wbuf)
        self.w_i += 1
        buf = self.wbuf[k]
        src = wap[r0:r0 + 2048, c0:c0 + ncols].rearrange("(kc p) n -> p kc n", p=128)
        S.dma("pool", lambda e, buf=buf, src=src, ncols=ncols: e.dma_start(out=buf[:, :, 0:ncols], in_=src),
              self.wsem[k], w=[("w", k)])
        return buf, ("w", k)

    def ntiles(self, ncols):
        out = []
        n0 = 0
        while n0 < ncols:
            n = min(512, ncols - n0)
            out.append((n0, n))
            n0 += n
        return out

    def linear_fm(self, wap, r0, c0, nout_chunks, src, src_key, ncols, evac):
        S = self.S
        per = SW // 128
        for s0 in range(0, nout_chunks, per):
            buf, wkey = self.load_slab(wap, r0, c0 + s0 * 128, SW)
            for mi in range(per):
                m = s0 + mi
                for (n0, n) in self.ntiles(ncols):
                    bk, psb, pskey = self.bank()
                    fns = []
                    for kc in range(16):
                        fns.append(lambda e, psb=psb, buf=buf, kc=kc, mi=mi, n0=n0, n=n, src=src:
                                   e.matmul(psb[:, 0:n], buf[:, kc, mi * 128:(mi + 1) * 128], src[:, kc, n0:n0 + n],
                                            start=(kc == 0), stop=(kc == 15)))
                    S.group("pe", fns, r=[wkey] + [(src_key, kc) for kc in range(16)], w=[pskey])
                    evac(m, n0, n, psb[:, 0:n], pskey)

    def rmsnorm(self, gcol, ncols):
        S = self.S
        R, xn, rstd = self.R, self.xn, self.rstd
        nts = self.ntiles(ncols)
        pbanks = [self.bank() for _ in nts]
        for c in range(16):
            sq = self.sq[0]
            sk = ("sq", 0)
            S.op("act", lambda e, sq=sq, c=c: e.activation(sq[:, 0:ncols], R[:, c, 0:ncols], AF.Square),
                 r=[("R", c)], w=[sk])
            for (n0, n), (bk, psb, pskey) in zip(nts, pbanks):
                S.op("pe", lambda e, psb=psb, sq=sq, n0=n0, n=n, c=c:
                     e.matmul(psb[:, 0:n], self.ones_f[:], sq[:, n0:n0 + n], start=(c == 0), stop=(c == 15)),
                     r=[sk, "const"], w=[pskey])
        for (n0, n), (bk, psb, pskey) in zip(nts, pbanks):
            S.op("act", lambda e, psb=psb, n0=n0, n=n: e.activation(rstd[:, n0:n0 + n], psb[:, 0:n], AF.Sqrt,
                                                                      bias=self.eps_t[:, 0:1], scale=1.0 / D),
                 r=[pskey, "const"], w=["rstd"])
        S.op("dve", lambda e: e.reciprocal(rstd[:, 0:ncols], rstd[:, 0:ncols]), r=["rstd"], w=["rstd"])
        for c in range(16):
            S.op("dve", lambda e, c=c: e.scalar_tensor_tensor(xn[:, c, 0:ncols], R[:, c, 0:ncols],
                                                           self.gv[:, gcol + c:gcol + c + 1], rstd[:, 0:ncols],
                                                           ALU.mult, ALU.mult),
                 r=[("R", c), "rstd", "gv"], w=[("xn", c)])

    def mlp(self, li, ncols):
        S = self.S
        R, xn, hid = self.R, self.xn, self.hid
        w_up = self.ins["mlp_w_up"]
        w_dn = self.ins["mlp_w_down"]
        for q in range(4):
            def evac_up(m, n0, n, ps, pskey):
                t = self.tmpb[self.tmp_i % 2]
                tk = ("tmpb", self.tmp_i % 2)
                self.tmp_i += 1
                S.op("act", lambda e, t=t, ps=ps, n=n: e.activation(t[:, 0:n], ps, AF.Relu), r=[pskey], w=[tk])
                S.op("pool", lambda e, t=t, m=m, n0=n0, n=n: e.tensor_tensor(hid[:, m, n0:n0 + n], t[:, 0:n], t[:, 0:n], ALU.mult),
                     r=[tk], w=[("hid", m)])
            self.linear_fm(w_up[li], 0, q * 2048, 16, xn, "xn", ncols, evac_up)

            def evac_dn(m, n0, n, ps, pskey):
                S.op("dve", lambda e, m=m, n0=n0, n=n, ps=ps: e.tensor_tensor(R[:, m, n0:n0 + n], ps, R[:, m, n0:n0 + n], ALU.add),
                     r=[pskey, ("R", m)], w=[("R", m)])
            self.linear_fm(w_dn[li], q * 2048, 0, 16, hid, "hid", ncols, evac_dn)

    def load_tile(self, tile, with_sample):
        S = self.S
        R = self.R
        xp = self.ins["xp"]
        for tb in range(TT // 128):
            st = self.stage_f[0]
            sk = ("stage", 0)
            t0 = tile * TT + tb * 128
            S.dma("sp", lambda e, st=st, t0=t0: e.dma_start(out=st[:, :], in_=xp[t0:t0 + 128, :]), self.ssem[0], w=[sk])
            for g4 in range(4):
                bk, psb, pskey = self.bank()
                fns = []
                for j in range(4):
                    c = g4 * 4 + j
                    fns.append(lambda e, psb=psb, st=st, c=c, j=j: e.transpose(psb[:, j * 128:(j + 1) * 128], st[:, c * 128:(c + 1) * 128], self.ident_f[:]))
                S.group("pe", fns, r=[sk, "const"], w=[pskey])
                eng = "act" if g4 % 2 == 0 else "dve"
                if eng == "act":
                    S.op("act", lambda e, psb=psb, g4=g4, tb=tb: e.activation(
                        R[:, g4 * 4:(g4 + 1) * 4, tb * 128:(tb + 1) * 128], psb[:, 0:512].rearrange("p (j n) -> p j n", j=4), AF.Copy),
                        r=[pskey], w=[("R", g4 * 4 + j) for j in range(4)])
                else:
                    S.op("dve", lambda e, psb=psb, g4=g4, tb=tb: e.tensor_copy(
                        R[:, g4 * 4:(g4 + 1) * 4, tb * 128:(tb + 1) * 128], psb[:, 0:512].rearrange("p (j n) -> p j n", j=4)),
                        r=[pskey], w=[("R", g4 * 4 + j) for j in range(4)])
        if with_sample:
            xs = self.ins["xs"]
            st = self.stage_f[0]
            sk = ("stage", 0)
            S.dma("sp", lambda e, st=st: e.dma_start(out=st[0:NS, :], in_=xs[:, :]), self.ssem[0], w=[sk])
            bk, psb, pskey = self.bank()
            fns = []
            for c in range(16):
                fns.append(lambda e, psb=psb, st=st, c=c: e.transpose(psb[:, c * NS:(c + 1) * NS], st[0:NS, c * 128:(c + 1) * 128], self.ident_f[0:NS, 0:NS]))
            S.group("pe", fns, r=[sk, "const"], w=[pskey])
            S.op("dve", lambda e, psb=psb: e.tensor_copy(R[:, :, TT:TT + NS], psb[:, 0:16 * NS].rearrange("p (c n) -> p c n", c=16)),
                 r=[pskey], w=[("R", c) for c in range(16)])

    def store_tile(self, tile, with_sample):
        S = self.S
        R = self.R
        yp = self.outs["yp"]
        for tb in range(TT // 128):
            st = self.stage_f[0]
            sk = ("stage", 0)
            for g4 in range(4):
                bk, psb, pskey = self.bank()
                fns = []
                for j in range(4):
                    c = g4 * 4 + j
                    fns.append(lambda e, psb=psb, c=c, j=j, tb=tb: e.transpose(psb[:, j * 128:(j + 1) * 128], R[:, c, tb * 128:(tb + 1) * 128], self.ident_f[:]))
                S.group("pe", fns, r=[("R", g4 * 4 + j) for j in range(4)] + ["const"], w=[pskey])
                if g4 % 2 == 0:
                    S.op("act", lambda e, psb=psb, st=st, g4=g4: e.activation(st[:, g4 * 512:(g4 + 1) * 512], psb[:, 0:512], AF.Copy), r=[pskey], w=[sk])
                else:
                    S.op("dve", lambda e, psb=psb, st=st, g4=g4: e.tensor_copy(st[:, g4 * 512:(g4 + 1) * 512], psb[:, 0:512]), r=[pskey], w=[sk])
            t0 = tile * TT + tb * 128
            S.dma("sp", lambda e, st=st, t0=t0: e.dma_start(out=yp[t0:t0 + 128, :], in_=st[:, :]), self.osem, r=[sk])
        if with_sample:
            ys = self.outs["ys"]
            st = self.stage_f[0]
            sk = ("stage", 0)
            for g4 in range(4):
                bk, psb, pskey = self.bank()
                fns = []
                for j in range(4):
                    c = g4 * 4 + j
                    fns.append(lambda e, psb=psb, c=c, j=j: e.transpose(psb[0:NS, j * 128:(j + 1) * 128], R[:, c, TT:TT + NS], self.ident_f[:]))
                S.group("pe", fns, r=[("R", g4 * 4 + j) for j in range(4)] + ["const"], w=[pskey])
                S.op("dve", lambda e, psb=psb, st=st, g4=g4: e.tensor_copy(st[0:NS, g4 * 512:(g4 + 1) * 512], psb[0:NS, 0:512]), r=[pskey], w=[sk])
            S.dma("sp", lambda e, st=st: e.dma_start(out=ys[:, :], in_=st[0:NS, :]), self.osem, r=[sk])


    def load_cols(self, src_rows, k, dst_fn, wkeys, q="sp"):
        S = self.S
        st = self.stage_f[0]
        sk = ("stage", 0)
        S.dma(q, lambda e: e.dma_start(out=st[0:k, :], in_=src_rows), self.ssem[0], w=[sk])
        bk, psb, pskey = self.bank()
        fns = []
        for c in range(16):
            fns.append(lambda e, c=c: e.transpose(psb[:, c * k:(c + 1) * k], st[0:k, c * 128:(c + 1) * 128], self.ident_f[0:k, 0:k]))
        S.group("pe", fns, r=[sk, "const"], w=[pskey])
        S.op("dve", lambda e: e.tensor_copy(dst_fn(), psb[:, 0:16 * k].rearrange("p (c n) -> p c n", c=16)), r=[pskey], w=wkeys)

    def store_cols(self, src_fn, k, dst_rows, rkeys, osem=None):
        S = self.S
        st = self.stage_f[0]
        sk = ("stage", 0)
        for g4 in range(4):
            bk, psb, pskey = self.bank()
            fns = []
            for j in range(4):
                c = g4 * 4 + j
                fns.append(lambda e, c=c, j=j, psb=psb: e.transpose(psb[0:k, j * 128:(j + 1) * 128], src_fn(c), self.ident_f[:]))
            S.group("pe", fns, r=list(rkeys) + ["const"], w=[pskey])
            S.op("dve", lambda e, g4=g4, psb=psb: e.tensor_copy(st[0:k, g4 * 512:(g4 + 1) * 512], psb[0:k, 0:512]), r=[pskey], w=[sk])
        S.dma("sp", lambda e: e.dma_start(out=dst_rows, in_=st[0:k, :]), osem or self.osem, r=[sk])

    def evac_add_R(self):
        S = self.S
        R = self.R

        def ev(m, n0, n, ps, pskey):
            S.op("dve", lambda e: e.tensor_tensor(R[:, m, n0:n0 + n], ps, R[:, m, n0:n0 + n], ALU.add),
                 r=[pskey, ("R", m)], w=[("R", m)])
        return ev

    def conv_mixer(self, tile, ncols, ws):
        S = self.S
        F1, xn, hid = self.F1, self.xn, self.hid
        w_in = self.ins["conv_w_in"][0]
        w_out = self.ins["conv_w_out"][0]
        allF = [("F1", m) for m in range(16)]

        def col(n0):
            return 2 + n0 if n0 < TT else n0 + 4
        S.op("pool", lambda e: e.tensor_copy(F1[:, :, 0:2], self.ccarry[:, :, :]), r=["ccarry"], w=allF)
        if ws:
            S.op("pool", lambda e: e.tensor_copy(F1[:, :, TT + 2:TT + 4], self.sconv[:, :, :]), r=["sconv"], w=allF)

        def evA(m, n0, n, ps, pskey):
            c0 = col(n0)
            S.op("act", lambda e: e.activation(F1[:, m, c0:c0 + n], ps, AF.Copy), r=[pskey], w=[("F1", m)])
        self.linear_fm(w_in, 0, 2048, 16, xn, "xn", ncols, evA)

        def evB(m, n0, n, ps, pskey):
            c0 = col(n0)
            S.op("dve", lambda e: e.tensor_tensor(F1[:, m, c0:c0 + n], ps, F1[:, m, c0:c0 + n], ALU.mult), r=[pskey, ("F1", m)], w=[("F1", m)])
        self.linear_fm(w_in, 0, 4096, 16, xn, "xn", ncols, evB)

        def evC(m, n0, n, ps, pskey):
            c0 = col(n0)
            t = self.tmpf[self.tmpf_i % 2]
            tk = ("tmpf", self.tmpf_i % 2)
            self.tmpf_i += 1
            S.op("dve", lambda e: e.tensor_scalar(t[:, 0:n], F1[:, m, c0 - 2:c0 - 2 + n], self.dwv[:, m:m + 1], None, ALU.mult), r=[("F1", m), "gv"], w=[tk])
            S.op("dve", lambda e: e.scalar_tensor_tensor(t[:, 0:n], F1[:, m, c0 - 1:c0 - 1 + n], self.dwv[:, 16 + m:17 + m], t[:, 0:n], ALU.mult, ALU.add), r=[("F1", m), tk, "gv"], w=[tk])
            S.op("dve", lambda e: e.scalar_tensor_tensor(t[:, 0:n], F1[:, m, c0:c0 + n], self.dwv[:, 32 + m:33 + m], t[:, 0:n], ALU.mult, ALU.add), r=[("F1", m), tk, "gv"], w=[tk])
            S.op("dve", lambda e: e.tensor_tensor(hid[:, m, n0:n0 + n], ps, t[:, 0:n], ALU.mult), r=[pskey, tk], w=[("hid", m)])
        self.linear_fm(w_in, 0, 0, 16, xn, "xn", ncols, evC)
        self.linear_fm(w_out, 0, 0, 16, hid, "hid", ncols, self.evac_add_R())
        S.op("pool", lambda e: e.tensor_copy(self.ccarry[:, :, :], F1[:, :, TT:TT + 2]), r=allF, w=["ccarry"])
        if ws:
            self.store_cols(lambda c: self.ccarry[:, c, :], 2, self.outs["conv_p"], ["ccarry"])
            self.store_cols(lambda c: F1[:, c, TT + 6:TT + 8], 2, self.outs["conv_s"], allF)

    def tm_proj(self, wap, c0, nslabs, tok_blocks, evac):
        S = self.S
        xn = self.xn
        for sl in range(nslabs):
            buf, wkey = self.load_slab(wap, 0, c0 + sl * SW, SW)
            for (t0, ntok) in tok_blocks:
                bk, psb, pskey = self.bank()
                fns = []
                for kc in range(16):
                    fns.append(lambda e, psb=psb, buf=buf, kc=kc, t0=t0, ntok=ntok: e.matmul(
                        psb[0:ntok, 0:SW], xn[:, kc, t0:t0 + ntok], buf[:, kc, 0:SW], start=(kc == 0), stop=(kc == 15)))
                S.group("pe", fns, r=[wkey] + [("xn", kc) for kc in range(16)], w=[pskey])
                evac(sl, t0, ntok, psb, pskey)

    def attn_block(self, zrhs, zk, N, kT_ap, kkeys, v_ap, vkeys, mask, bias_ap, carry, ckey, o_ps, okey, first, last, tri):
        S = self.S
        scale = 128.0 ** -0.5
        bk, zps, zkey = self.bank()
        S.op("pe", lambda e: e.matmul(zps[:, 0:N], kT_ap, zrhs, start=True, stop=True), r=list(kkeys) + list(zk), w=[zkey])
        i = self.ab_i % 2
        self.ab_i += 1
        E, L, Lb, T1, Ab = self.abE[0], self.abL[0], self.abLb[i], self.abT[0], self.abA[i]
        ek, lk, lbk, tk, ak = ("abE", 0), ("abL", 0), ("abLb", i), ("abT", 0), ("abA", i)
        S.op("act", lambda e: e.activation(E[:, 0:N], zps[:, 0:N], AF.Exp, bias=bias_ap, scale=scale), r=[zkey, "abias"], w=[ek])
        S.op("act", lambda e: e.activation(L[:, 0:N], E[:, 0:N], AF.Ln, bias=self.one_t[:, 0:1]), r=[ek, "const"], w=[lk])
        if mask is not None:
            S.op("dve", lambda e: e.tensor_tensor(Lb[:, 0:N], L[:, 0:N], mask, ALU.mult), r=[lk, "masks"], w=[lbk])
        else:
            S.op("dve", lambda e: e.tensor_copy(Lb[:, 0:N], L[:, 0:N]), r=[lk], w=[lbk])
        bk2, sps, skey = self.bank()
        S.op("pe", lambda e: e.matmul(sps[:, 0:N], tri, Lb[:, 0:N], start=True, stop=True), r=[lbk, "const"], w=[skey])
        bk3, cps, cpkey = self.bank()
        S.op("pe", lambda e: e.matmul(cps[:, 0:N], self.ones_b[:], Lb[:, 0:N], start=True, stop=True), r=[lbk, "const"], w=[cpkey])
        if first:
            S.op("dve", lambda e: e.tensor_scalar(T1[:, 0:N], zps[:, 0:N], scale, None, ALU.mult), r=[zkey], w=[tk])
        else:
            S.op("dve", lambda e: e.scalar_tensor_tensor(T1[:, 0:N], zps[:, 0:N], scale, carry, ALU.mult, ALU.subtract), r=[zkey, ckey], w=[tk])
        S.op("dve", lambda e: e.tensor_tensor(T1[:, 0:N], T1[:, 0:N], sps[:, 0:N], ALU.subtract), r=[tk, skey], w=[tk])
        if mask is not None:
            S.op("act", lambda e: e.activation(E[:, 0:N], T1[:, 0:N], AF.Exp, bias=bias_ap), r=[tk, "abias"], w=[ek])
            S.op("dve", lambda e: e.tensor_tensor(Ab[:, 0:N], E[:, 0:N], mask, ALU.mult), r=[ek, "masks"], w=[ak])
        else:
            S.op("act", lambda e: e.activation(Ab[:, 0:N], T1[:, 0:N], AF.Exp, bias=bias_ap), r=[tk, "abias"], w=[ak])
        if first:
            S.op("dve", lambda e: e.tensor_copy(carry, cps[:, 0:N]), r=[cpkey], w=[ckey])
        else:
            S.op("dve", lambda e: e.tensor_tensor(carry, carry, cps[:, 0:N], ALU.add), r=[cpkey, ckey], w=[ckey])
        S.op("pe", lambda e: e.matmul(o_ps, v_ap, Ab[:, 0:N], start=first, stop=last), r=[ak] + list(vkeys), w=[okey])

    def attention(self, tile, ncols, ws):
        S = self.S
        xn, hid, F1 = self.xn, self.hid, self.F1
        wqkv = self.ins["attn_w_qkv"][0]
        wo = self.ins["attn_w_o"][0]
        qT, kT, Vt = self.qT, self.kT, self.Vt
        tok_blocks = [(tb * 128, 128) for tb in range(TT // 128)] + ([(TT, NS)] if ws else [])
        kout, vout = self.outs["kp"], self.outs["vp"]

        import os
        def norm_evac(which):
            dst = qT if which == 0 else kT
            dkey = "qT" if which == 0 else "kT"
            gain = self.qgain if which == 0 else self.kgain

            def ev(sl, t0, ntok, psb, pskey):
                i = self.nq_i % 2
                self.nq_i += 1
                sqt, ssq, nrm = self.nsq[i], self.nss[i], self.nrm[i]
                NE = int(os.environ.get("NE", "9"))
                S.op("act", lambda e: e.activation(sqt[0:ntok, :], psb[0:ntok, 0:SW], AF.Square), r=[pskey], w=[("nsq", i)])
                if NE <= 1:
                    return
                S.op("dve", lambda e: e.tensor_reduce(ssq[0:ntok, :], sqt[0:ntok, :].rearrange("p (h d) -> p h d", h=2), AX.X, ALU.add), r=[("nsq", i)], w=[("nss", i)])
                S.op("act", lambda e: e.activation(ssq[0:ntok, :], ssq[0:ntok, :], AF.Sqrt, bias=self.eps_t[0:ntok, 0:1], scale=1.0 / 128), r=[("nss", i), "const"], w=[("nss", i)])
                S.op("dve", lambda e: e.reciprocal(ssq[0:ntok, :], ssq[0:ntok, :]), r=[("nss", i)], w=[("nss", i)])
                if NE <= 2:
                    return
                for h2 in range(2):
                    S.op("dve", lambda e, h2=h2: e.scalar_tensor_tensor(nrm[0:ntok, h2 * 128:(h2 + 1) * 128], psb[0:ntok, h2 * 128:(h2 + 1) * 128],
                                                                      ssq[0:ntok, h2:h2 + 1], gain[0:ntok, :], ALU.mult, ALU.mult),
                         r=[pskey, ("nss", i), "gains"], w=[("nrm", i)])
                if which == 1 and not os.environ.get("NOKDMA"):
                    dstk = self.outs["ks"][:, sl * SW:(sl + 1) * SW] if t0 >= TT else kout[tile * TT + t0:tile * TT + t0 + ntok, sl * SW:(sl + 1) * SW]
                    S.dma("sp", lambda e: e.dma_start(out=dstk, in_=nrm[0:ntok, :]), self.osem_n[i], r=[("nrm", i)])
                if NE <= 3:
                    return
                bk, tps, tkey = self.bank()
                fns = [lambda e, h2=h2: e.transpose(tps[:, h2 * 128:h2 * 128 + ntok], nrm[0:ntok, h2 * 128:(h2 + 1) * 128], self.ident_f[0:ntok, 0:ntok]) for h2 in range(2)]
                S.group("pe", fns, r=[("nrm", i), "const"], w=[tkey])
                if NE <= 4:
                    return
                for h2 in range(2):
                    h = sl * 2 + h2
                    S.op("dve", lambda e, h=h, h2=h2: e.tensor_copy(dst[:, h, t0:t0 + ntok], tps[:, h2 * 128:h2 * 128 + ntok]), r=[tkey], w=[(dkey, h)])
            return ev
        import os
        sub = int(os.environ.get("ATT_SUB", "9"))
        if sub <= 0:
            return
        self.tm_proj(wqkv, 0, 8, tok_blocks, norm_evac(0))
        if sub == 1 and os.environ.get("ATT_Q"):
            return
        self.tm_proj(wqkv, 2048, 8, tok_blocks, norm_evac(1))

        def v_evac(sl, t0, ntok, psb, pskey):
            i = self.nq_i % 2
            self.nq_i += 1
            nrm = self.nrm[i]
            tb = t0 // 128
            S.op("dve", lambda e: e.tensor_copy(nrm[0:ntok, :], psb[0:ntok, 0:SW]), r=[pskey], w=[("nrm", i)])
            S.op("dve", lambda e: e.tensor_copy(Vt[0:ntok, tb, sl * SW:(sl + 1) * SW], psb[0:ntok, 0:SW]), r=[pskey], w=[("Vt", tb)])
            dstv = self.outs["vs"][:, sl * SW:(sl + 1) * SW] if t0 >= TT else vout[tile * TT + t0:tile * TT + t0 + ntok, sl * SW:(sl + 1) * SW]
            S.dma("sp", lambda e: e.dma_start(out=dstv, in_=nrm[0:ntok, :]), self.osem_n[i], r=[("nrm", i)])
        self.tm_proj(wqkv, 4096, 8, tok_blocks, v_evac)
        import os
        sub = int(os.environ.get("ATT_SUB", "9"))
        if sub <= 1:
            return
        if tile < NTILES - 1:
            S.dma("sp", lambda e: e.dma_start(out=self.kT_scr[:, :, tile * TT:(tile + 1) * TT].rearrange("h d t -> d h t"), in_=kT[:, :, 0:TT]),
                  self.scrsem, r=[("kT", h) for h in range(16)], w=[("kscr", tile)])
            S.dma("sp", lambda e: e.dma_start(out=self.v_scr[tile * TT:(tile + 1) * TT, :].rearrange("(tb p) n -> p tb n", p=128), in_=Vt[:, 0:TT // 128, :]),
                  self.scrsem, r=[("Vt", tb) for tb in range(TT // 128)], w=[("vscr", tile)])
        if sub <= 2:
            return
        nprev = tile * TT
        for h in range(16 if sub > 3 else 1):
            pi = 0
            if nprev:
                S.dma("sp", lambda e, h=h: e.dma_start(out=self.kprev[pi][:, 0:nprev], in_=self.kT_scr[h, :, 0:nprev]), self.pvsem[pi],
                      r=[("kscr", t) for t in range(tile)], w=[("kprev", pi)])
                S.dma("sp", lambda e, h=h: e.dma_start(out=self.vprev[pi][:, 0:nprev // 128, :], in_=self.v_scr[0:nprev, h * 128:(h + 1) * 128].rearrange("(tb p) d -> p tb d", p=128)),
                      self.pvsem_v, r=[("vscr", t) for t in range(tile)], w=[("vprev", pi)])
            ci = self.ab_c % 2
            self.ab_c += 1
            carry = self.carry[ci]
            ckey = ("carry", ci)
            obk, ops_, okey = self.obank(ci)
            nblk = TT // 128 + nprev // 128
            bi = 0
            bias_ap = self.abias[:, h:h + 1]
            for kb in reversed(range(TT // 128)):
                self.attn_block(qT[:, h, 0:TT], [("qT", h)], TT, kT[:, h, kb * 128:(kb + 1) * 128], [("kT", h)],
                                Vt[:, kb, h * 128:(h + 1) * 128], [("Vt", kb)], self.masks[:, kb, :], bias_ap,
                                carry[:, 0:TT], ckey, ops_[:, 0:TT], okey, bi == 0, bi == nblk - 1, self.tri_b[:])
                bi += 1
            for pb in reversed(range(nprev // 128)):
                self.attn_block(qT[:, h, 0:TT], [("qT", h)], TT, self.kprev[pi][:, pb * 128:(pb + 1) * 128], [("kprev", pi)],
                                self.vprev[pi][:, pb, :], [("vprev", pi)], None, bias_ap,
                                carry[:, 0:TT], ckey, ops_[:, 0:TT], okey, bi == 0, bi == nblk - 1, self.tri_b[:])
                bi += 1
            S.op("dve", lambda e, h=h, ops_=ops_: e.tensor_copy(hid[:, h, 0:TT], ops_[:, 0:TT]), r=[okey], w=[("hid", h)])
        if ws and self.sample_on:
            self.sample_attention()
        self.linear_fm(wo, 0, 0, 16, hid, "hid", ncols, self.evac_add_R())

    def sample_attention(self):
        S = self.S
        qT, kT, Vt, hid = self.qT, self.kT, self.Vt, self.hid
        scale = 128.0 ** -0.5
        NQ = NS * 16
        pk = self.ins["cache_k"]
        pv = self.ins["cache_v"]
        rows_k = pk.rearrange("a r h d -> (a r) (h d)")
        rows_v = pv.rearrange("a r h d -> (a r) (h d)")
        carry = self.scarry
        obk, ops_, okey = self.obank(0)
        nblk = NPAGES + 1
        for bi in range(nblk):
            first = bi == 0
            last = bi == nblk - 1
            i = bi % 2
            KTp = self.KTp[0]
            kk = ("KTp", 0)
            if first:
                S.op("dve", lambda e: e.memset(KTp[:, :, :], 0.0), w=[kk])
                S.op("dve", lambda e: e.tensor_copy(KTp[:, :, 0:NS], kT[:, :, TT:TT + NS]), r=[("kT", h) for h in range(16)] + [kk], w=[kk])
                vsrc = lambda h: Vt[:, TT // 128, h * 128:(h + 1) * 128]
                vkeys = [("Vt", TT // 128)]
                mask = self.smask[:, :]
            else:
                page = NPAGES - bi
                pg = self.pgK[0]
                S.dma("pool", lambda e, pg=pg, page=page: e.indirect_dma_start(out=pg[:, :], out_offset=None, in_=rows_k,
                      in_offset=bass.IndirectOffsetOnAxis(ap=self.pidx[:, page:page + 1], axis=0)), self.pgsem[0], r=["pidx"], w=[("stage", 0)])
                pgv = self.pgV[i]
                S.dma("pool", lambda e, pgv=pgv, page=page: e.indirect_dma_start(out=pgv[:, :], out_offset=None, in_=rows_v,
                      in_offset=bass.IndirectOffsetOnAxis(ap=self.pidx[:, page:page + 1], axis=0)), self.pgsemV[i], r=["pidx"], w=[("pgV", i)])
                for g4 in range(4):
                    bk, tps, tkey = self.bank()
                    fns = [lambda e, j=j, tps=tps, pg=pg, g4=g4: e.transpose(tps[:, j * 128:(j + 1) * 128], pg[:, (g4 * 4 + j) * 128:(g4 * 4 + j + 1) * 128], self.ident_f[:]) for j in range(4)]
                    S.group("pe", fns, r=[("stage", 0), "const"], w=[tkey])
                    if False:
                        pass
                    else:
                        S.op("dve", lambda e, tps=tps, g4=g4: e.tensor_copy(KTp[:, g4 * 4:(g4 + 1) * 4, :], tps[:, 0:512].rearrange("p (j n) -> p j n", j=4)), r=[tkey], w=[kk])
                vsrc = lambda h, i=i: self.pgV[i][:, h * 128:(h + 1) * 128]
                vkeys = [("pgV", i)]
                mask = None
            bk, zps, zkey = self.bank()
            fns = [lambda e, h=h, zps=zps, KTp=KTp: e.matmul(zps[:, h * 16:h * 16 + NS], KTp[:, h, :], qT[:, h, TT:TT + NS], start=True, stop=True) for h in range(16)]
            S.group("pe", fns, r=[kk] + [("qT", h) for h in range(16)], w=[zkey])
            j = self.ab_i % 2
            self.ab_i += 1
            E, L, Lb, T1, Ab = self.abE[0], self.abL[0], self.abLb[j], self.abT[0], self.abA[j]
            ek, lk, lbk, tk, ak = ("abE", 0), ("abL", 0), ("abLb", j), ("abT", 0), ("abA", j)
            S.op("dve", lambda e, zps=zps, T1=T1: e.scalar_tensor_tensor(T1[:, 0:NQ].rearrange("p (h q) -> p h q", q=NS), zps[:, 0:256].rearrange("p (h c) -> p h c", c=16)[:, :, 0:NS], scale,
                                                                        self.sbiasrow[:, :].rearrange("p (h q) -> p h q", q=NS), ALU.mult, ALU.add), r=[zkey, "abias"], w=[tk])
            S.op("act", lambda e, E=E, T1=T1: e.activation(E[:, 0:NQ], T1[:, 0:NQ], AF.Exp), r=[tk], w=[ek])
            S.op("act", lambda e, E=E, L=L: e.activation(L[:, 0:NQ], E[:, 0:NQ], AF.Ln, bias=self.one_t[:, 0:1]), r=[ek, "const"], w=[lk])
            if mask is not None:
                S.op("dve", lambda e, Lb=Lb, L=L, mask=mask: e.tensor_tensor(Lb[:, 0:NQ], L[:, 0:NQ], mask, ALU.mult), r=[lk, "masks"], w=[lbk])
            else:
                S.op("dve", lambda e, Lb=Lb, L=L: e.tensor_copy(Lb[:, 0:NQ], L[:, 0:NQ]), r=[lk], w=[lbk])
            bk2, sps, skey = self.bank()
            S.op("pe", lambda e, sps=sps, Lb=Lb: e.matmul(sps[:, 0:NQ], self.tri_b[:], Lb[:, 0:NQ], start=True, stop=True), r=[lbk, "const"], w=[skey])
            bk3, cps, cpkey = self.bank()
            S.op("pe", lambda e, cps=cps, Lb=Lb: e.matmul(cps[:, 0:NQ], self.ones_b[:], Lb[:, 0:NQ], start=True, stop=True), r=[lbk, "const"], w=[cpkey])
            if not first:
                S.op("dve", lambda e, T1=T1: e.tensor_tensor(T1[:, 0:NQ], T1[:, 0:NQ], carry[:, 0:NQ], ALU.subtract), r=[tk, "scarry"], w=[tk])
            S.op("dve", lambda e, T1=T1, sps=sps: e.tensor_tensor(T1[:, 0:NQ], T1[:, 0:NQ], sps[:, 0:NQ], ALU.subtract), r=[tk, skey], w=[tk])
            if mask is not None:
                S.op("act", lambda e, E=E, T1=T1: e.activation(E[:, 0:NQ], T1[:, 0:NQ], AF.Exp), r=[tk], w=[ek])
                S.op("dve", lambda e, Ab=Ab, E=E, mask=mask: e.tensor_tensor(Ab[:, 0:NQ], E[:, 0:NQ], mask, ALU.mult), r=[ek, "masks"], w=[ak])
            else:
                S.op("act", lambda e, Ab=Ab, T1=T1: e.activation(Ab[:, 0:NQ], T1[:, 0:NQ], AF.Exp), r=[tk], w=[ak])
            if first:
                S.op("dve", lambda e, cps=cps: e.tensor_copy(carry[:, 0:NQ], cps[:, 0:NQ]), r=[cpkey], w=["scarry"])
            else:
                S.op("dve", lambda e, cps=cps: e.tensor_tensor(carry[:, 0:NQ], carry[:, 0:NQ], cps[:, 0:NQ], ALU.add), r=[cpkey, "scarry"], w=["scarry"])
            bk4, obp, obkey = self.bank()
            fns = [lambda e, h=h, Ab=Ab, vsrc=vsrc, obp=obp: e.matmul(obp[:, h * 16:h * 16 + NS], vsrc(h), Ab[:, h * NS:(h + 1) * NS], start=True, stop=True) for h in range(16)]
            S.group("pe", fns, r=[ak] + vkeys, w=[obkey])
            oview = obp[:, 0:256].rearrange("p (h c) -> p h c", c=16)[:, :, 0:NS]
            oacc = self.soacc[:, :].rearrange("p (h q) -> p h q", q=NS)
            if first:
                S.op("dve", lambda e, oview=oview: e.tensor_copy(oacc, oview), r=[obkey], w=["soacc"])
            else:
                S.op("dve", lambda e, oview=oview: e.tensor_tensor(oacc, oacc, oview, ALU.add), r=[obkey, "soacc"], w=["soacc"])
        ops_ = self.soacc
        okey = "soacc"
        S.op("dve", lambda e: e.tensor_copy(hid[:, :, TT:TT + NS], ops_[:, 0:NQ].rearrange("p (h n) -> p h n", h=16)), r=[okey], w=[("hid", h) for h in range(16)])
        import os
        if os.environ.get("DEBUG_SA"):
            dsm = self.osem
            S.dma("sp", lambda e: e.dma_start(out=self.outs["dbg_pgk"], in_=self.pgK[0][:, :]), dsm, r=[("stage", 0)])
            S.dma("sp", lambda e: e.dma_start(out=self.outs["dbg_pgv"], in_=self.pgV[0][:, :]), dsm, r=[("pgV", 0)])
            S.dma("sp", lambda e: e.dma_start(out=self.outs["dbg_carry"], in_=carry[:, 0:NQ]), dsm, r=["scarry"])
            t = self.tmpf[0]
            S.op("dve", lambda e: e.tensor_copy(t[:, 0:NQ], ops_[:, 0:NQ]), r=[okey], w=[("tmpf", 0)])
            S.dma("sp", lambda e: e.dma_start(out=self.outs["dbg_o"], in_=t[:, 0:NQ]), dsm, r=[("tmpf", 0)])
            S.dma("sp", lambda e: e.dma_start(out=self.outs["dbg_q"], in_=qT[:, :, TT:TT + NS]), dsm, r=[("qT", h) for h in range(16)])
            S.dma("sp", lambda e: e.dma_start(out=self.outs["dbg_ktp"], in_=self.KTp[0][:, :, :]), dsm, r=[("KTp", 0)])


    def s5_prep(self, j):
        S = self.S
        ins = self.ins
        A = {}
        o = [0]

        def f32(name, shape):
            ap = self.av(o[0], shape, F32)
            n = 1
            for d in shape[1:]:
                n *= d
            o[0] += n * 4
            A[name] = ap
            return ap
        for nm in ("lr", "li", "xr", "th", "c", "s", "t0", "t1", "t2", "t3", "nr", "cr", "ci"):
            f32(nm, [128, 64])
        f32("dt", [128, 1])
        uc = f32("uc", [128, 9, 64])
        us = f32("us", [128, 9, 64])
        mk = f32("mk", [128, 16, 64])
        pr = f32("pr", [128, 16, 64])
        pi = f32("pi", [128, 16, 64])
        br = f32("br", [128, 64, 16])
        bi = f32("bi", [128, 64, 16])
        Br = f32("Br", [128, 64, 16])
        Bi = f32("Bi", [128, 64, 16])
        Cr = f32("Cr", [128, 16, 64])
        Ci = f32("Ci", [128, 16, 64])
        tb = f32("tb", [128, 64, 16])
        tb2 = f32("tb2", [128, 64, 16])
        big = f32("big", [128, 64, 128])
        nat = f32("nat", [128, 128])
        f32("nat2", [128, 128])
        eg = [f32("eg%d" % i, [128, 3, 128]) for i in range(2)]
        wo = [self.av(o[0] + i * 1024, [128, 4, 128], BF16) for i in range(2)]
        o[0] += 2048
        K = "s5p"
        sem = self.s5sem
        q = "sp"
        S.dma(q, lambda e: e.dma_start(out=A["lr"], in_=ins["ssm_lambda_re"][j]), sem, w=[K])
        S.dma(q, lambda e: e.dma_start(out=A["li"], in_=ins["ssm_lambda_im"][j]), sem, w=[K])
        S.dma(q, lambda e: e.dma_start(out=A["dt"], in_=ins["ssm_log_dt"][j].rearrange("(g o) -> g o", o=1)), sem, w=[K])
        S.dma(q, lambda e: e.dma_start(out=br, in_=ins["ssm_b_re"][j]), sem, w=[K])
        S.dma(q, lambda e: e.dma_start(out=bi, in_=ins["ssm_b_im"][j]), sem, w=[K])
        S.dma(q, lambda e: e.dma_start(out=Cr, in_=ins["ssm_c_re"][j]), sem, w=[K])
        S.dma(q, lambda e: e.dma_start(out=Ci, in_=ins["ssm_c_im"][j]), sem, w=[K])

        def dv(fn):
            S.op("dve", fn, r=[K], w=[K])

        def ac(fn):
            S.op("act", fn, r=[K, "const"], w=[K])
        ac(lambda e: e.activation(A["dt"], A["dt"], AF.Exp))
        dv(lambda e: e.tensor_scalar(A["xr"], A["lr"], A["dt"][:, 0:1], None, ALU.mult))
        dv(lambda e: e.tensor_scalar(A["th"], A["li"], A["dt"][:, 0:1], None, ALU.mult))
        ac(lambda e: e.activation(A["s"], A["th"], AF.Sin, scale=1.0 / 32))
        ac(lambda e: e.activation(A["c"], A["th"], AF.Sin, bias=self.hpi_t[:, 0:1], scale=1.0 / 32))
        for _ in range(5):
            dv(lambda e: e.tensor_tensor(A["t0"], A["c"], A["c"], ALU.mult))
            dv(lambda e: e.tensor_tensor(A["t1"], A["s"], A["s"], ALU.mult))
            dv(lambda e: e.tensor_tensor(A["t2"], A["c"], A["s"], ALU.mult))
            dv(lambda e: e.tensor_tensor(A["c"], A["t0"], A["t1"], ALU.subtract))
            dv(lambda e: e.tensor_scalar(A["s"], A["t2"], 2.0, None, ALU.mult))
        dv(lambda e: e.memset(uc[:, 0, :], 1.0))
        dv(lambda e: e.memset(us[:, 0, :], 0.0))
        dv(lambda e: e.tensor_copy(uc[:, 1, :], A["c"]))
        dv(lambda e: e.tensor_copy(us[:, 1, :], A["s"]))
        for k in range(2, 9):
            dv(lambda e, k=k: e.tensor_tensor(A["t0"], uc[:, k - 1, :], A["c"], ALU.mult))
            dv(lambda e, k=k: e.tensor_tensor(A["t1"], us[:, k - 1, :], A["s"], ALU.mult))
            dv(lambda e, k=k: e.tensor_tensor(uc[:, k, :], A["t0"], A["t1"], ALU.subtract))
            dv(lambda e, k=k: e.tensor_tensor(A["t0"], uc[:, k - 1, :], A["s"], ALU.mult))
            dv(lambda e, k=k: e.tensor_tensor(A["t1"], us[:, k - 1, :], A["c"], ALU.mult))
            dv(lambda e, k=k: e.tensor_tensor(us[:, k, :], A["t0"], A["t1"], ALU.add))
        for ki in range(16):
            k = ki - 7
            ac(lambda e, ki=ki, k=k: e.activation(mk[:, ki, :], A["xr"], AF.Exp, scale=float(k)))
            dv(lambda e, ki=ki, k=k: e.tensor_tensor(pr[:, ki, :], mk[:, ki, :], uc[:, abs(k), :], ALU.mult))
            dv(lambda e, ki=ki, k=k: e.tensor_tensor(pi[:, ki, :], mk[:, ki, :], us[:, abs(k), :], ALU.mult))
            if k < 0:
                dv(lambda e, ki=ki: e.tensor_scalar(pi[:, ki, :], pi[:, ki, :], -1.0, None, ALU.mult))
        dv(lambda e: e.tensor_scalar(A["nr"], pr[:, 8, :], -1.0, None, ALU.add))
        dv(lambda e: e.tensor_tensor(A["t0"], A["lr"], A["lr"], ALU.mult))
        dv(lambda e: e.tensor_tensor(A["t1"], A["li"], A["li"], ALU.mult))
        dv(lambda e: e.tensor_tensor(A["t0"], A["t0"], A["t1"], ALU.add))
        dv(lambda e: e.reciprocal(A["t0"], A["t0"]))
        dv(lambda e: e.tensor_tensor(A["t1"], A["nr"], A["lr"], ALU.mult))
        dv(lambda e: e.tensor_tensor(A["t2"], pi[:, 8, :], A["li"], ALU.mult))
        dv(lambda e: e.tensor_tensor(A["t1"], A["t1"], A["t2"], ALU.add))
        dv(lambda e: e.tensor_tensor(A["cr"], A["t1"], A["t0"], ALU.mult))
        dv(lambda e: e.tensor_tensor(A["t1"], pi[:, 8, :], A["lr"], ALU.mult))
        dv(lambda e: e.tensor_tensor(A["t2"], A["nr"], A["li"], ALU.mult))
        dv(lambda e: e.tensor_tensor(A["t1"], A["t1"], A["t2"], ALU.subtract))
        dv(lambda e: e.tensor_tensor(A["ci"], A["t1"], A["t0"], ALU.mult))

        def bc_c(x):
            return x.unsqueeze(2).to_broadcast([128, 64, 16])
        dv(lambda e: e.tensor_tensor(tb, br, bc_c(A["cr"]), ALU.mult))
        dv(lambda e: e.tensor_tensor(tb2, bi, bc_c(A["ci"]), ALU.mult))
        dv(lambda e: e.tensor_tensor(Br, tb, tb2, ALU.subtract))
        dv(lambda e: e.tensor_tensor(tb, bi, bc_c(A["cr"]), ALU.mult))
        dv(lambda e: e.tensor_tensor(tb2, br, bc_c(A["ci"]), ALU.mult))
        dv(lambda e: e.tensor_tensor(Bi, tb, tb2, ALU.add))
        CrT = Cr.rearrange("g c p -> g p c")
        CiT = Ci.rearrange("g c p -> g p c")
        for kind in range(3):
            for ri in range(2):
                for i in range(8):
                    if kind == 0:
                        ki = (7 - i) + 7
                        X, Y = Br, Bi
                    else:
                        ki = (i - 7) + 7 if kind == 1 else (i + 1) + 7
                        X, Y = CrT, CiT
                    prk = bc_c(pr[:, ki, :])
                    pik = bc_c(pi[:, ki, :])
                    dst = big[:, :, i * 16:(i + 1) * 16]
                    if ri == 0:
                        dv(lambda e, X=X, prk=prk: e.tensor_tensor(tb, X, prk, ALU.mult))
                        dv(lambda e, Y=Y, pik=pik: e.tensor_tensor(tb2, Y, pik, ALU.mult))
                        dv(lambda e, dst=dst: e.tensor_tensor(dst, tb, tb2, ALU.subtract))
                    else:
                        dv(lambda e, Y=Y, prk=prk: e.tensor_tensor(tb, Y, prk, ALU.mult))
                        dv(lambda e, X=X, pik=pik: e.tensor_tensor(tb2, X, pik, ALU.mult))
                        dv(lambda e, dst=dst: e.tensor_tensor(dst, tb, tb2, ALU.add))
                        if kind > 0:
                            dv(lambda e, dst=dst: e.tensor_scalar(dst, dst, -1.0, None, ALU.mult))
                S.dma("sp", lambda e, kind=kind, ri=ri: e.dma_start(out=self.efg[:, kind, ri * 64:(ri + 1) * 64, :], in_=big), sem, r=[K], w=[K, "efg"])
        for name, ki in (("A8", 15), ("Am4", 3)):
            for vi, (sa, sb_) in enumerate(((1.0, 1.0), (-1.0, 1.0), (1.0, -1.0))):
                src = pr if vi == 0 else pi
                dv(lambda e, src=src, ki=ki, sa=sa: e.tensor_scalar(nat[:, 0:64], src[:, ki, :], sa, None, ALU.mult))
                dv(lambda e, src=src, ki=ki, sb_=sb_: e.tensor_scalar(nat[:, 64:128], src[:, ki, :], sb_, None, ALU.mult))
                bk, psb, pskey = self.bank()
                S.op("pe", lambda e, psb=psb: e.transpose(psb[:, 0:128], nat, self.ident_f[:]), r=[K, "const"], w=[pskey])
                idx = (0 if name == "A8" else 3) + vi
                S.op("dve", lambda e, psb=psb: e.tensor_copy(A["nat2"], psb[:, 0:128]), r=[pskey, K], w=[K])
                S.dma("sp", lambda e, idx=idx: e.dma_start(out=self.s5A_scr[j, idx], in_=A["nat2"]), sem, r=[K], w=[K, ("s5A", j)])
        for g in range(128):
            bi_ = g % 2
            et = eg[bi_]
            ek = ("s5eg", bi_)
            S.dma("sp", lambda e, g=g, et=et: e.dma_start(out=et, in_=self.efg[g].rearrange("k r n -> r k n")), self.s5sem2[bi_], r=["efg"], w=[ek])
            bk, p1, k1 = self.bank()
            S.op("pe", lambda e, p1=p1, et=et: e.transpose(p1[:, 0:128], et[:, 0, :], self.ident_f[:]), r=[ek, "const"], w=[k1])
            bk, p2, k2 = self.bank()
            S.op("pe", lambda e, p2=p2, et=et: e.matmul(p2[:, 0:128], et[:, 0, :], et[:, 1, :], start=True, stop=True), r=[ek], w=[k2])
            wt = wo[bi_]
            wk = ("s5wo", bi_)
            S.op("dve", lambda e, wt=wt, p1=p1: e.tensor_copy(wt[:, 0, :], p1[:, 0:128]), r=[k1], w=[wk])
            S.op("dve", lambda e, wt=wt, p1=p1: e.tensor_copy(wt[:, 1, 0:64], p1[:, 64:128]), r=[k1], w=[wk])
            S.op("dve", lambda e, wt=wt, p1=p1: e.tensor_copy(wt[:, 1, 64:128], p1[:, 0:64]), r=[k1], w=[wk])
            S.op("dve", lambda e, wt=wt, p2=p2: e.tensor_tensor(wt[:, 2, :], p2[:, 0:128], self.blkmask[:, :], ALU.mult), r=[k2, "const"], w=[wk])
            S.op("dve", lambda e, wt=wt, et=et: e.tensor_copy(wt[:, 3, :], et[:, 2, :]), r=[ek], w=[wk])
            S.dma("sp", lambda e, g=g, wt=wt: e.dma_start(out=self.s5w[j, g], in_=wt), self.s5sem3, r=[wk], w=[("s5w", j)])

    def s5_apply(self, j, li, tile, ncols, ws):
        S = self.S
        R, xn, hid = self.R, self.xn, self.hid
        ncol = NCH + 1 if ws else NCH
        G7 = 7
        V = self.av(0, [128, 128, NCH + 1], F32)
        Vs = self.av(128 * (NCH + 1) * 4, [128, 128, NCH + 1], F32)
        o = 2 * 128 * (NCH + 1) * 4
        wb = [self.av(o + i * 8192, [128, 8, 512], BF16) for i in range(2)]
        o += 16384
        Yg = self.av(0, [128, 128, NCH + 1], BF16)
        F2 = self.av(128 * (NCH + 1) * 2, [128, 16, NT], F32)
        XG = hid.rearrange("p a b -> p (a b)")[:, 0:128 * (NCH + 1)].rearrange("p (g n) -> p g n", g=128)
        Hall = xn.rearrange("p a b -> p (a b)")[:, 0:128 * (NCH + 1)].rearrange("p (g n) -> p g n", g=128)
        allhid = [("hid", c) for c in range(16)]
        allxn = [("xn", c) for c in range(16)]
        Atl = [self.av(o + i * 512, [128, 128], F32) for i in range(6)]
        o += 6 * 512
        A8, Am4 = Atl[0:3], Atl[3:6]
        st = dict(self.s5state[j])
        for nm in ("t1", "t2", "t3", "t4", "h0", "h0s", "hs0", "hs0s", "hsf", "so", "natA"):
            st[nm] = self.av(o, [128, 128], F32)
            o += 512
        self.s5u = self.av(2 * 128 * (NCH + 1) * 4, [128, NT], F32)
        self.s5y = self.av(2 * 128 * (NCH + 1) * 4 + NT * 4, [128, NT], F32)
        self.s5t = self.av(2 * 128 * (NCH + 1) * 4 + 2 * NT * 4, [128, NT], F32)
        for i in range(6):
            S.dma("sp", lambda e, i=i: e.dma_start(out=Atl[i], in_=self.s5A_scr[j, i]), self.s5sem, r=[("s5A", j)], w=["s5A"])
        if ws:
            for nm, first, second in (("h0", "sst_re", "sst_im"), ("h0s", "sst_im", "sst_re")):
                S.dma("sp", lambda e, first=first: e.dma_start(out=st["natA"][:, 0:64], in_=self.ins[first][j]), self.s5sem_n, w=["s5nat"])
                S.dma("sp", lambda e, second=second: e.dma_start(out=st["natA"][:, 64:128], in_=self.ins[second][j]), self.s5sem_n, w=["s5nat"])
                bk, psb, pskey = self.bank()
                S.op("pe", lambda e, psb=psb: e.transpose(psb[:, 0:128], st["natA"], self.ident_f[:]), r=["s5nat", "const"], w=[pskey])
                S.op("dve", lambda e, psb=psb, nm=nm: e.tensor_copy(st[nm], psb[:, 0:128]), r=[pskey], w=["s5h0"])
        for g0 in range(0, 128, G7):
            gs = list(range(g0, min(128, g0 + G7)))
            bk, psb, pskey = self.bank()
            fns = []
            for si, g in enumerate(gs):
                dc, gl = g // 8, g % 8
                for i in range(8):
                    fns.append(lambda e, si=si, dc=dc, gl=gl, i=i, psb=psb: e.matmul(
                        psb[:, si * 65:si * 65 + NCH], self.Zb[gl][:, 112 - 16 * i:240 - 16 * i], xn[:, dc, i:TT:8], start=(i == 0), stop=(i == 7)))
                if ws:
                    for k in range(NS):
                        fns.append(lambda e, si=si, dc=dc, gl=gl, k=k, psb=psb: e.matmul(
                            psb[:, si * 65 + NCH:si * 65 + NCH + 1], self.Zb[gl][:, 112 - 16 * (4 + k):240 - 16 * (4 + k)], xn[:, dc, TT + k:TT + k + 1],
                            start=(k == 0), stop=(k == NS - 1)))
            S.group("pe", fns, r=allxn + ["const", "Zb"], w=[pskey])
            n = len(gs)
            S.op("dve", lambda e, g0=g0, n=n, psb=psb: e.tensor_copy(XG[:, g0:g0 + n, 0:ncol], psb[:, 0:n * 65].rearrange("p (g c) -> p g c", c=65)[:, :, 0:ncol]),
                 r=[pskey], w=allhid)
        def load_wb(gb):
            bi_ = gb % 2
            S.dma("pool", lambda e, gb=gb, bi_=bi_: e.dma_start(out=wb[bi_], in_=self.s5w[j, gb * 8:(gb + 1) * 8].rearrange("g p k n -> p g (k n)")),
                  self.s5wsem[bi_], r=[("s5w", j)], w=[("s5wb", bi_)])
            return wb[bi_], ("s5wb", bi_)
        for gb in range(16):
            wt, wk = load_wb(gb)
            for half in range(2):
                gs = list(range(gb * 8 + half * 4, gb * 8 + half * 4 + 4))
                for which, dst in ((0, V), (1, Vs)):
                    bk, psb, pskey = self.bank()
                    fns = [lambda e, si=si, g=g, psb=psb, which=which, wt=wt: e.matmul(
                        psb[:, si * 65:si * 65 + ncol], wt[:, g % 8, which * 128:(which + 1) * 128], XG[:, g, 0:ncol], start=True, stop=True) for si, g in enumerate(gs)]
                    S.group("pe", fns, r=allhid + [wk], w=[pskey])
                    S.op("dve", lambda e, g0=gs[0], psb=psb, dst=dst: e.tensor_copy(dst[:, g0:g0 + 4, 0:ncol], psb[:, 0:4 * 65].rearrange("p (g c) -> p g c", c=65)[:, :, 0:ncol]),
                         r=[pskey], w=["s5V"])
        H = [h_[:, :] for h_ in st["H"]]
        Hs = [h_[:, :] for h_ in st["Hs"]]
        cur = st["cur"]
        for jj in range(NCH):
            nxt = 1 - cur
            S.op("dve", lambda e, jj=jj, cur=cur: e.tensor_copy(Hall[:, :, jj], H[cur][:, :]), r=[("H", cur)], w=allxn)
            S.op("dve", lambda e, cur=cur: e.tensor_tensor(st["t1"], A8[0], H[cur], ALU.mult), r=[("H", cur), "s5A"], w=["s5t1"])
            S.op("dve", lambda e, cur=cur: e.tensor_tensor(st["t2"], A8[1], Hs[cur], ALU.mult), r=[("Hs", cur), "s5A"], w=["s5t2"])
            S.op("dve", lambda e: e.tensor_tensor(st["t1"], st["t1"], st["t2"], ALU.add), r=["s5t1", "s5t2"], w=["s5t1"])
            S.op("dve", lambda e, jj=jj, nxt=nxt: e.tensor_tensor(H[nxt], st["t1"], V[:, :, jj], ALU.add), r=["s5t1", "s5V"], w=[("H", nxt)])
            S.op("pool", lambda e, cur=cur: e.tensor_tensor(st["t3"], A8[0], Hs[cur], ALU.mult), r=[("Hs", cur), "s5A"], w=["s5t3"])
            S.op("pool", lambda e, cur=cur: e.tensor_tensor(st["t4"], A8[2], H[cur], ALU.mult), r=[("H", cur), "s5A"], w=["s5t4"])
            S.op("pool", lambda e: e.tensor_tensor(st["t3"], st["t3"], st["t4"], ALU.add), r=["s5t3", "s5t4"], w=["s5t3"])
            S.op("pool", lambda e, jj=jj, nxt=nxt: e.tensor_tensor(Hs[nxt], st["t3"], Vs[:, :, jj], ALU.add), r=["s5t3", "s5V"], w=[("Hs", nxt)])
            cur = nxt
        self.s5state[j]["cur"] = cur
        if ws:
            h0, h0s = st["h0"], st["h0s"]
            hs0, hs0s = st["hs0"], st["hs0s"]
            dv = lambda fn, r, w: S.op("dve", fn, r=r, w=w)
            dv(lambda e: e.tensor_tensor(st["t1"], Am4[0], h0, ALU.mult), ["s5h0", "s5A"], ["s5t1"])
            dv(lambda e: e.tensor_tensor(st["t2"], Am4[1], h0s, ALU.mult), ["s5h0", "s5A"], ["s5t2"])
            dv(lambda e: e.tensor_tensor(hs0, st["t1"], st["t2"], ALU.add), ["s5t1", "s5t2"], ["s5hs0"])
            dv(lambda e: e.tensor_tensor(st["t1"], Am4[0], h0s, ALU.mult), ["s5h0", "s5A"], ["s5t1"])
            dv(lambda e: e.tensor_tensor(st["t2"], Am4[2], h0, ALU.mult), ["s5h0", "s5A"], ["s5t2"])
            dv(lambda e: e.tensor_tensor(hs0s, st["t1"], st["t2"], ALU.add), ["s5t1", "s5t2"], ["s5hs0"])
            dv(lambda e: e.tensor_copy(Hall[:, :, NCH], hs0), ["s5hs0"], allxn)
            dv(lambda e: e.tensor_tensor(st["t1"], A8[0], hs0, ALU.mult), ["s5hs0", "s5A"], ["s5t1"])
            dv(lambda e: e.tensor_tensor(st["t2"], A8[1], hs0s, ALU.mult), ["s5hs0", "s5A"], ["s5t2"])
            dv(lambda e: e.tensor_tensor(st["t1"], st["t1"], st["t2"], ALU.add), ["s5t1", "s5t2"], ["s5t1"])
            dv(lambda e: e.tensor_tensor(st["hsf"], st["t1"], V[:, :, NCH], ALU.add), ["s5t1", "s5V"], ["s5hsf"])
            for src, skey, o_re, o_im in ((H[cur], ("H", cur), "ssm_re_p", "ssm_im_p"), (st["hsf"], "s5hsf", "ssm_re_s", "ssm_im_s")):
                bk, psb, pskey = self.bank()
                S.op("pe", lambda e, psb=psb, src=src: e.transpose(psb[:, 0:128], src, self.ident_f[:]), r=[skey, "const"], w=[pskey])
                so = st["so"]
                S.op("dve", lambda e, psb=psb, so=so: e.tensor_copy(so[:, :], psb[:, 0:128]), r=[pskey], w=["s5so"])
                S.dma("sp", lambda e, so=so, o_re=o_re: e.dma_start(out=self.outs[o_re][j], in_=so[:, 0:64]), self.osem, r=["s5so"])
                S.dma("sp", lambda e, so=so, o_im=o_im: e.dma_start(out=self.outs[o_im][j], in_=so[:, 64:128]), self.osem, r=["s5so"])
        for gb in range(16):
            wt, wk = load_wb(gb)
            for half in range(2):
                gs = list(range(gb * 8 + half * 4, gb * 8 + half * 4 + 4))
                bk, psb, pskey = self.bank()
                fns = []
                for si, g in enumerate(gs):
                    fns.append(lambda e, si=si, g=g, psb=psb, wt=wt: e.matmul(psb[:, si * 65:si * 65 + ncol], wt[:, g % 8, 256:384], XG[:, g, 0:ncol], start=True, stop=False))
                    fns.append(lambda e, si=si, g=g, psb=psb, wt=wt: e.matmul(psb[:, si * 65:si * 65 + ncol], wt[:, g % 8, 384:512], Hall[:, g, 0:ncol], start=False, stop=True))
                S.group("pe", fns, r=allhid + allxn + [wk], w=[pskey])
                S.op("dve", lambda e, g0=gs[0], psb=psb: e.tensor_copy(Yg[:, g0:g0 + 4, 0:ncol], psb[:, 0:4 * 65].rearrange("p (g c) -> p g c", c=65)[:, :, 0:ncol]),
                     r=[pskey, "s5V"], w=["s5Y"])
        gcol = (li * 2) * 16
        for dc in range(16):
            bk, psb, pskey = self.bank()
            fns = []
            for i in range(8):
                for gl in range(8):
                    fns.append(lambda e, i=i, gl=gl, dc=dc, psb=psb: e.matmul(psb[:, i * NCH:(i + 1) * NCH], self.Zb[i][:, 112 - 16 * gl:240 - 16 * gl], Yg[:, dc * 8 + gl, 0:NCH],
                                                                         start=(gl == 0), stop=(gl == 7)))
            S.group("pe", fns, r=["s5Y", "Zb"], w=[pskey])
            if ws:
                bk2, psb2, pskey2 = self.bank()
                fns = []
                for k in range(NS):
                    for gl in range(8):
                        fns.append(lambda e, k=k, gl=gl, dc=dc, psb2=psb2: e.matmul(psb2[:, k:k + 1], self.Zb[4 + k][:, 112 - 16 * gl:240 - 16 * gl], Yg[:, dc * 8 + gl, NCH:NCH + 1],
                                                                               start=(gl == 0), stop=(gl == 7)))
                S.group("pe", fns, r=["s5Y", "Zb"], w=[pskey2])
            u = self.s5u
            y = self.s5y
            t = self.s5t
            S.op("dve", lambda e, dc=dc: e.scalar_tensor_tensor(u[:, 0:ncols], R[:, dc, 0:ncols], self.gv[:, gcol + dc:gcol + dc + 1], self.rstd[:, 0:ncols], ALU.mult, ALU.mult),
                 r=[("R", dc), "rstd", "gv", ("s5wb", 0), ("s5wb", 1)], w=["s5u", ("s5wb", 0)])
            S.op("dve", lambda e, dc=dc, psb=psb: e.scalar_tensor_tensor(y[:, 0:TT].rearrange("p (j i) -> p j i", i=8), u[:, 0:TT].rearrange("p (j i) -> p j i", i=8),
                                                                    self.dvv[:, j * 16 + dc:j * 16 + dc + 1], psb[:, 0:TT].rearrange("p (i j) -> p j i", i=8), ALU.mult, ALU.add),
                 r=["s5u", pskey, "gv"], w=["s5y"])
            if ws:
                S.op("dve", lambda e, dc=dc, psb2=psb2: e.scalar_tensor_tensor(y[:, TT:TT + NS], u[:, TT:TT + NS], self.dvv[:, j * 16 + dc:j * 16 + dc + 1], psb2[:, 0:NS], ALU.mult, ALU.add),
                     r=["s5u", pskey2, "gv"], w=["s5y"])
            S.op("pool", lambda e: e.tensor_tensor(t[:, 0:ncols], y[:, 0:ncols], y[:, 0:ncols], ALU.mult), r=["s5y"], w=["s5t"])
            S.op("dve", lambda e: e.tensor_scalar(t[:, 0:ncols], t[:, 0:ncols], 0.044715, 1.0, ALU.mult, ALU.add), r=["s5t"], w=["s5t"])
            S.op("dve", lambda e: e.tensor_tensor(t[:, 0:ncols], t[:, 0:ncols], y[:, 0:ncols], ALU.mult), r=["s5t", "s5y"], w=["s5t"])
            S.op("act", lambda e: e.activation(t[:, 0:ncols], t[:, 0:ncols], AF.Sigmoid, scale=1.5957691216057308), r=["s5t"], w=["s5t"])
            S.op("dve", lambda e, dc=dc: e.tensor_tensor(hid[:, dc, 0:ncols], y[:, 0:ncols], t[:, 0:ncols], ALU.mult), r=["s5t", "s5y", "s5Y"], w=[("hid", dc)])
        wg = self.ins["ssm_w_glu"][j]

        def ev_g(m, n0, n, ps, pskey):
            S.op("act", lambda e: e.activation(F2[:, m, n0:n0 + n], ps, AF.Sigmoid), r=[pskey, "s5Y"], w=[("F2", m)])
        self.linear_fm(wg, 0, 2048, 16, hid, "hid", ncols, ev_g)

        def ev_a(m, n0, n, ps, pskey):
            tt = self.tmpf[self.tmpf_i % 2]
            tk = ("tmpf", self.tmpf_i % 2)
            self.tmpf_i += 1
            S.op("dve", lambda e: e.tensor_tensor(tt[:, 0:n], ps, F2[:, m, n0:n0 + n], ALU.mult), r=[pskey, ("F2", m)], w=[tk])
            S.op("pool", lambda e: e.tensor_tensor(R[:, m, n0:n0 + n], R[:, m, n0:n0 + n], tt[:, 0:n], ALU.add), r=[tk, ("R", m)], w=[("R", m)])
        self.linear_fm(wg, 0, 0, 16, hid, "hid", ncols, ev_a)

    def obank(self, i):
        return i, self.obanks[i], ("ops", i)

    def av(self, off, shape, dt):
        n = 1
        for d in shape[1:]:
            n *= d
        if dt == BF16:
            ap = self.arena[:, off // 2: off // 2 + n]
        else:
            ap = self.arena[:, off // 2: off // 2 + 2 * n].bitcast(dt)
        if len(shape) == 3:
            ap = ap.rearrange("p (a b) -> p a b", a=shape[1])
        self._av_end = max(getattr(self, "_av_end", 0), off + n * (2 if dt == BF16 else 4))
        assert self._av_end <= self.ARENA_BYTES, self._av_end
        return ap

    def barrier(self):
        S = self.S
        snap = [(S.sem[e], S.cnt[e]) for e in S.ENG if S.cnt[e]]
        dsn = [(d.h, d.count) for d in self.all_dsems if d.count]
        for eng in S.ENG:
            waits = []
            for s_, v in snap + dsn:
                if S.waited[eng].get(s_.name, 0) < v:
                    S.waited[eng][s_.name] = v
                    waits.append((s_, v))

            def emit(e, waits=waits):
                for s_, v in waits:
                    e.wait_ge(s_, v)
            S.ops[eng].append(emit)

    def build(self):
        nc = self.nc
        es = self.es
        S = self.S = Sched(nc, es)
        self.all_dsems = []
        _ds = S.dsem

        def dsem(name=None):
            d = _ds(name)
            self.all_dsems.append(d)
            return d
        S.dsem = dsem
        self.din("xp", [SEQ, D])
        self.din("xs", [NS, D])
        self.din("norm_mix", [DEPTH, D])
        self.din("norm_mlp", [DEPTH, D])
        self.din("mlp_w_up", [DEPTH, D, FF])
        self.din("mlp_w_down", [DEPTH, FF, D])
        self.din("conv_w_in", [1, D, 3 * D])
        self.din("conv_w_dw", [1, 3, D])
        self.din("conv_w_out", [1, D, D])
        self.din("sconv_in", [2, D])
        self.din("attn_w_qkv", [1, D, 3 * D])
        self.din("attn_q_norm", [1, 128])
        self.din("attn_k_norm", [1, 128])
        self.din("attn_sb_bias", [1, 16])
        self.din("attn_w_o", [1, D, D])
        import os
        self.sample_on = self.stage >= 4 and not os.environ.get("NOSAMPLE")
        if self.sample_on:
            self.din("cache_k", [1280, 128, 16, 128])
            self.din("cache_v", [1280, 128, 16, 128])
        self.din("pt", [1, NPAGES], I32)
        for nm, shp in (("ssm_lambda_re", [2, 128, 64]), ("ssm_lambda_im", [2, 128, 64]), ("ssm_log_dt", [2, 128]),
                        ("ssm_b_re", [2, 128, 64, 16]), ("ssm_b_im", [2, 128, 64, 16]), ("ssm_c_re", [2, 128, 16, 64]),
                        ("ssm_c_im", [2, 128, 16, 64]), ("ssm_d", [2, D]), ("ssm_w_glu", [2, D, 2 * D]),
                        ("sst_re", [2, 128, 64]), ("sst_im", [2, 128, 64]), ("c_blkmask", [128, 128])):
            self.din(nm, shp)
        self.din("c_ident", [128, 128])
        self.din("c_tri", [128, 128])
        self.din("c_masks", [128, TT // 128, TT])
        self.din("c_smask", [128, 16 * NS])
        self.dout("yp", [SEQ, D])
        self.dout("ys", [NS, D])
        self.dout("conv_p", [2, D])
        self.dout("conv_s", [2, D])
        for nm in ("ssm_re_p", "ssm_im_p", "ssm_re_s", "ssm_im_s"):
            self.dout(nm, [2, 128, 64])
        import os
        if os.environ.get("DEBUG_SA"):
            self.dout("dbg_pgk", [128, D])
            self.dout("dbg_pgv", [128, D], BF16)
            self.dout("dbg_carry", [128, 16 * NS])
            self.dout("dbg_o", [128, 16 * NS])
            self.dout("dbg_q", [128, 16, NS], BF16)
            self.dout("dbg_ktp", [128, 16, 128], BF16)
        self.dout("kp", [SEQ, D])
        self.dout("vp", [SEQ, D])
        self.dout("ks", [NS, D])
        self.dout("vs", [NS, D])
        self.kT_scr = nc.dram_tensor("kT_scr", [16, 128, SEQ], BF16).ap()
        self.v_scr = nc.dram_tensor("v_scr", [SEQ, D], BF16).ap()
        self.efg = nc.dram_tensor("efg_scr", [128, 3, 128, 128], F32).ap()
        self.s5w = nc.dram_tensor("s5w_scr", [2, 128, 128, 4, 128], BF16).ap()
        self.s5A_scr = nc.dram_tensor("s5A_scr", [2, 6, 128, 128], F32).ap()
        self.R = self.sb("R", [128, 16, NT], F32)
        self.xn = self.sb("xn", [128, 16, NT + 4], BF16)
        self.hid = self.sb("hid", [128, 16, NT + 4], BF16)
        self.wbuf = [self.sb("w%d" % i, [128, 16, SW], BF16) for i in range(2)]
        self.wsem = [S.dsem("ws%d" % i) for i in range(2)]
        self.w_i = 0
        self.rstd = self.sb("rstd", [128, NT], F32)
        self.sq = [self.sb("sq0", [128, NT], F32)]
        self.tmpb = [self.sb("tmpb%d" % i, [128, 512], BF16) for i in range(2)]
        self.tmp_i = 0
        self.tmpf = [self.sb("tmpf%d" % i, [128, 512], F32) for i in range(2)]
        self.tmpf_i = 0
        self.stage_f = [self.sb("stage0", [128, D], F32)] * 2
        self.ssem = [S.dsem("ss%d" % i) for i in range(2)]
        self.osem = S.dsem("osem")
        self.scrsem = S.dsem("scrsem")
        self.pvsem = [S.dsem("pv%d" % i) for i in range(2)]
        self.pgsem = [S.dsem("pg%d" % i) for i in range(2)]
        self.pgsemV = [S.dsem("pgv%d" % i) for i in range(2)]
        self.osem_n = [S.dsem("on%d" % i) for i in range(2)]
        self.pvsem_v = S.dsem("pvv")
        self.s5sem_n = S.dsem("s5n")
        self.gv = self.sb("gv", [128, 16 * 8], F32)
        self.dwv = self.sb("dwv", [128, 48], F32)
        self.ccarry = self.sb("ccarry", [128, 16, 2], F32)
        self.sconv = self.sb("sconv", [128, 16, 2], F32)
        self.ident_f = self.sb("ident_f", [128, 128], F32)
        self.ones_f = self.sb("ones_f", [128, 128], F32)
        self.ones_b = self.sb("ones_b", [128, 128], BF16)
        self.tri_f = self.sb("tri_f", [128, 128], F32)
        self.tri_b = self.sb("tri_b", [128, 128], BF16)
        self.eps_t = self.sb("eps_t", [128, 1], F32)
        self.one_t = self.sb("one_t", [128, 1], F32)
        self.qgain = self.sb("qgain", [128, 128], F32)
        self.kgain = self.sb("kgain", [128, 128], F32)
        self.abias = self.sb("abias", [128, 16], F32)
        self.sbiasrow = self.sb("sbiasrow", [128, 16 * NS], F32)
        self.ptsb = self.sb("ptsb", [128, NPAGES], I32)
        self.pidx = self.sb("pidx", [128, NPAGES], I32)
        self.iota_p = self.sb("iota_p", [128, 1], I32)
        self.s5sem = S.dsem("s5sem")
        self.s5sem2 = [S.dsem("s5e%d" % i) for i in range(2)]
        self.s5sem3 = S.dsem("s5sem3")
        self.s5wsem = [S.dsem("s5w%d" % i) for i in range(2)]
        self.Zb = [self.sb("Zb%d" % a, [128, 240], BF16) for a in range(8)]
        self.blkmask = self.sb("blkmask", [128, 128], F32)
        self.hpi_t = self.sb("hpi_t", [128, 1], F32)
        self.dvv = self.sb("dvv", [128, 32], F32)
        self.s5state = []
        for jj in range(2):
            self.s5state.append({"H": [self.sb("H%d_%d" % (jj, i), [128, 128], F32) for i in range(2)],
                                 "Hs": [self.sb("Hs%d_%d" % (jj, i), [128, 128], F32) for i in range(2)], "cur": 0})
        self.ARENA_BYTES = 93 * 1024
        self.arena = self.sb("arena", [128, self.ARENA_BYTES // 2], BF16)
        allb = [self.ps("bank%d" % i, [128, 512], F32) for i in range(8)]
        self.banks = allb[0:6]
        self.obanks = allb[6:8]
        self.bank_i = 0
        self.ab_i = 0
        self.ab_c = 0
        self.nq_i = 0
        self.F1 = self.av(0, [128, 16, TT + 8], F32)
        o = 0
        self.qT = self.av(o, [128, 16, NT], BF16); o += 16 * NT * 2
        self.kT = self.av(o, [128, 16, NT], BF16); o += 16 * NT * 2
        self.Vt = self.av(o, [128, TT // 128 + 1, D], BF16); o += (TT // 128 + 1) * D * 2
        npv = SEQ - TT
        self.kprev = [self.av(o, [128, npv], BF16)] * 2; o += npv * 2
        self.vprev = [self.av(o, [128, npv // 128, 128], BF16)] * 2; o += npv * 2
        self.abLb = [self.av(o + i * 1024, [128, 512], BF16) for i in range(2)]; o += 2048
        self.abA = [self.av(o + i * 1024, [128, 512], BF16) for i in range(2)]; o += 2048
        self.masks = self.av(o, [128, TT // 128, TT], BF16); o += (TT // 128) * TT * 2
        self.smask = self.av(o, [128, 16 * NS], BF16); o += 16 * NS * 2
        self.pgV = [self.av(o + i * 4096, [128, D], BF16) for i in range(2)]; o += 8192
        self.pgVb = self.pgV
        self.KTp = [self.av(o, [128, 16, 128], BF16)] * 2; o += 4096
        self.abE = [self.av(o, [128, 512], F32)] * 2; o += 2048
        self.abL = [self.av(o, [128, 512], F32)] * 2; o += 2048
        self.abT = [self.av(o, [128, 512], F32)] * 2; o += 2048
        self.carry = [self.av(o + i * 2048, [128, 512], F32) for i in range(2)]; o += 4096
        self.scarry = self.av(o, [128, 16 * NS], F32); o += 16 * NS * 4
        self.soacc = self.av(o, [128, 16 * NS], F32); o += 16 * NS * 4
        self.nsq = [self.av(o + i * 1024, [128, SW], F32) for i in range(2)]; o += 2048
        self.nrm = [self.av(o + i * 1024, [128, SW], F32) for i in range(2)]; o += 2048
        self.nss = [self.av(o + i * 8, [128, 2], F32) for i in range(2)]; o += 16
        self.pgK = self.stage_f

        self.csem = None
        S.dma("sp", lambda e: e.dma_start(out=self.ident_f[:], in_=self.ins["c_ident"]), S.dsem(), w=["const0"])
        S.dma("sp", lambda e: e.dma_start(out=self.tri_f[:], in_=self.ins["c_tri"]), S.dsem(), w=["const2"])
        S.op("dve", lambda e: e.memset(self.ones_f[:], 1.0), w=["const1"])
        S.op("dve", lambda e: e.memset(self.ones_b[:], 1.0), w=["const3"])
        S.op("dve", lambda e: e.memset(self.one_t[:], 1.0), w=["const4"])
        S.op("dve", lambda e: e.tensor_copy(self.tri_b[:], self.tri_f[:]), r=["const2"], w=["const5"])
        S.op("dve", lambda e: e.memset(self.eps_t[:], EPS), r=["const0", "const1", "const3", "const4", "const5"], w=["const"])
        S.op("dve", lambda e: e.tensor_copy(self.eps_t[:], self.eps_t[:]), r=["const"], w=["const"])
        for i in range(DEPTH):
            for k, nm in enumerate(("norm_mix", "norm_mlp")):
                col = (i * 2 + k) * 16
                src = self.ins[nm][i].rearrange("(c p) -> p c", p=128)
                S.dma("sp", lambda e, col=col, src=src: e.dma_start(out=self.gv[:, col:col + 16], in_=src, allow_slow_non_contiguous=True),
                      S.dsem(), w=["gv"])
        for k in range(3):
            src = self.ins["conv_w_dw"][0, k].rearrange("(c p) -> p c", p=128)
            S.dma("sp", lambda e, k=k, src=src: e.dma_start(out=self.dwv[:, k * 16:(k + 1) * 16], in_=src, allow_slow_non_contiguous=True), S.dsem(), w=["gv"])
        S.dma("sp", lambda e: e.dma_start(out=self.qgain[:], in_=self.ins["attn_q_norm"][0].partition_broadcast(128)), S.dsem(), w=["gains"])
        S.dma("sp", lambda e: e.dma_start(out=self.kgain[:], in_=self.ins["attn_k_norm"][0].partition_broadcast(128)), S.dsem(), w=["gains"])
        S.dma("sp", lambda e: e.dma_start(out=self.abias[:], in_=self.ins["attn_sb_bias"][0].partition_broadcast(128)), S.dsem(), w=["abias0"])
        for h in range(16):
            S.op("dve", lambda e, h=h: e.tensor_scalar(self.sbiasrow[:, h * NS:(h + 1) * NS], self.ones_f[:, 0:NS], self.abias[:, h:h + 1], None, ALU.mult),
                 r=["abias0", "const"], w=["abias"])
        S.dma("pool", lambda e: e.dma_start(out=self.ptsb[:], in_=self.ins["pt"][0].partition_broadcast(128)), S.dsem(), w=["pt"])
        S.op("pool", lambda e: e.iota(self.iota_p[:], pattern=[[0, 1]], base=0, channel_multiplier=1), w=["iota"])
        S.op("pool", lambda e: e.tensor_scalar(self.pidx[:], self.ptsb[:], 128, self.iota_p[:, 0:1], ALU.mult, ALU.add), r=["pt", "iota"], w=["pidx"])
        S.op("pool", lambda e: e.memset(self.ccarry[:, :, :], 0.0), w=["ccarry"])
        S.dma("sp", lambda e: e.dma_start(out=self.blkmask[:], in_=self.ins["c_blkmask"]), S.dsem(), w=["const6"])
        S.op("dve", lambda e: e.memset(self.hpi_t[:], float(np.pi / 2)), r=["const6"], w=["const7"])
        for a in range(8):
            S.op("dve", lambda e, a=a: e.memset(self.Zb[a][:], 0.0), w=["Zb"])
            S.op("dve", lambda e, a=a: e.tensor_copy(self.Zb[a][:, 112:128], self.ident_f[:, a * 16:(a + 1) * 16]), r=["const", "Zb"], w=["Zb"])
        for jj in range(2):
            src = self.ins["ssm_d"][jj].rearrange("(c p) -> p c", p=128)
            S.dma("sp", lambda e, jj=jj, src=src: e.dma_start(out=self.dvv[:, jj * 16:(jj + 1) * 16], in_=src, allow_slow_non_contiguous=True), S.dsem(), w=["gv"])
            for i in range(2):
                S.op("dve", lambda e, jj=jj, i=i: e.memset(self.s5state[jj]["H"][i][:], 0.0), w=[("H", i)])
                S.op("pool", lambda e, jj=jj, i=i: e.memset(self.s5state[jj]["Hs"][i][:], 0.0), w=[("Hs", i)])
        self.barrier()
        if self.stage >= 5:
            for jj in range(2):
                self.barrier()
                self.s5_prep(jj)
            self.barrier()
        self.load_cols(self.ins["sconv_in"], 2, lambda: self.sconv[:, :, :], ["sconv"])

        for tile in range(NTILES):
            ws = (tile == NTILES - 1)
            ncols = NT if ws else TT
            self.load_tile(tile, ws)
            for li in range(DEPTH):
                kind = li % 3
                if kind == 0 and self.stage >= 5:
                    self.rmsnorm((li * 2) * 16, ncols)
                    self.barrier()
                    self.s5_apply(li // 3, li, tile, ncols, ws)
                    self.barrier()
                if (kind == 1 and self.stage >= 2) or (kind == 2 and self.stage >= 3):
                    self.rmsnorm((li * 2) * 16, ncols)
                    self.barrier()
                    if kind == 1:
                        self.conv_mixer(tile, ncols, ws)
                    else:
                        if tile == 0:
                            S.dma("sp", lambda e: e.dma_start(out=self.masks[:, :, :], in_=self.ins["c_masks"]), csem, w=["masks"]) if False else None
                        self.attn_consts()
                        self.attention(tile, ncols, ws)
                    self.barrier()
                self.rmsnorm((li * 2 + 1) * 16, ncols)
                self.mlp(li, ncols)
                if self.stage == 0:
                    break
            self.store_tile(tile, ws)
        S.final_wait("sp", self.all_dsems)
        S.flush()
        return nc

    def attn_consts(self):
        S = self.S
        st = self.stage_f[0]
        for kb in range(TT // 128):
            S.dma("sp", lambda e, kb=kb: e.dma_start(out=st[:, 0:TT], in_=self.ins["c_masks"][:, kb, :]), self.ssem[0], w=[("stage", 0)])
            S.op("dve", lambda e, kb=kb: e.tensor_copy(self.masks[:, kb, :], st[:, 0:TT]), r=[("stage", 0)], w=["masks"])
        S.dma("sp", lambda e: e.dma_start(out=st[:, 0:16 * NS], in_=self.ins["c_smask"]), self.ssem[0], w=[("stage", 0)])
        S.op("dve", lambda e: e.tensor_copy(self.smask[:, :], st[:, 0:16 * NS]), r=[("stage", 0)], w=["masks"])
        S.op("pool", lambda e: e.memset(self.Vt[:, TT // 128, :], 0.0), w=[("Vt", TT // 128)])


_CONST = {}


def _consts():
    if not _CONST:
        _CONST["c_ident"] = np.eye(128, dtype=np.float32)
        jj = np.arange(128)
        _CONST["c_tri"] = (jj[:, None] >= jj[None, :]).astype(np.float32)
        t = np.arange(TT)
        m = np.zeros((128, TT // 128, TT), np.float32)
        for kb in range(TT // 128):
            m[:, kb, :] = ((kb * 128 + jj)[:, None] < t[None, :])
        _CONST["c_masks"] = m
        sm = np.zeros((128, 16 * NS), np.float32)
        for h in range(16):
            for q in range(NS):
                sm[:q, h * NS + q] = 1.0
        _CONST["c_smask"] = sm
        ii = jj // 16
        _CONST["c_blkmask"] = (ii[None, :] >= ii[:, None]).astype(np.float32)

    return _CONST


_SHARED = ("ssm_lambda_re", "ssm_lambda_im", "ssm_log_dt", "ssm_b_re", "ssm_b_im", "ssm_c_re", "ssm_c_im", "ssm_d", "ssm_w_glu",
           "norm_mix", "norm_mlp", "mlp_w_up", "mlp_w_down", "conv_w_in", "conv_w_dw", "conv_w_out",
           "attn_w_qkv", "attn_q_norm", "attn_k_norm", "attn_sb_bias", "attn_w_o")


def make_in_maps(inputs, cores, with_cache=True):
    cst = _consts()
    ck = np.ascontiguousarray(inputs["cache_k"][0]) if with_cache else None
    cv = np.ascontiguousarray(inputs["cache_v"][0]) if with_cache else None
    in_maps = []
    for c in cores:
        m = {
            "xp": np.ascontiguousarray(inputs["x_prompt"][c % 4]),
            "xs": np.ascontiguousarray(inputs["x_sample"][c]),
            "sconv_in": np.ascontiguousarray(inputs["state_conv"][0, c]),
            "pt": np.ascontiguousarray(inputs["page_table"][c:c + 1]).astype(np.int32),
            "sst_re": np.ascontiguousarray(inputs["state_ssm_re"][:, c]),
            "sst_im": np.ascontiguousarray(inputs["state_ssm_im"][:, c]),
        }
        if with_cache:
            m["cache_k"] = ck
            m["cache_v"] = cv
        for k in _SHARED:
            m[k] = inputs[k]
        m.update(cst)
        in_maps.append(m)
    return in_maps


def kernel(**inputs):
    inputs = {k: np.asarray(v) for k, v in inputs.items()}
    b = Builder()
    nc = b.build()
    in_maps = make_in_maps(inputs, list(range(8)))
    res = run_bass_kernel_spmd(nc, in_maps, core_ids=list(range(8)))
    rs = res.results
    f = np.float32
    y_p = np.stack([rs[c]["yp"] for c in range(4)]).astype(f)
    y_s = np.stack([rs[c]["ys"] for c in range(8)]).astype(f)
    ssm_re_p = np.stack([rs[c]["ssm_re_p"] for c in range(4)], axis=1).astype(f)
    ssm_im_p = np.stack([rs[c]["ssm_im_p"] for c in range(4)], axis=1).astype(f)
    ssm_re_s = np.stack([rs[c]["ssm_re_s"] for c in range(8)], axis=1).astype(f)
    ssm_im_s = np.stack([rs[c]["ssm_im_s"] for c in range(8)], axis=1).astype(f)
    conv_p = np.stack([rs[c]["conv_p"] for c in range(4)])[None].astype(f)
    conv_s = np.stack([rs[c]["conv_s"] for c in range(8)])[None].astype(f)
    k_p = np.stack([rs[c]["kp"] for c in range(4)]).reshape(1, 4, SEQ, NH, 128).astype(f)
    v_p = np.stack([rs[c]["vp"] for c in range(4)]).reshape(1, 4, SEQ, NH, 128).astype(f)
    k_s = np.stack([rs[c]["ks"] for c in range(8)]).reshape(1, 8, NS, NH, 128).astype(f)
    v_s = np.stack([rs[c]["vs"] for c in range(8)]).reshape(1, 8, NS, NH, 128).astype(f)
    return (y_p, y_s, ssm_re_p, ssm_im_p, ssm_re_s, ssm_im_s, conv_p, conv_s, k_p, v_p, k_s, v_s)
```

```python
import numpy as np
import ml_dtypes
from contextlib import ExitStack
import concourse.bass as bass
import concourse.mybir as mybir
from concourse.bass_utils import run_bass_kernel_spmd

F32 = mybir.dt.float32
BF16 = mybir.dt.bfloat16
I32 = mybir.dt.int32
AF = mybir.ActivationFunctionType
ALU = mybir.AluOpType
AX = mybir.AxisListType

D = 2048
DC = 16
SEQ = 2048
TT = 512
NTILES = SEQ // TT
NS = 4
NT = TT + NS
FF = 8192
EPS = 1e-6
DEPTH = 4
NH = 16
PAST = 16384
NPAGES = 128
LCH = 8
NCH = TT // LCH
SW = 256


class DSem:
    def __init__(self, h):
        self.h = h
        self.count = 0


class Sched:
    ENG = ("pe", "act", "dve", "pool", "sp")

    def __init__(self, nc, es):
        self.nc = nc
        self.es = es
        self.ops = {e: [] for e in self.ENG}
        self.sem = {e: es.enter_context(nc.semaphore("c_" + e)) for e in self.ENG}
        self.cnt = {e: 0 for e in self.ENG}
        self.waited = {e: {} for e in self.ENG}
        self.lastw = {}
        self.readers = {}
        self.nsem = 0

    def dsem(self, name=None):
        self.nsem += 1
        return DSem(self.es.enter_context(self.nc.semaphore(name or ("d%d" % self.nsem))))

    def _deps(self, eng, r, w):
        need = {}

        def add(sv):
            if sv is None:
                return
            s, v = sv
            k = s.name
            if k not in need or need[k][1] < v:
                need[k] = (s, v)
        for t in r:
            add(self.lastw.get(t))
        for t in w:
            add(self.lastw.get(t))
            for sv in self.readers.get(t, ()):
                add(sv)
        out = []
        wd = self.waited[eng]
        for k, (s, v) in need.items():
            if wd.get(k, 0) < v:
                wd[k] = v
                out.append((s, v))
        return out

    def _mark(self, r, w, sv):
        for t in w:
            self.lastw[t] = sv
            self.readers[t] = []
        for t in r:
            self.readers.setdefault(t, []).append(sv)
            if len(self.readers[t]) > 12:
                best = {}
                for s, v in self.readers[t]:
                    if s.name not in best or best[s.name][1] < v:
                        best[s.name] = (s, v)
                self.readers[t] = list(best.values())

    def op(self, eng, fn, r=(), w=()):
        waits = self._deps(eng, r, w)
        self.cnt[eng] += 1
        sem = self.sem[eng]
        sv = (sem, self.cnt[eng])

        def emit(e, fn=fn, waits=waits, sem=sem):
            for s, v in waits:
                e.wait_ge(s, v)
            fn(e).then_inc(sem, 1)
        self.ops[eng].append(emit)
        self._mark(r, w, sv)

    def group(self, eng, fns, r=(), w=()):
        waits = self._deps(eng, r, w)
        self.cnt[eng] += 1
        sem = self.sem[eng]
        sv = (sem, self.cnt[eng])

        def emit(e, fns=fns, waits=waits, sem=sem):
            for s, v in waits:
                e.wait_ge(s, v)
            for f in fns[:-1]:
                f(e)
            fns[-1](e).then_inc(sem, 1)
        self.ops[eng].append(emit)
        self._mark(r, w, sv)

    def dma(self, q, fn, ds, r=(), w=()):
        waits = self._deps(q, r, w)
        ds.count += 16
        sv = (ds.h, ds.count)

        def emit(e, fn=fn, waits=waits, h=ds.h):
            for s, v in waits:
                e.wait_ge(s, v)
            fn(e).then_inc(h, 16)
        self.ops[q].append(emit)
        self._mark(r, w, sv)

    def final_wait(self, eng, dsems):
        def emit(e, dsems=dsems):
            for d in dsems:
                if d.count:
                    e.wait_ge(d.h, d.count)
        self.ops[eng].append(emit)

    def flush(self):
        with self.nc.Block() as block:
            ops = self.ops

            @block.tensor
            def _(e):
                for f in ops["pe"]:
                    f(e)

            @block.scalar
            def _(e):
                for f in ops["act"]:
                    f(e)

            @block.vector
            def _(e):
                for f in ops["dve"]:
                    f(e)

            @block.gpsimd
            def _(e):
                for f in ops["pool"]:
                    f(e)

            @block.sync
            def _(e):
                for f in ops["sp"]:
                    f(e)


class Builder:
    def __init__(self, stage=99, dump=None):
        self.stage = stage
        self.dump = dump
        self.nc = bass.Bass("TRN2", target_bir_lowering=False)
        self.es = ExitStack()
        self.S = None
        self.ins = {}
        self.outs = {}
        self.out_sems = []

    def din(self, name, shape, dt=F32):
        ap = self.nc.dram_tensor(name, list(shape), dt, kind="ExternalInput").ap()
        self.ins[name] = ap
        return ap

    def dout(self, name, shape, dt=F32):
        ap = self.nc.dram_tensor(name, list(shape), dt, kind="ExternalOutput").ap()
        self.outs[name] = ap
        return ap

    def sb(self, name, shape, dt):
        return self.es.enter_context(self.nc.sbuf_tensor(name, list(shape), dt))

    def ps(self, name, shape, dt=F32):
        return self.es.enter_context(self.nc.psum_tensor(name, list(shape), dt))

    def bank(self):
        k = self.bank_i % len(self.banks)
        self.bank_i += 1
        return k, self.banks[k], ("ps", k)

    def load_slab(self, wap, r0, c0, ncols=SW, big=False):
        S = self.S
        if big:
            k = self.wb_i % 2
            self.wb_i += 1
            buf, sem, key = self.wbig[k], self.wbigsem[k], ("wbig", k)
        else:
            k = self.w_i % len(self.wbuf)
            self.w_i += 1
            buf, sem, key = self.wbuf[k], self.wsem[k], ("w", k)
        src = wap[r0:r0 + 2048, c0:c0 + ncols].rearrange("(kc p) n -> p kc n", p=128)
        S.dma("pool", lambda e, buf=buf, src=src, ncols=ncols: e.dma_start(out=buf[:, :, 0:ncols], in_=src),
              sem, w=[key])
        return buf, key

    def ntiles(self, ncols):
        out = []
        n0 = 0
        while n0 < ncols:
            n = min(512, ncols - n0)
            out.append((n0, n))
            n0 += n
        return out

    def linear_fm(self, wap, r0, c0, nout_chunks, src, src_key, ncols, evac, sw=SW, big=False):
        S = self.S
        per = sw // 128
        slabs = list(range(0, nout_chunks, per))
        nxt = self.load_slab(wap, r0, c0 + slabs[0] * 128, sw, big=big)
        for si_, s0 in enumerate(slabs):
            buf, wkey = nxt
            if si_ + 1 < len(slabs):
                nxt = self.load_slab(wap, r0, c0 + slabs[si_ + 1] * 128, sw, big=big)
            for mi in range(per):
                m = s0 + mi
                for (n0, n) in self.ntiles(ncols):
                    bk, psb, pskey = self.bank()
                    fns = []
                    for kc in range(16):
                        fns.append(lambda e, psb=psb, buf=buf, kc=kc, mi=mi, n0=n0, n=n, src=src:
                                   e.matmul(psb[:, 0:n], buf[:, kc, mi * 128:(mi + 1) * 128], src[:, kc, n0:n0 + n],
                                            start=(kc == 0), stop=(kc == 15)))
                    S.group("pe", fns, r=[wkey] + [(src_key, kc) for kc in range(16)], w=[pskey])
                    evac(m, n0, n, psb[:, 0:n], pskey)

    def rmsnorm(self, gcol, ncols):
        S = self.S
        R, xn, rstd = self.R, self.xn, self.rstd
        nts = self.ntiles(ncols)
        pbanks = [self.bank() for _ in nts]
        for c in range(16):
            sq = self.sq[0]
            sk = ("sq", 0)
            S.op("act", lambda e, sq=sq, c=c: e.activation(sq[:, 0:ncols], R[:, c, 0:ncols], AF.Square),
                 r=[("R", c)], w=[sk])
            for (n0, n), (bk, psb, pskey) in zip(nts, pbanks):
                S.op("pe", lambda e, psb=psb, sq=sq, n0=n0, n=n, c=c:
                     e.matmul(psb[:, 0:n], self.ones_f[:], sq[:, n0:n0 + n], start=(c == 0), stop=(c == 15)),
                     r=[sk, "const"], w=[pskey])
        for (n0, n), (bk, psb, pskey) in zip(nts, pbanks):
            S.op("act", lambda e, psb=psb, n0=n0, n=n: e.activation(rstd[:, n0:n0 + n], psb[:, 0:n], AF.Sqrt,
                                                                      bias=self.eps_t[:, 0:1], scale=1.0 / D),
                 r=[pskey, "const"], w=["rstd"])
        S.op("dve", lambda e: e.reciprocal(rstd[:, 0:ncols], rstd[:, 0:ncols]), r=["rstd"], w=["rstd"])
        for c in range(16):
            S.op("dve", lambda e, c=c: e.scalar_tensor_tensor(xn[:, c, 0:ncols], R[:, c, 0:ncols],
                                                           self.gv[:, gcol + c:gcol + c + 1], rstd[:, 0:ncols],
                                                           ALU.mult, ALU.mult),
                 r=[("R", c), "rstd", "gv"], w=[("xn", c)])

    def mlp(self, li, ncols):
        S = self.S
        R, xn, hid = self.R, self.xn, self.hid
        w_up = self.ins["mlp_w_up"]
        w_dn = self.ins["mlp_w_down"]
        for q in range(4):
            def evac_up(m, n0, n, ps, pskey):
                t = self.tmpb[self.tmp_i % 2]
                tk = ("tmpb", self.tmp_i % 2)
                self.tmp_i += 1
                S.op("act", lambda e, t=t, ps=ps, n=n: e.activation(t[:, 0:n], ps, AF.Relu), r=[pskey], w=[tk])
                S.op("dve", lambda e, t=t, m=m, n0=n0, n=n: e.tensor_tensor(hid[:, m, n0:n0 + n], t[:, 0:n], t[:, 0:n], ALU.mult),
                     r=[tk], w=[("hid", m)])
            self.linear_fm(w_up[li], 0, q * 2048, 16, xn, "xn", ncols, evac_up, sw=1024, big=True)

            def evac_dn(m, n0, n, ps, pskey):
                S.op("dve", lambda e, m=m, n0=n0, n=n, ps=ps: e.tensor_tensor(R[:, m, n0:n0 + n], ps, R[:, m, n0:n0 + n], ALU.add),
                     r=[pskey, ("R", m)], w=[("R", m)])
            self.linear_fm(w_dn[li], q * 2048, 0, 16, hid, "hid", ncols, evac_dn, sw=1024, big=True)

    def load_tile(self, tile, with_sample):
        S = self.S
        R = self.R
        xp = self.ins["xp"]
        for tb in range(TT // 128):
            st = self.stage_f[0]
            sk = ("stage", 0)
            t0 = tile * TT + tb * 128
            S.dma("sp", lambda e, st=st, t0=t0: e.dma_start(out=st[:, :], in_=xp[t0:t0 + 128, :]), self.ssem[0], w=[sk])
            for g4 in range(4):
                bk, psb, pskey = self.bank()
                fns = []
                for j in range(4):
                    c = g4 * 4 + j
                    fns.append(lambda e, psb=psb, st=st, c=c, j=j: e.transpose(psb[:, j * 128:(j + 1) * 128], st[:, c * 128:(c + 1) * 128], self.ident_f[:]))
                S.group("pe", fns, r=[sk, "const"], w=[pskey])
                eng = "act" if g4 % 2 == 0 else "dve"
                if eng == "act":
                    S.op("act", lambda e, psb=psb, g4=g4, tb=tb: e.activation(
                        R[:, g4 * 4:(g4 + 1) * 4, tb * 128:(tb + 1) * 128], psb[:, 0:512].rearrange("p (j n) -> p j n", j=4), AF.Copy),
                        r=[pskey], w=[("R", g4 * 4 + j) for j in range(4)])
                else:
                    S.op("dve", lambda e, psb=psb, g4=g4, tb=tb: e.tensor_copy(
                        R[:, g4 * 4:(g4 + 1) * 4, tb * 128:(tb + 1) * 128], psb[:, 0:512].rearrange("p (j n) -> p j n", j=4)),
                        r=[pskey], w=[("R", g4 * 4 + j) for j in range(4)])
        if with_sample:
            xs = self.ins["xs"]
            st = self.stage_f[0]
            sk = ("stage", 0)
            S.dma("sp", lambda e, st=st: e.dma_start(out=st[0:NS, :], in_=xs[:, :]), self.ssem[0], w=[sk])
            bk, psb, pskey = self.bank()
            fns = []
            for c in range(16):
                fns.append(lambda e, psb=psb, st=st, c=c: e.transpose(psb[:, c * NS:(c + 1) * NS], st[0:NS, c * 128:(c + 1) * 128], self.ident_f[0:NS, 0:NS]))
            S.group("pe", fns, r=[sk, "const"], w=[pskey])
            S.op("dve", lambda e, psb=psb: e.tensor_copy(R[:, :, TT:TT + NS], psb[:, 0:16 * NS].rearrange("p (c n) -> p c n", c=16)),
                 r=[pskey], w=[("R", c) for c in range(16)])

    def store_tile(self, tile, with_sample):
        S = self.S
        R = self.R
        yp = self.outs["yp"]
        for tb in range(TT // 128):
            st = self.stage_f[0]
            sk = ("stage", 0)
            for g4 in range(4):
                bk, psb, pskey = self.bank()
                fns = []
                for j in range(4):
                    c = g4 * 4 + j
                    fns.append(lambda e, psb=psb, c=c, j=j, tb=tb: e.transpose(psb[:, j * 128:(j + 1) * 128], R[:, c, tb * 128:(tb + 1) * 128], self.ident_f[:]))
                S.group("pe", fns, r=[("R", g4 * 4 + j) for j in range(4)] + ["const"], w=[pskey])
                if g4 % 2 == 0:
                    S.op("act", lambda e, psb=psb, st=st, g4=g4: e.activation(st[:, g4 * 512:(g4 + 1) * 512], psb[:, 0:512], AF.Copy), r=[pskey], w=[sk])
                else:
                    S.op("dve", lambda e, psb=psb, st=st, g4=g4: e.tensor_copy(st[:, g4 * 512:(g4 + 1) * 512], psb[:, 0:512]), r=[pskey], w=[sk])
            t0 = tile * TT + tb * 128
            S.dma("sp", lambda e, st=st, t0=t0: e.dma_start(out=yp[t0:t0 + 128, :], in_=st[:, :]), self.osem, r=[sk])
        if with_sample:
            ys = self.outs["ys"]
            st = self.stage_f[0]
            sk = ("stage", 0)
            for g4 in range(4):
                bk, psb, pskey = self.bank()
                fns = []
                for j in range(4):
                    c = g4 * 4 + j
                    fns.append(lambda e, psb=psb, c=c, j=j: e.transpose(psb[0:NS, j * 128:(j + 1) * 128], R[:, c, TT:TT + NS], self.ident_f[:]))
                S.group("pe", fns, r=[("R", g4 * 4 + j) for j in range(4)] + ["const"], w=[pskey])
                S.op("dve", lambda e, psb=psb, st=st, g4=g4: e.tensor_copy(st[0:NS, g4 * 512:(g4 + 1) * 512], psb[0:NS, 0:512]), r=[pskey], w=[sk])
            S.dma("sp", lambda e, st=st: e.dma_start(out=ys[:, :], in_=st[0:NS, :]), self.osem, r=[sk])


    def load_cols(self, src_rows, k, dst_fn, wkeys, q="sp"):
        S = self.S
        st = self.stage_f[0]
        sk = ("stage", 0)
        S.dma(q, lambda e: e.dma_start(out=st[0:k, :], in_=src_rows), self.ssem[0], w=[sk])
        bk, psb, pskey = self.bank()
        fns = []
        for c in range(16):
            fns.append(lambda e, c=c: e.transpose(psb[:, c * k:(c + 1) * k], st[0:k, c * 128:(c + 1) * 128], self.ident_f[0:k, 0:k]))
        S.group("pe", fns, r=[sk, "const"], w=[pskey])
        S.op("dve", lambda e: e.tensor_copy(dst_fn(), psb[:, 0:16 * k].rearrange("p (c n) -> p c n", c=16)), r=[pskey], w=wkeys)

    def store_cols(self, src_fn, k, dst_rows, rkeys, osem=None):
        S = self.S
        st = self.stage_f[0]
        sk = ("stage", 0)
        for g4 in range(4):
            bk, psb, pskey = self.bank()
            fns = []
            for j in range(4):
                c = g4 * 4 + j
                fns.append(lambda e, c=c, j=j, psb=psb: e.transpose(psb[0:k, j * 128:(j + 1) * 128], src_fn(c), self.ident_f[:]))
            S.group("pe", fns, r=list(rkeys) + ["const"], w=[pskey])
            S.op("dve", lambda e, g4=g4, psb=psb: e.tensor_copy(st[0:k, g4 * 512:(g4 + 1) * 512], psb[0:k, 0:512]), r=[pskey], w=[sk])
        S.dma("sp", lambda e: e.dma_start(out=dst_rows, in_=st[0:k, :]), osem or self.osem, r=[sk])

    def evac_add_R(self):
        S = self.S
        R = self.R

        def ev(m, n0, n, ps, pskey):
            S.op("dve", lambda e: e.tensor_tensor(R[:, m, n0:n0 + n], ps, R[:, m, n0:n0 + n], ALU.add),
                 r=[pskey, ("R", m)], w=[("R", m)])
        return ev

    def conv_mixer(self, tile, ncols, ws):
        S = self.S
        F1, xn, hid = self.F1, self.xn, self.hid
        w_in = self.ins["conv_w_in"][0]
        w_out = self.ins["conv_w_out"][0]
        allF = [("F1", m) for m in range(16)]

        def col(n0):
            return 2 + n0 if n0 < TT else n0 + 4
        S.op("pool", lambda e: e.tensor_copy(F1[:, :, 0:2], self.ccarry[:, :, :]), r=["ccarry"], w=allF)
        if ws:
            S.op("pool", lambda e: e.tensor_copy(F1[:, :, TT + 2:TT + 4], self.sconv[:, :, :]), r=["sconv"], w=allF)

        def evA(m, n0, n, ps, pskey):
            c0 = col(n0)
            S.op("act", lambda e: e.activation(F1[:, m, c0:c0 + n], ps, AF.Copy), r=[pskey], w=[("F1", m)])
        self.linear_fm(w_in, 0, 2048, 16, xn, "xn", ncols, evA)

        def evB(m, n0, n, ps, pskey):
            c0 = col(n0)
            S.op("dve", lambda e: e.tensor_tensor(F1[:, m, c0:c0 + n], ps, F1[:, m, c0:c0 + n], ALU.mult), r=[pskey, ("F1", m)], w=[("F1", m)])
        self.linear_fm(w_in, 0, 4096, 16, xn, "xn", ncols, evB)

        def evC(m, n0, n, ps, pskey):
            c0 = col(n0)
            t = self.tmpf[self.tmpf_i % 2]
            tk = ("tmpf", self.tmpf_i % 2)
            self.tmpf_i += 1
            S.op("dve", lambda e: e.tensor_scalar(t[:, 0:n], F1[:, m, c0 - 2:c0 - 2 + n], self.dwv[:, m:m + 1], None, ALU.mult), r=[("F1", m), "gv"], w=[tk])
            S.op("dve", lambda e: e.scalar_tensor_tensor(t[:, 0:n], F1[:, m, c0 - 1:c0 - 1 + n], self.dwv[:, 16 + m:17 + m], t[:, 0:n], ALU.mult, ALU.add), r=[("F1", m), tk, "gv"], w=[tk])
            S.op("dve", lambda e: e.scalar_tensor_tensor(t[:, 0:n], F1[:, m, c0:c0 + n], self.dwv[:, 32 + m:33 + m], t[:, 0:n], ALU.mult, ALU.add), r=[("F1", m), tk, "gv"], w=[tk])
            S.op("dve", lambda e: e.tensor_tensor(hid[:, m, n0:n0 + n], ps, t[:, 0:n], ALU.mult), r=[pskey, tk], w=[("hid", m)])
        self.linear_fm(w_in, 0, 0, 16, xn, "xn", ncols, evC)
        self.linear_fm(w_out, 0, 0, 16, hid, "hid", ncols, self.evac_add_R())
        S.op("pool", lambda e: e.tensor_copy(self.ccarry[:, :, :], F1[:, :, TT:TT + 2]), r=allF, w=["ccarry"])
        if ws:
            self.store_cols(lambda c: self.ccarry[:, c, :], 2, self.outs["conv_p"], ["ccarry"])
            self.store_cols(lambda c: F1[:, c, TT + 6:TT + 8], 2, self.outs["conv_s"], allF)

    def tm_proj(self, wap, c0, nslabs, tok_blocks, evac):
        S = self.S
        xn = self.xn
        nxt = self.load_slab(wap, 0, c0, SW)
        for sl in range(nslabs):
            buf, wkey = nxt
            if sl + 1 < nslabs:
                nxt = self.load_slab(wap, 0, c0 + (sl + 1) * SW, SW)
            for (t0, ntok) in tok_blocks:
                bk, psb, pskey = self.bank()
                fns = []
                for kc in range(16):
                    fns.append(lambda e, psb=psb, buf=buf, kc=kc, t0=t0, ntok=ntok: e.matmul(
                        psb[0:ntok, 0:SW], xn[:, kc, t0:t0 + ntok], buf[:, kc, 0:SW], start=(kc == 0), stop=(kc == 15)))
                S.group("pe", fns, r=[wkey] + [("xn", kc) for kc in range(16)], w=[pskey])
                evac(sl, t0, ntok, psb, pskey)

    def attn_block(self, zrhs, zk, N, kT_ap, kkeys, v_ap, vkeys, mask, bias_ap, carry, ckey, o_ps, okey, first, last, tri):
        S = self.S
        scale = 128.0 ** -0.5
        bk, zps, zkey = self.bank()
        S.op("pe", lambda e: e.matmul(zps[:, 0:N], kT_ap, zrhs, start=True, stop=True), r=list(kkeys) + list(zk), w=[zkey])
        i = self.ab_i % 2
        self.ab_i += 1
        E, L, Lb, T1, Ab = self.abE[0], self.abL[0], self.abLb[i], self.abT[0], self.abA[i]
        ek, lk, lbk, tk, ak = ("abE", 0), ("abL", 0), ("abLb", i), ("abT", 0), ("abA", i)
        S.op("act", lambda e: e.activation(E[:, 0:N], zps[:, 0:N], AF.Exp, bias=bias_ap, scale=scale), r=[zkey, "abias"], w=[ek])
        S.op("act", lambda e: e.activation(L[:, 0:N], E[:, 0:N], AF.Ln, bias=self.one_t[:, 0:1]), r=[ek, "const"], w=[lk])
        if mask is not None:
            S.op("dve", lambda e: e.tensor_tensor(Lb[:, 0:N], L[:, 0:N], mask, ALU.mult), r=[lk, "masks"], w=[lbk])
        else:
            S.op("dve", lambda e: e.tensor_copy(Lb[:, 0:N], L[:, 0:N]), r=[lk], w=[lbk])
        bk2, sps, skey = self.bank()
        S.op("pe", lambda e: e.matmul(sps[:, 0:N], tri, Lb[:, 0:N], start=True, stop=True), r=[lbk, "const"], w=[skey])
        bk3, cps, cpkey = self.bank()
        S.op("pe", lambda e: e.matmul(cps[:, 0:N], self.ones_b[:], Lb[:, 0:N], start=True, stop=True), r=[lbk, "const"], w=[cpkey])
        if first:
            S.op("dve", lambda e: e.tensor_scalar(T1[:, 0:N], zps[:, 0:N], scale, None, ALU.mult), r=[zkey], w=[tk])
        else:
            S.op("dve", lambda e: e.scalar_tensor_tensor(T1[:, 0:N], zps[:, 0:N], scale, carry, ALU.mult, ALU.subtract), r=[zkey, ckey], w=[tk])
        S.op("dve", lambda e: e.tensor_tensor(T1[:, 0:N], T1[:, 0:N], sps[:, 0:N], ALU.subtract), r=[tk, skey], w=[tk])
        if mask is not None:
            S.op("act", lambda e: e.activation(E[:, 0:N], T1[:, 0:N], AF.Exp, bias=bias_ap), r=[tk, "abias"], w=[ek])
            S.op("dve", lambda e: e.tensor_tensor(Ab[:, 0:N], E[:, 0:N], mask, ALU.mult), r=[ek, "masks"], w=[ak])
        else:
            S.op("act", lambda e: e.activation(Ab[:, 0:N], T1[:, 0:N], AF.Exp, bias=bias_ap), r=[tk, "abias"], w=[ak])
        if first:
            S.op("dve", lambda e: e.tensor_copy(carry, cps[:, 0:N]), r=[cpkey], w=[ckey])
        else:
            S.op("dve", lambda e: e.tensor_tensor(carry, carry, cps[:, 0:N], ALU.add), r=[cpkey, ckey], w=[ckey])
        S.op("pe", lambda e: e.matmul(o_ps, v_ap, Ab[:, 0:N], start=first, stop=last), r=[ak] + list(vkeys), w=[okey])

    def attention(self, tile, ncols, ws):
        S = self.S
        xn, hid, F1 = self.xn, self.hid, self.F1
        wqkv = self.ins["attn_w_qkv"][0]
        wo = self.ins["attn_w_o"][0]
        qT, kT, Vt = self.qT, self.kT, self.Vt
        tok_blocks = [(tb * 128, 128) for tb in range(TT // 128)] + ([(TT, NS)] if ws else [])
        kout, vout = self.outs["kp"], self.outs["vp"]

        import os
        def norm_evac(which):
            dst = qT if which == 0 else kT
            dkey = "qT" if which == 0 else "kT"
            gain = self.qgain if which == 0 else self.kgain

            def ev(sl, t0, ntok, psb, pskey):
                i = self.nq_i % 2
                self.nq_i += 1
                sqt, ssq, nrm = self.nsq[i], self.nss[i], self.nrm[i]
                NE = int(os.environ.get("NE", "9"))
                S.op("act", lambda e: e.activation(sqt[0:ntok, :], psb[0:ntok, 0:SW], AF.Square), r=[pskey], w=[("nsq", i)])
                if NE <= 1:
                    return
                S.op("dve", lambda e: e.tensor_reduce(ssq[0:ntok, :], sqt[0:ntok, :].rearrange("p (h d) -> p h d", h=2), AX.X, ALU.add), r=[("nsq", i)], w=[("nss", i)])
                S.op("act", lambda e: e.activation(ssq[0:ntok, :], ssq[0:ntok, :], AF.Sqrt, bias=self.eps_t[0:ntok, 0:1], scale=1.0 / 128), r=[("nss", i), "const"], w=[("nss", i)])
                S.op("dve", lambda e: e.reciprocal(ssq[0:ntok, :], ssq[0:ntok, :]), r=[("nss", i)], w=[("nss", i)])
                if NE <= 2:
                    return
                for h2 in range(2):
                    S.op("dve", lambda e, h2=h2: e.scalar_tensor_tensor(nrm[0:ntok, h2 * 128:(h2 + 1) * 128], psb[0:ntok, h2 * 128:(h2 + 1) * 128],
                                                                      ssq[0:ntok, h2:h2 + 1], gain[0:ntok, :], ALU.mult, ALU.mult),
                         r=[pskey, ("nss", i), "gains"], w=[("nrm", i)])
                if which == 1 and not os.environ.get("NOKDMA"):
                    dstk = self.outs["ks"][:, sl * SW:(sl + 1) * SW] if t0 >= TT else kout[tile * TT + t0:tile * TT + t0 + ntok, sl * SW:(sl + 1) * SW]
                    S.dma("sp", lambda e: e.dma_start(out=dstk, in_=nrm[0:ntok, :]), self.osem_n[i], r=[("nrm", i)])
                if NE <= 3:
                    return
                bk, tps, tkey = self.bank()
                fns = [lambda e, h2=h2: e.transpose(tps[:, h2 * 128:h2 * 128 + ntok], nrm[0:ntok, h2 * 128:(h2 + 1) * 128], self.ident_f[0:ntok, 0:ntok]) for h2 in range(2)]
                S.group("pe", fns, r=[("nrm", i), "const"], w=[tkey])
                if NE <= 4:
                    return
                for h2 in range(2):
                    h = sl * 2 + h2
                    S.op("dve", lambda e, h=h, h2=h2: e.tensor_copy(dst[:, h, t0:t0 + ntok], tps[:, h2 * 128:h2 * 128 + ntok]), r=[tkey], w=[(dkey, h)])
            return ev
        import os
        sub = int(os.environ.get("ATT_SUB", "9"))
        if sub <= 0:
            return
        self.tm_proj(wqkv, 0, 8, tok_blocks, norm_evac(0))
        if sub == 1 and os.environ.get("ATT_Q"):
            return
        self.tm_proj(wqkv, 2048, 8, tok_blocks, norm_evac(1))

        def v_evac(sl, t0, ntok, psb, pskey):
            i = self.nq_i % 2
            self.nq_i += 1
            nrm = self.nrm[i]
            tb = t0 // 128
            S.op("dve", lambda e: e.tensor_copy(nrm[0:ntok, :], psb[0:ntok, 0:SW]), r=[pskey], w=[("nrm", i)])
            S.op("dve", lambda e: e.tensor_copy(Vt[0:ntok, tb, sl * SW:(sl + 1) * SW], psb[0:ntok, 0:SW]), r=[pskey], w=[("Vt", tb)])
            dstv = self.outs["vs"][:, sl * SW:(sl + 1) * SW] if t0 >= TT else vout[tile * TT + t0:tile * TT + t0 + ntok, sl * SW:(sl + 1) * SW]
            S.dma("sp", lambda e: e.dma_start(out=dstv, in_=nrm[0:ntok, :]), self.osem_n[i], r=[("nrm", i)])
        self.tm_proj(wqkv, 4096, 8, tok_blocks, v_evac)
        import os
        sub = int(os.environ.get("ATT_SUB", "9"))
        if sub <= 1:
            return
        if tile < NTILES - 1:
            S.dma("sp", lambda e: e.dma_start(out=self.kT_scr[:, :, tile * TT:(tile + 1) * TT].rearrange("h d t -> d h t"), in_=kT[:, :, 0:TT]),
                  self.scrsem, r=[("kT", h) for h in range(16)], w=[("kscr", tile)])
            S.dma("sp", lambda e: e.dma_start(out=self.v_scr[tile * TT:(tile + 1) * TT, :].rearrange("(tb p) n -> p tb n", p=128), in_=Vt[:, 0:TT // 128, :]),
                  self.scrsem, r=[("Vt", tb) for tb in range(TT // 128)], w=[("vscr", tile)])
        if sub <= 2:
            return
        nprev = tile * TT
        for h in range(16 if sub > 3 else 1):
            pi = 0
            if nprev:
                S.dma("sp", lambda e, h=h: e.dma_start(out=self.kprev[pi][:, 0:nprev], in_=self.kT_scr[h, :, 0:nprev]), self.pvsem[pi],
                      r=[("kscr", t) for t in range(tile)], w=[("kprev", pi)])
                S.dma("sp", lambda e, h=h: e.dma_start(out=self.vprev[pi][:, 0:nprev // 128, :], in_=self.v_scr[0:nprev, h * 128:(h + 1) * 128].rearrange("(tb p) d -> p tb d", p=128)),
                      self.pvsem_v, r=[("vscr", t) for t in range(tile)], w=[("vprev", pi)])
            ci = self.ab_c % 2
            self.ab_c += 1
            carry = self.carry[ci]
            ckey = ("carry", ci)
            obk, ops_, okey = self.obank(ci)
            nblk = TT // 128 + nprev // 128
            bi = 0
            bias_ap = self.abias[:, h:h + 1]
            for kb in reversed(range(TT // 128)):
                self.attn_block(qT[:, h, 0:TT], [("qT", h)], TT, kT[:, h, kb * 128:(kb + 1) * 128], [("kT", h)],
                                Vt[:, kb, h * 128:(h + 1) * 128], [("Vt", kb)], self.masks[:, kb, :], bias_ap,
                                carry[:, 0:TT], ckey, ops_[:, 0:TT], okey, bi == 0, bi == nblk - 1, self.tri_b[:])
                bi += 1
            for pb in reversed(range(nprev // 128)):
                self.attn_block(qT[:, h, 0:TT], [("qT", h)], TT, self.kprev[pi][:, pb * 128:(pb + 1) * 128], [("kprev", pi)],
                                self.vprev[pi][:, pb, :], [("vprev", pi)], None, bias_ap,
                                carry[:, 0:TT], ckey, ops_[:, 0:TT], okey, bi == 0, bi == nblk - 1, self.tri_b[:])
                bi += 1
            S.op("dve", lambda e, h=h, ops_=ops_: e.tensor_copy(hid[:, h, 0:TT], ops_[:, 0:TT]), r=[okey], w=[("hid", h)])
        if ws and self.sample_on:
            self.sample_attention()
        self.linear_fm(wo, 0, 0, 16, hid, "hid", ncols, self.evac_add_R())

    def sample_attention(self):
        S = self.S
        qT, kT, Vt, hid = self.qT, self.kT, self.Vt, self.hid
        scale = 128.0 ** -0.5
        NQ = NS * 16
        pk = self.ins["cache_k"]
        pv = self.ins["cache_v"]
        rows_k = pk.rearrange("a r h d -> (a r) (h d)")
        rows_v = pv.rearrange("a r h d -> (a r) (h d)")
        carry = self.scarry
        obk, ops_, okey = self.obank(0)
        nblk = NPAGES + 1
        for bi in range(nblk):
            first = bi == 0
            last = bi == nblk - 1
            i = bi % 2
            KTp = self.KTp[0]
            kk = ("KTp", 0)
            if first:
                S.op("dve", lambda e: e.memset(KTp[:, :, :], 0.0), w=[kk])
                S.op("dve", lambda e: e.tensor_copy(KTp[:, :, 0:NS], kT[:, :, TT:TT + NS]), r=[("kT", h) for h in range(16)] + [kk], w=[kk])
                vsrc = lambda h: Vt[:, TT // 128, h * 128:(h + 1) * 128]
                vkeys = [("Vt", TT // 128)]
                mask = self.smask[:, :]
            else:
                page = NPAGES - bi
                pg = self.pgK[0]
                S.dma("pool", lambda e, pg=pg, page=page: e.indirect_dma_start(out=pg[:, :], out_offset=None, in_=rows_k,
                      in_offset=bass.IndirectOffsetOnAxis(ap=self.pidx[:, page:page + 1], axis=0)), self.pgsem[0], r=["pidx"], w=[("stage", 0)])
                pgv = self.pgV[i]
                S.dma("pool", lambda e, pgv=pgv, page=page: e.indirect_dma_start(out=pgv[:, :], out_offset=None, in_=rows_v,
                      in_offset=bass.IndirectOffsetOnAxis(ap=self.pidx[:, page:page + 1], axis=0)), self.pgsemV[i], r=["pidx"], w=[("pgV", i)])
                for g4 in range(4):
                    bk, tps, tkey = self.bank()
                    fns = [lambda e, j=j, tps=tps, pg=pg, g4=g4: e.transpose(tps[:, j * 128:(j + 1) * 128], pg[:, (g4 * 4 + j) * 128:(g4 * 4 + j + 1) * 128], self.ident_f[:]) for j in range(4)]
                    S.group("pe", fns, r=[("stage", 0), "const"], w=[tkey])
                    if False:
                        pass
                    else:
                        S.op("dve", lambda e, tps=tps, g4=g4: e.tensor_copy(KTp[:, g4 * 4:(g4 + 1) * 4, :], tps[:, 0:512].rearrange("p (j n) -> p j n", j=4)), r=[tkey], w=[kk])
                vsrc = lambda h, i=i: self.pgV[i][:, h * 128:(h + 1) * 128]
                vkeys = [("pgV", i)]
                mask = None
            bk, zps, zkey = self.bank()
            fns = [lambda e, h=h, zps=zps, KTp=KTp: e.matmul(zps[:, h * 16:h * 16 + NS], KTp[:, h, :], qT[:, h, TT:TT + NS], start=True, stop=True) for h in range(16)]
            S.group("pe", fns, r=[kk] + [("qT", h) for h in range(16)], w=[zkey])
            j = self.ab_i % 2
            self.ab_i += 1
            E, L, Lb, T1, Ab = self.abE[0], self.abL[0], self.abLb[j], self.abT[0], self.abA[j]
            ek, lk, lbk, tk, ak = ("abE", 0), ("abL", 0), ("abLb", j), ("abT", 0), ("abA", j)
            S.op("dve", lambda e, zps=zps, T1=T1: e.scalar_tensor_tensor(T1[:, 0:NQ].rearrange("p (h q) -> p h q", q=NS), zps[:, 0:256].rearrange("p (h c) -> p h c", c=16)[:, :, 0:NS], scale,
                                                                        self.sbiasrow[:, :].rearrange("p (h q) -> p h q", q=NS), ALU.mult, ALU.add), r=[zkey, "abias"], w=[tk])
            S.op("act", lambda e, E=E, T1=T1: e.activation(E[:, 0:NQ], T1[:, 0:NQ], AF.Exp), r=[tk], w=[ek])
            S.op("act", lambda e, E=E, L=L: e.activation(L[:, 0:NQ], E[:, 0:NQ], AF.Ln, bias=self.one_t[:, 0:1]), r=[ek, "const"], w=[lk])
            if mask is not None:
                S.op("dve", lambda e, Lb=Lb, L=L, mask=mask: e.tensor_tensor(Lb[:, 0:NQ], L[:, 0:NQ], mask, ALU.mult), r=[lk, "masks"], w=[lbk])
            else:
                S.op("dve", lambda e, Lb=Lb, L=L: e.tensor_copy(Lb[:, 0:NQ], L[:, 0:NQ]), r=[lk], w=[lbk])
            bk2, sps, skey = self.bank()
            S.op("pe", lambda e, sps=sps, Lb=Lb: e.matmul(sps[:, 0:NQ], self.tri_b[:], Lb[:, 0:NQ], start=True, stop=True), r=[lbk, "const"], w=[skey])
            bk3, cps, cpkey = self.bank()
            S.op("pe", lambda e, cps=cps, Lb=Lb: e.matmul(cps[:, 0:NQ], self.ones_b[:], Lb[:, 0:NQ], start=True, stop=True), r=[lbk, "const"], w=[cpkey])
            if not first:
                S.op("dve", lambda e, T1=T1: e.tensor_tensor(T1[:, 0:NQ], T1[:, 0:NQ], carry[:, 0:NQ], ALU.subtract), r=[tk, "scarry"], w=[tk])
            S.op("dve", lambda e, T1=T1, sps=sps: e.tensor_tensor(T1[:, 0:NQ], T1[:, 0:NQ], sps[:, 0:NQ], ALU.subtract), r=[tk, skey], w=[tk])
            if mask is not None:
                S.op("act", lambda e, E=E, T1=T1: e.activation(E[:, 0:NQ], T1[:, 0:NQ], AF.Exp), r=[tk], w=[ek])
                S.op("dve", lambda e, Ab=Ab, E=E, mask=mask: e.tensor_tensor(Ab[:, 0:NQ], E[:, 0:NQ], mask, ALU.mult), r=[ek, "masks"], w=[ak])
            else:
                S.op("act", lambda e, Ab=Ab, T1=T1: e.activation(Ab[:, 0:NQ], T1[:, 0:NQ], AF.Exp), r=[tk], w=[ak])
            if first:
                S.op("dve", lambda e, cps=cps: e.tensor_copy(carry[:, 0:NQ], cps[:, 0:NQ]), r=[cpkey], w=["scarry"])
            else:
                S.op("dve", lambda e, cps=cps: e.tensor_tensor(carry[:, 0:NQ], carry[:, 0:NQ], cps[:, 0:NQ], ALU.add), r=[cpkey, "scarry"], w=["scarry"])
            bk4, obp, obkey = self.bank()
            fns = [lambda e, h=h, Ab=Ab, vsrc=vsrc, obp=obp: e.matmul(obp[:, h * 16:h * 16 + NS], vsrc(h), Ab[:, h * NS:(h + 1) * NS], start=True, stop=True) for h in range(16)]
            S.group("pe", fns, r=[ak] + vkeys, w=[obkey])
            oview = obp[:, 0:256].rearrange("p (h c) -> p h c", c=16)[:, :, 0:NS]
            oacc = self.soacc[:, :].rearrange("p (h q) -> p h q", q=NS)
            if first:
                S.op("dve", lambda e, oview=oview: e.tensor_copy(oacc, oview), r=[obkey], w=["soacc"])
            else:
                S.op("dve", lambda e, oview=oview: e.tensor_tensor(oacc, oacc, oview, ALU.add), r=[obkey, "soacc"], w=["soacc"])
        ops_ = self.soacc
        okey = "soacc"
        S.op("dve", lambda e: e.tensor_copy(hid[:, :, TT:TT + NS], ops_[:, 0:NQ].rearrange("p (h n) -> p h n", h=16)), r=[okey], w=[("hid", h) for h in range(16)])
        import os
        if os.environ.get("DEBUG_SA"):
            dsm = self.osem
            S.dma("sp", lambda e: e.dma_start(out=self.outs["dbg_pgk"], in_=self.pgK[0][:, :]), dsm, r=[("stage", 0)])
            S.dma("sp", lambda e: e.dma_start(out=self.outs["dbg_pgv"], in_=self.pgV[0][:, :]), dsm, r=[("pgV", 0)])
            S.dma("sp", lambda e: e.dma_start(out=self.outs["dbg_carry"], in_=carry[:, 0:NQ]), dsm, r=["scarry"])
            t = self.tmpf[0]
            S.op("dve", lambda e: e.tensor_copy(t[:, 0:NQ], ops_[:, 0:NQ]), r=[okey], w=[("tmpf", 0)])
            S.dma("sp", lambda e: e.dma_start(out=self.outs["dbg_o"], in_=t[:, 0:NQ]), dsm, r=[("tmpf", 0)])
            S.dma("sp", lambda e: e.dma_start(out=self.outs["dbg_q"], in_=qT[:, :, TT:TT + NS]), dsm, r=[("qT", h) for h in range(16)])
            S.dma("sp", lambda e: e.dma_start(out=self.outs["dbg_ktp"], in_=self.KTp[0][:, :, :]), dsm, r=[("KTp", 0)])


    def s5_prep(self, j):
        S = self.S
        ins = self.ins
        A = {}
        o = [0]

        def f32(name, shape):
            ap = self.av(o[0], shape, F32)
            n = 1
            for d in shape[1:]:
                n *= d
            o[0] += n * 4
            A[name] = ap
            return ap
        for nm in ("lr", "li", "xr", "th", "c", "s", "t0", "t1", "t2", "t3", "nr", "cr", "ci"):
            f32(nm, [128, 64])
        f32("dt", [128, 1])
        uc = f32("uc", [128, 9, 64])
        us = f32("us", [128, 9, 64])
        mk = f32("mk", [128, 16, 64])
        pr = f32("pr", [128, 16, 64])
        pi = f32("pi", [128, 16, 64])
        br = f32("br", [128, 64, 16])
        bi = f32("bi", [128, 64, 16])
        Br = f32("Br", [128, 64, 16])
        Bi = f32("Bi", [128, 64, 16])
        Cr = f32("Cr", [128, 16, 64])
        Ci = f32("Ci", [128, 16, 64])
        tb = f32("tb", [128, 64, 16])
        tb2 = f32("tb2", [128, 64, 16])
        big = f32("big", [128, 64, 128])
        nat = f32("nat", [128, 128])
        f32("nat2", [128, 128])
        eg = [f32("eg%d" % i, [128, 3, 128]) for i in range(2)]
        wo = [self.av(o[0] + i * 1024, [128, 4, 128], BF16) for i in range(2)]
        o[0] += 2048
        K = "s5p"
        sem = self.s5sem
        q = "sp"
        S.dma(q, lambda e: e.dma_start(out=A["lr"], in_=ins["ssm_lambda_re"][j]), sem, w=[K])
        S.dma(q, lambda e: e.dma_start(out=A["li"], in_=ins["ssm_lambda_im"][j]), sem, w=[K])
        S.dma(q, lambda e: e.dma_start(out=A["dt"], in_=ins["ssm_log_dt"][j].rearrange("(g o) -> g o", o=1)), sem, w=[K])
        S.dma(q, lambda e: e.dma_start(out=br, in_=ins["ssm_b_re"][j]), sem, w=[K])
        S.dma(q, lambda e: e.dma_start(out=bi, in_=ins["ssm_b_im"][j]), sem, w=[K])
        S.dma(q, lambda e: e.dma_start(out=Cr, in_=ins["ssm_c_re"][j]), sem, w=[K])
        S.dma(q, lambda e: e.dma_start(out=Ci, in_=ins["ssm_c_im"][j]), sem, w=[K])

        def dv(fn):
            S.op("dve", fn, r=[K], w=[K])

        def ac(fn):
            S.op("act", fn, r=[K, "const"], w=[K])
        ac(lambda e: e.activation(A["dt"], A["dt"], AF.Exp))
        dv(lambda e: e.tensor_scalar(A["xr"], A["lr"], A["dt"][:, 0:1], None, ALU.mult))
        dv(lambda e: e.tensor_scalar(A["th"], A["li"], A["dt"][:, 0:1], None, ALU.mult))
        ac(lambda e: e.activation(A["s"], A["th"], AF.Sin, scale=1.0 / 32))
        ac(lambda e: e.activation(A["c"], A["th"], AF.Sin, bias=self.hpi_t[:, 0:1], scale=1.0 / 32))
        for _ in range(5):
            dv(lambda e: e.tensor_tensor(A["t0"], A["c"], A["c"], ALU.mult))
            dv(lambda e: e.tensor_tensor(A["t1"], A["s"], A["s"], ALU.mult))
            dv(lambda e: e.tensor_tensor(A["t2"], A["c"], A["s"], ALU.mult))
            dv(lambda e: e.tensor_tensor(A["c"], A["t0"], A["t1"], ALU.subtract))
            dv(lambda e: e.tensor_scalar(A["s"], A["t2"], 2.0, None, ALU.mult))
        dv(lambda e: e.memset(uc[:, 0, :], 1.0))
        dv(lambda e: e.memset(us[:, 0, :], 0.0))
        dv(lambda e: e.tensor_copy(uc[:, 1, :], A["c"]))
        dv(lambda e: e.tensor_copy(us[:, 1, :], A["s"]))
        for k in range(2, 9):
            dv(lambda e, k=k: e.tensor_tensor(A["t0"], uc[:, k - 1, :], A["c"], ALU.mult))
            dv(lambda e, k=k: e.tensor_tensor(A["t1"], us[:, k - 1, :], A["s"], ALU.mult))
            dv(lambda e, k=k: e.tensor_tensor(uc[:, k, :], A["t0"], A["t1"], ALU.subtract))
            dv(lambda e, k=k: e.tensor_tensor(A["t0"], uc[:, k - 1, :], A["s"], ALU.mult))
            dv(lambda e, k=k: e.tensor_tensor(A["t1"], us[:, k - 1, :], A["c"], ALU.mult))
            dv(lambda e, k=k: e.tensor_tensor(us[:, k, :], A["t0"], A["t1"], ALU.add))
        for ki in range(16):
            k = ki - 7
            ac(lambda e, ki=ki, k=k: e.activation(mk[:, ki, :], A["xr"], AF.Exp, scale=float(k)))
            dv(lambda e, ki=ki, k=k: e.tensor_tensor(pr[:, ki, :], mk[:, ki, :], uc[:, abs(k), :], ALU.mult))
            dv(lambda e, ki=ki, k=k: e.tensor_tensor(pi[:, ki, :], mk[:, ki, :], us[:, abs(k), :], ALU.mult))
            if k < 0:
                dv(lambda e, ki=ki: e.tensor_scalar(pi[:, ki, :], pi[:, ki, :], -1.0, None, ALU.mult))
        dv(lambda e: e.tensor_scalar(A["nr"], pr[:, 8, :], -1.0, None, ALU.add))
        dv(lambda e: e.tensor_tensor(A["t0"], A["lr"], A["lr"], ALU.mult))
        dv(lambda e: e.tensor_tensor(A["t1"], A["li"], A["li"], ALU.mult))
        dv(lambda e: e.tensor_tensor(A["t0"], A["t0"], A["t1"], ALU.add))
        dv(lambda e: e.reciprocal(A["t0"], A["t0"]))
        dv(lambda e: e.tensor_tensor(A["t1"], A["nr"], A["lr"], ALU.mult))
        dv(lambda e: e.tensor_tensor(A["t2"], pi[:, 8, :], A["li"], ALU.mult))
        dv(lambda e: e.tensor_tensor(A["t1"], A["t1"], A["t2"], ALU.add))
        dv(lambda e: e.tensor_tensor(A["cr"], A["t1"], A["t0"], ALU.mult))
        dv(lambda e: e.tensor_tensor(A["t1"], pi[:, 8, :], A["lr"], ALU.mult))
        dv(lambda e: e.tensor_tensor(A["t2"], A["nr"], A["li"], ALU.mult))
        dv(lambda e: e.tensor_tensor(A["t1"], A["t1"], A["t2"], ALU.subtract))
        dv(lambda e: e.tensor_tensor(A["ci"], A["t1"], A["t0"], ALU.mult))

        def bc_c(x):
            return x.unsqueeze(2).to_broadcast([128, 64, 16])
        dv(lambda e: e.tensor_tensor(tb, br, bc_c(A["cr"]), ALU.mult))
        dv(lambda e: e.tensor_tensor(tb2, bi, bc_c(A["ci"]), ALU.mult))
        dv(lambda e: e.tensor_tensor(Br, tb, tb2, ALU.subtract))
        dv(lambda e: e.tensor_tensor(tb, bi, bc_c(A["cr"]), ALU.mult))
        dv(lambda e: e.tensor_tensor(tb2, br, bc_c(A["ci"]), ALU.mult))
        dv(lambda e: e.tensor_tensor(Bi, tb, tb2, ALU.add))
        CrT = Cr.rearrange("g c p -> g p c")
        CiT = Ci.rearrange("g c p -> g p c")
        for kind in range(3):
            for ri in range(2):
                for i in range(8):
                    if kind == 0:
                        ki = (7 - i) + 7
                        X, Y = Br, Bi
                    else:
                        ki = (i - 7) + 7 if kind == 1 else (i + 1) + 7
                        X, Y = CrT, CiT
                    prk = bc_c(pr[:, ki, :])
                    pik = bc_c(pi[:, ki, :])
                    dst = big[:, :, i * 16:(i + 1) * 16]
                    if ri == 0:
                        dv(lambda e, X=X, prk=prk: e.tensor_tensor(tb, X, prk, ALU.mult))
                        dv(lambda e, Y=Y, pik=pik: e.tensor_tensor(tb2, Y, pik, ALU.mult))
                        dv(lambda e, dst=dst: e.tensor_tensor(dst, tb, tb2, ALU.subtract))
                    else:
                        dv(lambda e, Y=Y, prk=prk: e.tensor_tensor(tb, Y, prk, ALU.mult))
                        dv(lambda e, X=X, pik=pik: e.tensor_tensor(tb2, X, pik, ALU.mult))
                        dv(lambda e, dst=dst: e.tensor_tensor(dst, tb, tb2, ALU.add))
                        if kind > 0:
                            dv(lambda e, dst=dst: e.tensor_scalar(dst, dst, -1.0, None, ALU.mult))
                S.dma("sp", lambda e, kind=kind, ri=ri: e.dma_start(out=self.efg[:, kind, ri * 64:(ri + 1) * 64, :], in_=big), sem, r=[K], w=[K, "efg"])
        for name, ki in (("A8", 15), ("Am4", 3)):
            for vi, (sa, sb_) in enumerate(((1.0, 1.0), (-1.0, 1.0), (1.0, -1.0))):
                src = pr if vi == 0 else pi
                dv(lambda e, src=src, ki=ki, sa=sa: e.tensor_scalar(nat[:, 0:64], src[:, ki, :], sa, None, ALU.mult))
                dv(lambda e, src=src, ki=ki, sb_=sb_: e.tensor_scalar(nat[:, 64:128], src[:, ki, :], sb_, None, ALU.mult))
                bk, psb, pskey = self.bank()
                S.op("pe", lambda e, psb=psb: e.transpose(psb[:, 0:128], nat, self.ident_f[:]), r=[K, "const"], w=[pskey])
                idx = (0 if name == "A8" else 3) + vi
                S.op("dve", lambda e, psb=psb: e.tensor_copy(A["nat2"], psb[:, 0:128]), r=[pskey, K], w=[K])
                S.dma("sp", lambda e, idx=idx: e.dma_start(out=self.s5A_scr[j, idx], in_=A["nat2"]), sem, r=[K], w=[K, ("s5A", j)])
        for g in range(128):
            bi_ = g % 2
            et = eg[bi_]
            ek = ("s5eg", bi_)
            S.dma("sp", lambda e, g=g, et=et: e.dma_start(out=et, in_=self.efg[g].rearrange("k r n -> r k n")), self.s5sem2[bi_], r=["efg"], w=[ek])
            bk, p1, k1 = self.bank()
            S.op("pe", lambda e, p1=p1, et=et: e.transpose(p1[:, 0:128], et[:, 0, :], self.ident_f[:]), r=[ek, "const"], w=[k1])
            bk, p2, k2 = self.bank()
            S.op("pe", lambda e, p2=p2, et=et: e.matmul(p2[:, 0:128], et[:, 0, :], et[:, 1, :], start=True, stop=True), r=[ek], w=[k2])
            wt = wo[bi_]
            wk = ("s5wo", bi_)
            S.op("dve", lambda e, wt=wt, p1=p1: e.tensor_copy(wt[:, 0, :], p1[:, 0:128]), r=[k1], w=[wk])
            S.op("dve", lambda e, wt=wt, p1=p1: e.tensor_copy(wt[:, 1, 0:64], p1[:, 64:128]), r=[k1], w=[wk])
            S.op("dve", lambda e, wt=wt, p1=p1: e.tensor_copy(wt[:, 1, 64:128], p1[:, 0:64]), r=[k1], w=[wk])
            S.op("dve", lambda e, wt=wt, p2=p2: e.tensor_tensor(wt[:, 2, :], p2[:, 0:128], self.blkmask[:, :], ALU.mult), r=[k2, "const"], w=[wk])
            S.op("dve", lambda e, wt=wt, et=et: e.tensor_copy(wt[:, 3, :], et[:, 2, :]), r=[ek], w=[wk])
            S.dma("sp", lambda e, g=g, wt=wt: e.dma_start(out=self.s5w[j, g], in_=wt), self.s5sem3, r=[wk], w=[("s5w", j)])

    def s5_apply(self, j, li, tile, ncols, ws):
        S = self.S
        R, xn, hid = self.R, self.xn, self.hid
        ncol = NCH + 1 if ws else NCH
        G7 = 7
        V = self.av(0, [128, 128, NCH + 1], F32)
        Vs = self.av(128 * (NCH + 1) * 4, [128, 128, NCH + 1], F32)
        o = 2 * 128 * (NCH + 1) * 4
        wb = [self.av(o + i * 8192, [128, 8, 512], BF16) for i in range(2)]
        o += 16384
        Yg = self.av(0, [128, 128, NCH + 1], BF16)
        F2 = self.av(128 * (NCH + 1) * 2, [128, 16, NT], F32)
        XG = hid.rearrange("p a b -> p (a b)")[:, 0:128 * (NCH + 1)].rearrange("p (g n) -> p g n", g=128)
        Hall = xn.rearrange("p a b -> p (a b)")[:, 0:128 * (NCH + 1)].rearrange("p (g n) -> p g n", g=128)
        allhid = [("hid", c) for c in range(16)]
        allxn = [("xn", c) for c in range(16)]
        Atl = [self.av(o + i * 512, [128, 128], F32) for i in range(6)]
        o += 6 * 512
        A8, Am4 = Atl[0:3], Atl[3:6]
        st = dict(self.s5state[j])
        for nm in ("t1", "t2", "t3", "t4", "h0", "h0s", "hs0", "hs0s", "hsf", "so", "natA"):
            st[nm] = self.av(o, [128, 128], F32)
            o += 512
        self.s5u = self.av(2 * 128 * (NCH + 1) * 4, [128, NT], F32)
        self.s5y = self.av(2 * 128 * (NCH + 1) * 4 + NT * 4, [128, NT], F32)
        self.s5t = self.av(2 * 128 * (NCH + 1) * 4 + 2 * NT * 4, [128, NT], F32)
        for i in range(6):
            S.dma("sp", lambda e, i=i: e.dma_start(out=Atl[i], in_=self.s5A_scr[j, i]), self.s5sem, r=[("s5A", j)], w=["s5A"])
        if ws:
            for nm, first, second in (("h0", "sst_re", "sst_im"), ("h0s", "sst_im", "sst_re")):
                S.dma("sp", lambda e, first=first: e.dma_start(out=st["natA"][:, 0:64], in_=self.ins[first][j]), self.s5sem_n, w=["s5nat"])
                S.dma("sp", lambda e, second=second: e.dma_start(out=st["natA"][:, 64:128], in_=self.ins[second][j]), self.s5sem_n, w=["s5nat"])
                bk, psb, pskey = self.bank()
                S.op("pe", lambda e, psb=psb: e.transpose(psb[:, 0:128], st["natA"], self.ident_f[:]), r=["s5nat", "const"], w=[pskey])
                S.op("dve", lambda e, psb=psb, nm=nm: e.tensor_copy(st[nm], psb[:, 0:128]), r=[pskey], w=["s5h0"])
        for g0 in range(0, 128, G7):
            gs = list(range(g0, min(128, g0 + G7)))
            bk, psb, pskey = self.bank()
            fns = []
            for si, g in enumerate(gs):
                dc, gl = g // 8, g % 8
                for i in range(8):
                    fns.append(lambda e, si=si, dc=dc, gl=gl, i=i, psb=psb: e.matmul(
                        psb[:, si * 65:si * 65 + NCH], self.Zb[gl][:, 112 - 16 * i:240 - 16 * i], xn[:, dc, i:TT:8], start=(i == 0), stop=(i == 7)))
                if ws:
                    for k in range(NS):
                        fns.append(lambda e, si=si, dc=dc, gl=gl, k=k, psb=psb: e.matmul(
                            psb[:, si * 65 + NCH:si * 65 + NCH + 1], self.Zb[gl][:, 112 - 16 * (4 + k):240 - 16 * (4 + k)], xn[:, dc, TT + k:TT + k + 1],
                            start=(k == 0), stop=(k == NS - 1)))
            S.group("pe", fns, r=allxn + ["const", "Zb"], w=[pskey])
            n = len(gs)
            S.op("dve", lambda e, g0=g0, n=n, psb=psb: e.tensor_copy(XG[:, g0:g0 + n, 0:ncol], psb[:, 0:n * 65].rearrange("p (g c) -> p g c", c=65)[:, :, 0:ncol]),
                 r=[pskey], w=allhid)
        def load_wb(gb):
            bi_ = gb % 2
            S.dma("pool", lambda e, gb=gb, bi_=bi_: e.dma_start(out=wb[bi_], in_=self.s5w[j, gb * 8:(gb + 1) * 8].rearrange("g p k n -> p g (k n)")),
                  self.s5wsem[bi_], r=[("s5w", j)], w=[("s5wb", bi_)])
            return wb[bi_], ("s5wb", bi_)
        for gb in range(16):
            wt, wk = load_wb(gb)
            for half in range(2):
                gs = list(range(gb * 8 + half * 4, gb * 8 + half * 4 + 4))
                for which, dst in ((0, V), (1, Vs)):
                    bk, psb, pskey = self.bank()
                    fns = [lambda e, si=si, g=g, psb=psb, which=which, wt=wt: e.matmul(
                        psb[:, si * 65:si * 65 + ncol], wt[:, g % 8, which * 128:(which + 1) * 128], XG[:, g, 0:ncol], start=True, stop=True) for si, g in enumerate(gs)]
                    S.group("pe", fns, r=allhid + [wk], w=[pskey])
                    S.op("dve", lambda e, g0=gs[0], psb=psb, dst=dst: e.tensor_copy(dst[:, g0:g0 + 4, 0:ncol], psb[:, 0:4 * 65].rearrange("p (g c) -> p g c", c=65)[:, :, 0:ncol]),
                         r=[pskey], w=["s5V"])
        H = [h_[:, :] for h_ in st["H"]]
        Hs = [h_[:, :] for h_ in st["Hs"]]
        cur = st["cur"]
        for jj in range(NCH):
            nxt = 1 - cur
            S.op("dve", lambda e, jj=jj, cur=cur: e.tensor_copy(Hall[:, :, jj], H[cur][:, :]), r=[("H", cur)], w=allxn)
            S.op("dve", lambda e, cur=cur: e.tensor_tensor(st["t1"], A8[0], H[cur], ALU.mult), r=[("H", cur), "s5A"], w=["s5t1"])
            S.op("dve", lambda e, cur=cur: e.tensor_tensor(st["t2"], A8[1], Hs[cur], ALU.mult), r=[("Hs", cur), "s5A"], w=["s5t2"])
            S.op("dve", lambda e: e.tensor_tensor(st["t1"], st["t1"], st["t2"], ALU.add), r=["s5t1", "s5t2"], w=["s5t1"])
            S.op("dve", lambda e, jj=jj, nxt=nxt: e.tensor_tensor(H[nxt], st["t1"], V[:, :, jj], ALU.add), r=["s5t1", "s5V"], w=[("H", nxt)])
            S.op("pool", lambda e, cur=cur: e.tensor_tensor(st["t3"], A8[0], Hs[cur], ALU.mult), r=[("Hs", cur), "s5A"], w=["s5t3"])
            S.op("pool", lambda e, cur=cur: e.tensor_tensor(st["t4"], A8[2], H[cur], ALU.mult), r=[("H", cur), "s5A"], w=["s5t4"])
            S.op("pool", lambda e: e.tensor_tensor(st["t3"], st["t3"], st["t4"], ALU.add), r=["s5t3", "s5t4"], w=["s5t3"])
            S.op("pool", lambda e, jj=jj, nxt=nxt: e.tensor_tensor(Hs[nxt], st["t3"], Vs[:, :, jj], ALU.add), r=["s5t3", "s5V"], w=[("Hs", nxt)])
            cur = nxt
        self.s5state[j]["cur"] = cur
        if ws:
            h0, h0s = st["h0"], st["h0s"]
            hs0, hs0s = st["hs0"], st["hs0s"]
            dv = lambda fn, r, w: S.op("dve", fn, r=r, w=w)
            dv(lambda e: e.tensor_tensor(st["t1"], Am4[0], h0, ALU.mult), ["s5h0", "s5A"], ["s5t1"])
            dv(lambda e: e.tensor_tensor(st["t2"], Am4[1], h0s, ALU.mult), ["s5h0", "s5A"], ["s5t2"])
            dv(lambda e: e.tensor_tensor(hs0, st["t1"], st["t2"], ALU.add), ["s5t1", "s5t2"], ["s5hs0"])
            dv(lambda e: e.tensor_tensor(st["t1"], Am4[0], h0s, ALU.mult), ["s5h0", "s5A"], ["s5t1"])
            dv(lambda e: e.tensor_tensor(st["t2"], Am4[2], h0, ALU.mult), ["s5h0", "s5A"], ["s5t2"])
            dv(lambda e: e.tensor_tensor(hs0s, st["t1"], st["t2"], ALU.add), ["s5t1", "s5t2"], ["s5hs0"])
            dv(lambda e: e.tensor_copy(Hall[:, :, NCH], hs0), ["s5hs0"], allxn)
            dv(lambda e: e.tensor_tensor(st["t1"], A8[0], hs0, ALU.mult), ["s5hs0", "s5A"], ["s5t1"])
            dv(lambda e: e.tensor_tensor(st["t2"], A8[1], hs0s, ALU.mult), ["s5hs0", "s5A"], ["s5t2"])
            dv(lambda e: e.tensor_tensor(st["t1"], st["t1"], st["t2"], ALU.add), ["s5t1", "s5t2"], ["s5t1"])
            dv(lambda e: e.tensor_tensor(st["hsf"], st["t1"], V[:, :, NCH], ALU.add), ["s5t1", "s5V"], ["s5hsf"])
            for src, skey, o_re, o_im in ((H[cur], ("H", cur), "ssm_re_p", "ssm_im_p"), (st["hsf"], "s5hsf", "ssm_re_s", "ssm_im_s")):
                bk, psb, pskey = self.bank()
                S.op("pe", lambda e, psb=psb, src=src: e.transpose(psb[:, 0:128], src, self.ident_f[:]), r=[skey, "const"], w=[pskey])
                so = st["so"]
                S.op("dve", lambda e, psb=psb, so=so: e.tensor_copy(so[:, :], psb[:, 0:128]), r=[pskey], w=["s5so"])
                S.dma("sp", lambda e, so=so, o_re=o_re: e.dma_start(out=self.outs[o_re][j], in_=so[:, 0:64]), self.osem, r=["s5so"])
                S.dma("sp", lambda e, so=so, o_im=o_im: e.dma_start(out=self.outs[o_im][j], in_=so[:, 64:128]), self.osem, r=["s5so"])
        for gb in range(16):
            wt, wk = load_wb(gb)
            for half in range(2):
                gs = list(range(gb * 8 + half * 4, gb * 8 + half * 4 + 4))
                bk, psb, pskey = self.bank()
                fns = []
                for si, g in enumerate(gs):
                    fns.append(lambda e, si=si, g=g, psb=psb, wt=wt: e.matmul(psb[:, si * 65:si * 65 + ncol], wt[:, g % 8, 256:384], XG[:, g, 0:ncol], start=True, stop=False))
                    fns.append(lambda e, si=si, g=g, psb=psb, wt=wt: e.matmul(psb[:, si * 65:si * 65 + ncol], wt[:, g % 8, 384:512], Hall[:, g, 0:ncol], start=False, stop=True))
                S.group("pe", fns, r=allhid + allxn + [wk], w=[pskey])
                S.op("dve", lambda e, g0=gs[0], psb=psb: e.tensor_copy(Yg[:, g0:g0 + 4, 0:ncol], psb[:, 0:4 * 65].rearrange("p (g c) -> p g c", c=65)[:, :, 0:ncol]),
                     r=[pskey, "s5V"], w=["s5Y"])
        gcol = (li * 2) * 16
        for dc in range(16):
            bk, psb, pskey = self.bank()
            fns = []
            for i in range(8):
                for gl in range(8):
                    fns.append(lambda e, i=i, gl=gl, dc=dc, psb=psb: e.matmul(psb[:, i * NCH:(i + 1) * NCH], self.Zb[i][:, 112 - 16 * gl:240 - 16 * gl], Yg[:, dc * 8 + gl, 0:NCH],
                                                                         start=(gl == 0), stop=(gl == 7)))
            S.group("pe", fns, r=["s5Y", "Zb"], w=[pskey])
            if ws:
                bk2, psb2, pskey2 = self.bank()
                fns = []
                for k in range(NS):
                    for gl in range(8):
                        fns.append(lambda e, k=k, gl=gl, dc=dc, psb2=psb2: e.matmul(psb2[:, k:k + 1], self.Zb[4 + k][:, 112 - 16 * gl:240 - 16 * gl], Yg[:, dc * 8 + gl, NCH:NCH + 1],
                                                                               start=(gl == 0), stop=(gl == 7)))
                S.group("pe", fns, r=["s5Y", "Zb"], w=[pskey2])
            u = self.s5u
            y = self.s5y
            t = self.s5t
            S.op("dve", lambda e, dc=dc: e.scalar_tensor_tensor(u[:, 0:ncols], R[:, dc, 0:ncols], self.gv[:, gcol + dc:gcol + dc + 1], self.rstd[:, 0:ncols], ALU.mult, ALU.mult),
                 r=[("R", dc), "rstd", "gv", ("s5wb", 0), ("s5wb", 1)], w=["s5u", ("s5wb", 0)])
            S.op("dve", lambda e, dc=dc, psb=psb: e.scalar_tensor_tensor(y[:, 0:TT].rearrange("p (j i) -> p j i", i=8), u[:, 0:TT].rearrange("p (j i) -> p j i", i=8),
                                                                    self.dvv[:, j * 16 + dc:j * 16 + dc + 1], psb[:, 0:TT].rearrange("p (i j) -> p j i", i=8), ALU.mult, ALU.add),
                 r=["s5u", pskey, "gv"], w=["s5y"])
            if ws:
                S.op("dve", lambda e, dc=dc, psb2=psb2: e.scalar_tensor_tensor(y[:, TT:TT + NS], u[:, TT:TT + NS], self.dvv[:, j * 16 + dc:j * 16 + dc + 1], psb2[:, 0:NS], ALU.mult, ALU.add),
                     r=["s5u", pskey2, "gv"], w=["s5y"])
            S.op("pool", lambda e: e.tensor_tensor(t[:, 0:ncols], y[:, 0:ncols], y[:, 0:ncols], ALU.mult), r=["s5y"], w=["s5t"])
            S.op("dve", lambda e: e.tensor_scalar(t[:, 0:ncols], t[:, 0:ncols], 0.044715, 1.0, ALU.mult, ALU.add), r=["s5t"], w=["s5t"])
            S.op("dve", lambda e: e.tensor_tensor(t[:, 0:ncols], t[:, 0:ncols], y[:, 0:ncols], ALU.mult), r=["s5t", "s5y"], w=["s5t"])
            S.op("act", lambda e: e.activation(t[:, 0:ncols], t[:, 0:ncols], AF.Sigmoid, scale=1.5957691216057308), r=["s5t"], w=["s5t"])
            S.op("dve", lambda e, dc=dc: e.tensor_tensor(hid[:, dc, 0:ncols], y[:, 0:ncols], t[:, 0:ncols], ALU.mult), r=["s5t", "s5y", "s5Y"], w=[("hid", dc)])
        wg = self.ins["ssm_w_glu"][j]

        def ev_g(m, n0, n, ps, pskey):
            S.op("act", lambda e: e.activation(F2[:, m, n0:n0 + n], ps, AF.Sigmoid), r=[pskey, "s5Y"], w=[("F2", m)])
        self.linear_fm(wg, 0, 2048, 16, hid, "hid", ncols, ev_g)

        def ev_a(m, n0, n, ps, pskey):
            tt = self.tmpf[self.tmpf_i % 2]
            tk = ("tmpf", self.tmpf_i % 2)
            self.tmpf_i += 1
            S.op("dve", lambda e: e.tensor_tensor(tt[:, 0:n], ps, F2[:, m, n0:n0 + n], ALU.mult), r=[pskey, ("F2", m)], w=[tk])
            S.op("dve", lambda e: e.tensor_tensor(R[:, m, n0:n0 + n], R[:, m, n0:n0 + n], tt[:, 0:n], ALU.add), r=[tk, ("R", m)], w=[("R", m)])
        self.linear_fm(wg, 0, 0, 16, hid, "hid", ncols, ev_a)

    def obank(self, i):
        return i, self.obanks[i], ("ops", i)

    def av(self, off, shape, dt):
        n = 1
        for d in shape[1:]:
            n *= d
        if dt == BF16:
            ap = self.arena[:, off // 2: off // 2 + n]
        else:
            ap = self.arena[:, off // 2: off // 2 + 2 * n].bitcast(dt)
        if len(shape) == 3:
            ap = ap.rearrange("p (a b) -> p a b", a=shape[1])
        self._av_end = max(getattr(self, "_av_end", 0), off + n * (2 if dt == BF16 else 4))
        assert self._av_end <= self.ARENA_BYTES, self._av_end
        return ap

    def barrier(self):
        S = self.S
        snap = [(S.sem[e], S.cnt[e]) for e in S.ENG if S.cnt[e]]
        dsn = [(d.h, d.count) for d in self.all_dsems if d.count]
        for eng in S.ENG:
            waits = []
            for s_, v in snap + dsn:
                if S.waited[eng].get(s_.name, 0) < v:
                    S.waited[eng][s_.name] = v
                    waits.append((s_, v))

            def emit(e, waits=waits):
                for s_, v in waits:
                    e.wait_ge(s_, v)
            S.ops[eng].append(emit)

    def build(self):
        nc = self.nc
        es = self.es
        S = self.S = Sched(nc, es)
        self.all_dsems = []
        _ds = S.dsem

        def dsem(name=None):
            d = _ds(name)
            self.all_dsems.append(d)
            return d
        S.dsem = dsem
        self.din("xp", [SEQ, D])
        self.din("xs", [NS, D])
        self.din("norm_mix", [DEPTH, D])
        self.din("norm_mlp", [DEPTH, D])
        self.din("mlp_w_up", [DEPTH, D, FF])
        self.din("mlp_w_down", [DEPTH, FF, D])
        self.din("conv_w_in", [1, D, 3 * D])
        self.din("conv_w_dw", [1, 3, D])
        self.din("conv_w_out", [1, D, D])
        self.din("sconv_in", [2, D])
        self.din("attn_w_qkv", [1, D, 3 * D])
        self.din("attn_q_norm", [1, 128])
        self.din("attn_k_norm", [1, 128])
        self.din("attn_sb_bias", [1, 16])
        self.din("attn_w_o", [1, D, D])
        import os
        self.sample_on = self.stage >= 4 and not os.environ.get("NOSAMPLE")
        if self.sample_on:
            self.din("cache_k", [1280, 128, 16, 128])
            self.din("cache_v", [1280, 128, 16, 128])
        self.din("pt", [1, NPAGES], I32)
        for nm, shp in (("ssm_lambda_re", [2, 128, 64]), ("ssm_lambda_im", [2, 128, 64]), ("ssm_log_dt", [2, 128]),
                        ("ssm_b_re", [2, 128, 64, 16]), ("ssm_b_im", [2, 128, 64, 16]), ("ssm_c_re", [2, 128, 16, 64]),
                        ("ssm_c_im", [2, 128, 16, 64]), ("ssm_d", [2, D]), ("ssm_w_glu", [2, D, 2 * D]),
                        ("sst_re", [2, 128, 64]), ("sst_im", [2, 128, 64]), ("c_blkmask", [128, 128])):
            self.din(nm, shp)
        self.din("c_ident", [128, 128])
        self.din("c_tri", [128, 128])
        self.din("c_masks", [128, TT // 128, TT])
        self.din("c_smask", [128, 16 * NS])
        self.dout("yp", [SEQ, D])
        self.dout("ys", [NS, D])
        self.dout("conv_p", [2, D])
        self.dout("conv_s", [2, D])
        for nm in ("ssm_re_p", "ssm_im_p", "ssm_re_s", "ssm_im_s"):
            self.dout(nm, [2, 128, 64])
        import os
        if os.environ.get("DEBUG_SA"):
            self.dout("dbg_pgk", [128, D])
            self.dout("dbg_pgv", [128, D], BF16)
            self.dout("dbg_carry", [128, 16 * NS])
            self.dout("dbg_o", [128, 16 * NS])
            self.dout("dbg_q", [128, 16, NS], BF16)
            self.dout("dbg_ktp", [128, 16, 128], BF16)
        self.dout("kp", [SEQ, D])
        self.dout("vp", [SEQ, D])
        self.dout("ks", [NS, D])
        self.dout("vs", [NS, D])
        self.kT_scr = nc.dram_tensor("kT_scr", [16, 128, SEQ], BF16).ap()
        self.v_scr = nc.dram_tensor("v_scr", [SEQ, D], BF16).ap()
        self.efg = nc.dram_tensor("efg_scr", [128, 3, 128, 128], F32).ap()
        self.s5w = nc.dram_tensor("s5w_scr", [2, 128, 128, 4, 128], BF16).ap()
        self.s5A_scr = nc.dram_tensor("s5A_scr", [2, 6, 128, 128], F32).ap()
        self.R = self.sb("R", [128, 16, NT], F32)
        self.xn = self.sb("xn", [128, 16, NT + 4], BF16)
        self.hid = self.sb("hid", [128, 16, NT + 4], BF16)
        self.wbuf = [self.sb("w%d" % i, [128, 16, SW], BF16) for i in range(2)]
        self.wsem = [S.dsem("ws%d" % i) for i in range(2)]
        self.w_i = 0
        self.rstd = self.sb("rstd", [128, NT], F32)
        self.sq = [self.sb("sq0", [128, NT], F32)]
        self.tmpb = [self.sb("tmpb%d" % i, [128, 512], BF16) for i in range(2)]
        self.tmp_i = 0
        self.tmpf = [self.sb("tmpf%d" % i, [128, 512], F32) for i in range(2)]
        self.tmpf_i = 0
        self.stage_f = [self.sb("stage0", [128, D], F32)] * 2
        self.ssem = [S.dsem("ss%d" % i) for i in range(2)]
        self.osem = S.dsem("osem")
        self.scrsem = S.dsem("scrsem")
        self.pvsem = [S.dsem("pv%d" % i) for i in range(2)]
        self.pgsem = [S.dsem("pg%d" % i) for i in range(2)]
        self.pgsemV = [S.dsem("pgv%d" % i) for i in range(2)]
        self.osem_n = [S.dsem("on%d" % i) for i in range(2)]
        self.pvsem_v = S.dsem("pvv")
        self.s5sem_n = S.dsem("s5n")
        self.gv = self.sb("gv", [128, 16 * 8], F32)
        self.dwv = self.sb("dwv", [128, 48], F32)
        self.ccarry = self.sb("ccarry", [128, 16, 2], F32)
        self.sconv = self.sb("sconv", [128, 16, 2], F32)
        self.ident_f = self.sb("ident_f", [128, 128], F32)
        self.ones_f = self.sb("ones_f", [128, 128], F32)
        self.ones_b = self.sb("ones_b", [128, 128], BF16)
        self.tri_f = self.sb("tri_f", [128, 128], F32)
        self.tri_b = self.sb("tri_b", [128, 128], BF16)
        self.eps_t = self.sb("eps_t", [128, 1], F32)
        self.one_t = self.sb("one_t", [128, 1], F32)
        self.qgain = self.sb("qgain", [128, 128], F32)
        self.kgain = self.sb("kgain", [128, 128], F32)
        self.abias = self.sb("abias", [128, 16], F32)
        self.sbiasrow = self.sb("sbiasrow", [128, 16 * NS], F32)
        self.ptsb = self.sb("ptsb", [128, NPAGES], I32)
        self.pidx = self.sb("pidx", [128, NPAGES], I32)
        self.iota_p = self.sb("iota_p", [128, 1], I32)
        self.s5sem = S.dsem("s5sem")
        self.s5sem2 = [S.dsem("s5e%d" % i) for i in range(2)]
        self.s5sem3 = S.dsem("s5sem3")
        self.s5wsem = [S.dsem("s5w%d" % i) for i in range(2)]
        self.Zb = [self.sb("Zb%d" % a, [128, 240], BF16) for a in range(8)]
        self.blkmask = self.sb("blkmask", [128, 128], F32)
        self.hpi_t = self.sb("hpi_t", [128, 1], F32)
        self.dvv = self.sb("dvv", [128, 32], F32)
        self.s5state = []
        for jj in range(2):
            self.s5state.append({"H": [self.sb("H%d_%d" % (jj, i), [128, 128], F32) for i in range(2)],
                                 "Hs": [self.sb("Hs%d_%d" % (jj, i), [128, 128], F32) for i in range(2)], "cur": 0})
        self.ARENA_BYTES = 93 * 1024
        self.arena = self.sb("arena", [128, self.ARENA_BYTES // 2], BF16)
        allb = [self.ps("bank%d" % i, [128, 512], F32) for i in range(8)]
        self.banks = allb[0:6]
        self.obanks = allb[6:8]
        self.bank_i = 0
        self.ab_i = 0
        self.ab_c = 0
        self.nq_i = 0
        self.wbig = [self.av(i * 32768, [128, 16, 1024], BF16) for i in range(2)]
        self.wbigsem = [S.dsem("wbg%d" % i) for i in range(2)]
        self.wb_i = 0
        self.F1 = self.av(0, [128, 16, TT + 8], F32)
        o = 0
        self.qT = self.av(o, [128, 16, NT], BF16); o += 16 * NT * 2
        self.kT = self.av(o, [128, 16, NT], BF16); o += 16 * NT * 2
        self.Vt = self.av(o, [128, TT // 128 + 1, D], BF16); o += (TT // 128 + 1) * D * 2
        npv = SEQ - TT
        self.kprev = [self.av(o, [128, npv], BF16)] * 2; o += npv * 2
        self.vprev = [self.av(o, [128, npv // 128, 128], BF16)] * 2; o += npv * 2
        self.abLb = [self.av(o + i * 1024, [128, 512], BF16) for i in range(2)]; o += 2048
        self.abA = [self.av(o + i * 1024, [128, 512], BF16) for i in range(2)]; o += 2048
        self.masks = self.av(o, [128, TT // 128, TT], BF16); o += (TT // 128) * TT * 2
        self.smask = self.av(o, [128, 16 * NS], BF16); o += 16 * NS * 2
        self.pgV = [self.av(o + i * 4096, [128, D], BF16) for i in range(2)]; o += 8192
        self.pgVb = self.pgV
        self.KTp = [self.av(o, [128, 16, 128], BF16)] * 2; o += 4096
        self.abE = [self.av(o, [128, 512], F32)] * 2; o += 2048
        self.abL = [self.av(o, [128, 512], F32)] * 2; o += 2048
        self.abT = [self.av(o, [128, 512], F32)] * 2; o += 2048
        self.carry = [self.av(o + i * 2048, [128, 512], F32) for i in range(2)]; o += 4096
        self.scarry = self.av(o, [128, 16 * NS], F32); o += 16 * NS * 4
        self.soacc = self.av(o, [128, 16 * NS], F32); o += 16 * NS * 4
        self.nsq = [self.av(o + i * 1024, [128, SW], F32) for i in range(2)]; o += 2048
        self.nrm = [self.av(o + i * 1024, [128, SW], F32) for i in range(2)]; o += 2048
        self.nss = [self.av(o + i * 8, [128, 2], F32) for i in range(2)]; o += 16
        self.pgK = self.stage_f

        self.csem = None
        S.dma("sp", lambda e: e.dma_start(out=self.ident_f[:], in_=self.ins["c_ident"]), S.dsem(), w=["const0"])
        S.dma("sp", lambda e: e.dma_start(out=self.tri_f[:], in_=self.ins["c_tri"]), S.dsem(), w=["const2"])
        S.op("dve", lambda e: e.memset(self.ones_f[:], 1.0), w=["const1"])
        S.op("dve", lambda e: e.memset(self.ones_b[:], 1.0), w=["const3"])
        S.op("dve", lambda e: e.memset(self.one_t[:], 1.0), w=["const4"])
        S.op("dve", lambda e: e.tensor_copy(self.tri_b[:], self.tri_f[:]), r=["const2"], w=["const5"])
        S.op("dve", lambda e: e.memset(self.eps_t[:], EPS), r=["const0", "const1", "const3", "const4", "const5"], w=["const"])
        S.op("dve", lambda e: e.tensor_copy(self.eps_t[:], self.eps_t[:]), r=["const"], w=["const"])
        for i in range(DEPTH):
            for k, nm in enumerate(("norm_mix", "norm_mlp")):
                col = (i * 2 + k) * 16
                src = self.ins[nm][i].rearrange("(c p) -> p c", p=128)
                S.dma("sp", lambda e, col=col, src=src: e.dma_start(out=self.gv[:, col:col + 16], in_=src, allow_slow_non_contiguous=True),
                      S.dsem(), w=["gv"])
        for k in range(3):
            src = self.ins["conv_w_dw"][0, k].rearrange("(c p) -> p c", p=128)
            S.dma("sp", lambda e, k=k, src=src: e.dma_start(out=self.dwv[:, k * 16:(k + 1) * 16], in_=src, allow_slow_non_contiguous=True), S.dsem(), w=["gv"])
        S.dma("sp", lambda e: e.dma_start(out=self.qgain[:], in_=self.ins["attn_q_norm"][0].partition_broadcast(128)), S.dsem(), w=["gains"])
        S.dma("sp", lambda e: e.dma_start(out=self.kgain[:], in_=self.ins["attn_k_norm"][0].partition_broadcast(128)), S.dsem(), w=["gains"])
        S.dma("sp", lambda e: e.dma_start(out=self.abias[:], in_=self.ins["attn_sb_bias"][0].partition_broadcast(128)), S.dsem(), w=["abias0"])
        for h in range(16):
            S.op("dve", lambda e, h=h: e.tensor_scalar(self.sbiasrow[:, h * NS:(h + 1) * NS], self.ones_f[:, 0:NS], self.abias[:, h:h + 1], None, ALU.mult),
                 r=["abias0", "const"], w=["abias"])
        S.dma("pool", lambda e: e.dma_start(out=self.ptsb[:], in_=self.ins["pt"][0].partition_broadcast(128)), S.dsem(), w=["pt"])
        S.op("pool", lambda e: e.iota(self.iota_p[:], pattern=[[0, 1]], base=0, channel_multiplier=1), w=["iota"])
        S.op("pool", lambda e: e.tensor_scalar(self.pidx[:], self.ptsb[:], 128, self.iota_p[:, 0:1], ALU.mult, ALU.add), r=["pt", "iota"], w=["pidx"])
        S.op("pool", lambda e: e.memset(self.ccarry[:, :, :], 0.0), w=["ccarry"])
        S.dma("sp", lambda e: e.dma_start(out=self.blkmask[:], in_=self.ins["c_blkmask"]), S.dsem(), w=["const6"])
        S.op("dve", lambda e: e.memset(self.hpi_t[:], float(np.pi / 2)), r=["const6"], w=["const7"])
        for a in range(8):
            S.op("dve", lambda e, a=a: e.memset(self.Zb[a][:], 0.0), w=["Zb"])
            S.op("dve", lambda e, a=a: e.tensor_copy(self.Zb[a][:, 112:128], self.ident_f[:, a * 16:(a + 1) * 16]), r=["const", "Zb"], w=["Zb"])
        for jj in range(2):
            src = self.ins["ssm_d"][jj].rearrange("(c p) -> p c", p=128)
            S.dma("sp", lambda e, jj=jj, src=src: e.dma_start(out=self.dvv[:, jj * 16:(jj + 1) * 16], in_=src, allow_slow_non_contiguous=True), S.dsem(), w=["gv"])
            for i in range(2):
                S.op("dve", lambda e, jj=jj, i=i: e.memset(self.s5state[jj]["H"][i][:], 0.0), w=[("H", i)])
                S.op("pool", lambda e, jj=jj, i=i: e.memset(self.s5state[jj]["Hs"][i][:], 0.0), w=[("Hs", i)])
        self.barrier()
        if self.stage >= 5:
            for jj in range(2):
                self.barrier()
                self.s5_prep(jj)
            self.barrier()
        self.load_cols(self.ins["sconv_in"], 2, lambda: self.sconv[:, :, :], ["sconv"])

        for tile in range(NTILES):
            ws = (tile == NTILES - 1)
            ncols = NT if ws else TT
            self.load_tile(tile, ws)
            for li in range(DEPTH):
                kind = li % 3
                if kind == 0 and self.stage >= 5:
                    self.rmsnorm((li * 2) * 16, ncols)
                    self.barrier()
                    self.s5_apply(li // 3, li, tile, ncols, ws)
                    self.barrier()
                if (kind == 1 and self.stage >= 2) or (kind == 2 and self.stage >= 3):
                    self.rmsnorm((li * 2) * 16, ncols)
                    self.barrier()
                    if kind == 1:
                        self.conv_mixer(tile, ncols, ws)
                    else:
                        if tile == 0:
                            S.dma("sp", lambda e: e.dma_start(out=self.masks[:, :, :], in_=self.ins["c_masks"]), csem, w=["masks"]) if False else None
                        self.attn_consts()
                        self.attention(tile, ncols, ws)
                    self.barrier()
                self.rmsnorm((li * 2 + 1) * 16, ncols)
                self.barrier()
                self.mlp(li, ncols)
                self.barrier()
                if self.stage == 0:
                    break
            self.store_tile(tile, ws)
        S.final_wait("sp", self.all_dsems)
        S.flush()
        return nc

    def attn_consts(self):
        S = self.S
        st = self.stage_f[0]
        for kb in range(TT // 128):
            S.dma("sp", lambda e, kb=kb: e.dma_start(out=st[:, 0:TT], in_=self.ins["c_masks"][:, kb, :]), self.ssem[0], w=[("stage", 0)])
            S.op("dve", lambda e, kb=kb: e.tensor_copy(self.masks[:, kb, :], st[:, 0:TT]), r=[("stage", 0)], w=["masks"])
        S.dma("sp", lambda e: e.dma_start(out=st[:, 0:16 * NS], in_=self.ins["c_smask"]), self.ssem[0], w=[("stage", 0)])
        S.op("dve", lambda e: e.tensor_copy(self.smask[:, :], st[:, 0:16 * NS]), r=[("stage", 0)], w=["masks"])
        S.op("pool", lambda e: e.memset(self.Vt[:, TT // 128, :], 0.0), w=[("Vt", TT // 128)])


_CONST = {}


def _consts():
    if not _CONST:
        _CONST["c_ident"] = np.eye(128, dtype=np.float32)
        jj = np.arange(128)
        _CONST["c_tri"] = (jj[:, None] >= jj[None, :]).astype(np.float32)
        t = np.arange(TT)
        m = np.zeros((128, TT // 128, TT), np.float32)
        for kb in range(TT // 128):
            m[:, kb, :] = ((kb * 128 + jj)[:, None] < t[None, :])
        _CONST["c_masks"] = m
        sm = np.zeros((128, 16 * NS), np.float32)
        for h in range(16):
            for q in range(NS):
                sm[:q, h * NS + q] = 1.0
        _CONST["c_smask"] = sm
        ii = jj // 16
        _CONST["c_blkmask"] = (ii[None, :] >= ii[:, None]).astype(np.float32)

    return _CONST


_SHARED = ("ssm_lambda_re", "ssm_lambda_im", "ssm_log_dt", "ssm_b_re", "ssm_b_im", "ssm_c_re", "ssm_c_im", "ssm_d", "ssm_w_glu",
           "norm_mix", "norm_mlp", "mlp_w_up", "mlp_w_down", "conv_w_in", "conv_w_dw", "conv_w_out",
           "attn_w_qkv", "attn_q_norm", "attn_k_norm", "attn_sb_bias", "attn_w_o")


def make_in_maps(inputs, cores, with_cache=True):
    cst = _consts()
    ck = np.ascontiguousarray(inputs["cache_k"][0]) if with_cache else None
    cv = np.ascontiguousarray(inputs["cache_v"][0]) if with_cache else None
    in_maps = []
    for c in cores:
        m = {
            "xp": np.ascontiguousarray(inputs["x_prompt"][c % 4]),
            "xs": np.ascontiguousarray(inputs["x_sample"][c]),
            "sconv_in": np.ascontiguousarray(inputs["state_conv"][0, c]),
            "pt": np.ascontiguousarray(inputs["page_table"][c:c + 1]).astype(np.int32),
            "sst_re": np.ascontiguousarray(inputs["state_ssm_re"][:, c]),
            "sst_im": np.ascontiguousarray(inputs["state_ssm_im"][:, c]),
        }
        if with_cache:
            m["cache_k"] = ck
            m["cache_v"] = cv
        for k in _SHARED:
            m[k] = inputs[k]
        m.update(cst)
        in_maps.append(m)
    return in_maps


def kernel(**inputs):
    inputs = {k: np.asarray(v) for k, v in inputs.items()}
    b = Builder()
    nc = b.build()
    in_maps = make_in_maps(inputs, list(range(8)))
    res = run_bass_kernel_spmd(nc, in_maps, core_ids=list(range(8)))
    rs = res.results
    f = np.float32
    y_p = np.stack([rs[c]["yp"] for c in range(4)]).astype(f)
    y_s = np.stack([rs[c]["ys"] for c in range(8)]).astype(f)
    ssm_re_p = np.stack([rs[c]["ssm_re_p"] for c in range(4)], axis=1).astype(f)
    ssm_im_p = np.stack([rs[c]["ssm_im_p"] for c in range(4)], axis=1).astype(f)
    ssm_re_s = np.stack([rs[c]["ssm_re_s"] for c in range(8)], axis=1).astype(f)
    ssm_im_s = np.stack([rs[c]["ssm_im_s"] for c in range(8)], axis=1).astype(f)
    conv_p = np.stack([rs[c]["conv_p"] for c in range(4)])[None].astype(f)
    conv_s = np.stack([rs[c]["conv_s"] for c in range(8)])[None].astype(f)
    k_p = np.stack([rs[c]["kp"] for c in range(4)]).reshape(1, 4, SEQ, NH, 128).astype(f)
    v_p = np.stack([rs[c]["vp"] for c in range(4)]).reshape(1, 4, SEQ, NH, 128).astype(f)
    k_s = np.stack([rs[c]["ks"] for c in range(8)]).reshape(1, 8, NS, NH, 128).astype(f)
    v_s = np.stack([rs[c]["vs"] for c in range(8)]).reshape(1, 8, NS, NH, 128).astype(f)
    return (y_p, y_s, ssm_re_p, ssm_im_p, ssm_re_s, ssm_im_s, conv_p, conv_s, k_p, v_p, k_s, v_s)
```

```python
import numpy as np
import ml_dtypes
from contextlib import ExitStack
import concourse.bass as bass
import concourse.mybir as mybir
from concourse.bass_utils import run_bass_kernel_spmd

F32 = mybir.dt.float32
BF16 = mybir.dt.bfloat16
I32 = mybir.dt.int32
AF = mybir.ActivationFunctionType
ALU = mybir.AluOpType
AX = mybir.AxisListType

D = 2048
DC = 16
SEQ = 2048
TT = 512
NTILES = SEQ // TT
NS = 4
NT = TT + NS
FF = 8192
EPS = 1e-6
DEPTH = 4
NH = 16
PAST = 16384
NPAGES = 128
LCH = 8
NCH = TT // LCH
SW = 256


class DSem:
    def __init__(self, h):
        self.h = h
        self.count = 0


class Sched:
    ENG = ("pe", "act", "dve", "pool", "sp")

    def __init__(self, nc, es):
        self.nc = nc
        self.es = es
        self.ops = {e: [] for e in self.ENG}
        self.sem = {e: es.enter_context(nc.semaphore("c_" + e)) for e in self.ENG}
        self.cnt = {e: 0 for e in self.ENG}
        self.waited = {e: {} for e in self.ENG}
        self.lastw = {}
        self.readers = {}
        self.nsem = 0

    def dsem(self, name=None):
        self.nsem += 1
        return DSem(self.es.enter_context(self.nc.semaphore(name or ("d%d" % self.nsem))))

    def _deps(self, eng, r, w):
        need = {}

        def add(sv):
            if sv is None:
                return
            s, v = sv
            k = s.name
            if k not in need or need[k][1] < v:
                need[k] = (s, v)
        for t in r:
            add(self.lastw.get(t))
        for t in w:
            add(self.lastw.get(t))
            for sv in self.readers.get(t, ()):
                add(sv)
        out = []
        wd = self.waited[eng]
        for k, (s, v) in need.items():
            if wd.get(k, 0) < v:
                wd[k] = v
                out.append((s, v))
        return out

    def _mark(self, r, w, sv):
        for t in w:
            self.lastw[t] = sv
            self.readers[t] = []
        for t in r:
            self.readers.setdefault(t, []).append(sv)
            if len(self.readers[t]) > 12:
                best = {}
                for s, v in self.readers[t]:
                    if s.name not in best or best[s.name][1] < v:
                        best[s.name] = (s, v)
                self.readers[t] = list(best.values())

    def op(self, eng, fn, r=(), w=()):
        waits = self._deps(eng, r, w)
        self.cnt[eng] += 1
        sem = self.sem[eng]
        sv = (sem, self.cnt[eng])

        def emit(e, fn=fn, waits=waits, sem=sem):
            for s, v in waits:
                e.wait_ge(s, v)
            fn(e).then_inc(sem, 1)
        self.ops[eng].append(emit)
        self._mark(r, w, sv)

    def group(self, eng, fns, r=(), w=()):
        waits = self._deps(eng, r, w)
        self.cnt[eng] += 1
        sem = self.sem[eng]
        sv = (sem, self.cnt[eng])

        def emit(e, fns=fns, waits=waits, sem=sem):
            for s, v in waits:
                e.wait_ge(s, v)
            for f in fns[:-1]:
                f(e)
            fns[-1](e).then_inc(sem, 1)
        self.ops[eng].append(emit)
        self._mark(r, w, sv)

    def dma(self, q, fn, ds, r=(), w=()):
        waits = self._deps(q, r, w)
        ds.count += 16
        sv = (ds.h, ds.count)

        def emit(e, fn=fn, waits=waits, h=ds.h):
            for s, v in waits:
                e.wait_ge(s, v)
            fn(e).then_inc(h, 16)
        self.ops[q].append(emit)
        self._mark(r, w, sv)

    def final_wait(self, eng, dsems):
        def emit(e, dsems=dsems):
            for d in dsems:
                if d.count:
                    e.wait_ge(d.h, d.count)
        self.ops[eng].append(emit)

    def flush(self):
        with self.nc.Block() as block:
            ops = self.ops

            @block.tensor
            def _(e):
                for f in ops["pe"]:
                    f(e)

            @block.scalar
            def _(e):
                for f in ops["act"]:
                    f(e)

            @block.vector
            def _(e):
                for f in ops["dve"]:
                    f(e)

            @block.gpsimd
            def _(e):
                for f in ops["pool"]:
                    f(e)

            @block.sync
            def _(e):
                for f in ops["sp"]:
                    f(e)


class Builder:
    def __init__(self, stage=99, dump=None):
        self.stage = stage
        self.dump = dump
        self.nc = bass.Bass("TRN2", target_bir_lowering=False)
        self.es = ExitStack()
        self.S = None
        self.ins = {}
        self.outs = {}
        self.out_sems = []

    def din(self, name, shape, dt=F32):
        ap = self.nc.dram_tensor(name, list(shape), dt, kind="ExternalInput").ap()
        self.ins[name] = ap
        return ap

    def dout(self, name, shape, dt=F32):
        ap = self.nc.dram_tensor(name, list(shape), dt, kind="ExternalOutput").ap()
        self.outs[name] = ap
        return ap

    def sb(self, name, shape, dt):
        return self.es.enter_context(self.nc.sbuf_tensor(name, list(shape), dt))

    def ps(self, name, shape, dt=F32):
        return self.es.enter_context(self.nc.psum_tensor(name, list(shape), dt))

    def bank(self):
        k = self.bank_i % len(self.banks)
        self.bank_i += 1
        return k, self.banks[k], ("ps", k)

    def load_slab(self, wap, r0, c0, ncols=SW, big=False):
        S = self.S
        if big:
            k = self.wb_i % len(self.wbig)
            self.wb_i += 1
            buf, sem, key = self.wbig[k], self.wbigsem[k], ("wbig", k)
        else:
            k = self.w_i % len(self.wbuf)
            self.w_i += 1
            buf, sem, key = self.wbuf[k], self.wsem[k], ("w", k)
        src = wap[r0:r0 + 2048, c0:c0 + ncols].rearrange("(kc p) n -> p kc n", p=128)
        S.dma("pool", lambda e, buf=buf, src=src, ncols=ncols: e.dma_start(out=buf[:, :, 0:ncols], in_=src),
              sem, w=[key])
        return buf, key

    def ntiles(self, ncols):
        out = []
        n0 = 0
        while n0 < ncols:
            n = min(512, ncols - n0)
            out.append((n0, n))
            n0 += n
        return out

    def linear_fm(self, wap, r0, c0, nout_chunks, src, src_key, ncols, evac, sw=SW, big=False):
        S = self.S
        per = sw // 128
        slabs = list(range(0, nout_chunks, per))
        depth = 3 if big else 1
        pend = [self.load_slab(wap, r0, c0 + slabs[i] * 128, sw, big=big) for i in range(min(depth, len(slabs)))]
        for si_, s0 in enumerate(slabs):
            buf, wkey = pend.pop(0)
            if si_ + depth < len(slabs):
                pend.append(self.load_slab(wap, r0, c0 + slabs[si_ + depth] * 128, sw, big=big))
            for mi in range(per):
                m = s0 + mi
                for (n0, n) in self.ntiles(ncols):
                    bk, psb, pskey = self.bank()
                    fns = []
                    for kc in range(16):
                        fns.append(lambda e, psb=psb, buf=buf, kc=kc, mi=mi, n0=n0, n=n, src=src:
                                   e.matmul(psb[:, 0:n], buf[:, kc, mi * 128:(mi + 1) * 128], src[:, kc, n0:n0 + n],
                                            start=(kc == 0), stop=(kc == 15)))
                    S.group("pe", fns, r=[wkey] + [(src_key, kc) for kc in range(16)], w=[pskey])
                    evac(m, n0, n, psb[:, 0:n], pskey)

    def rmsnorm(self, gcol, ncols):
        S = self.S
        R, xn, rstd = self.R, self.xn, self.rstd
        nts = self.ntiles(ncols)
        pbanks = [self.bank() for _ in nts]
        for c in range(16):
            sq = self.sq[0]
            sk = ("sq", 0)
            S.op("act", lambda e, sq=sq, c=c: e.activation(sq[:, 0:ncols], R[:, c, 0:ncols], AF.Square),
                 r=[("R", c)], w=[sk])
            for (n0, n), (bk, psb, pskey) in zip(nts, pbanks):
                S.op("pe", lambda e, psb=psb, sq=sq, n0=n0, n=n, c=c:
                     e.matmul(psb[:, 0:n], self.ones_f[:], sq[:, n0:n0 + n], start=(c == 0), stop=(c == 15)),
                     r=[sk, "const"], w=[pskey])
        for (n0, n), (bk, psb, pskey) in zip(nts, pbanks):
            S.op("act", lambda e, psb=psb, n0=n0, n=n: e.activation(rstd[:, n0:n0 + n], psb[:, 0:n], AF.Sqrt,
                                                                      bias=self.eps_t[:, 0:1], scale=1.0 / D),
                 r=[pskey, "const"], w=["rstd"])
        S.op("dve", lambda e: e.reciprocal(rstd[:, 0:ncols], rstd[:, 0:ncols]), r=["rstd"], w=["rstd"])
        for c in range(16):
            S.op("dve", lambda e, c=c: e.scalar_tensor_tensor(xn[:, c, 0:ncols], R[:, c, 0:ncols],
                                                           self.gv[:, gcol + c:gcol + c + 1], rstd[:, 0:ncols],
                                                           ALU.mult, ALU.mult),
                 r=[("R", c), "rstd", "gv"], w=[("xn", c)])

    def mlp(self, li, ncols):
        S = self.S
        R, xn, hid = self.R, self.xn, self.hid
        w_up = self.ins["mlp_w_up"]
        w_dn = self.ins["mlp_w_down"]
        for q in range(4):
            def evac_up(m, n0, n, ps, pskey):
                t = self.tmpb[self.tmp_i % 2]
                tk = ("tmpb", self.tmp_i % 2)
                self.tmp_i += 1
                S.op("act", lambda e, t=t, ps=ps, n=n: e.activation(t[:, 0:n], ps, AF.Relu), r=[pskey], w=[tk])
                S.op("dve", lambda e, t=t, m=m, n0=n0, n=n: e.tensor_tensor(hid[:, m, n0:n0 + n], t[:, 0:n], t[:, 0:n], ALU.mult),
                     r=[tk], w=[("hid", m)])
            self.linear_fm(w_up[li], 0, q * 2048, 16, xn, "xn", ncols, evac_up, sw=512, big=True)

            def evac_dn(m, n0, n, ps, pskey):
                S.op("dve", lambda e, m=m, n0=n0, n=n, ps=ps: e.tensor_tensor(R[:, m, n0:n0 + n], ps, R[:, m, n0:n0 + n], ALU.add),
                     r=[pskey, ("R", m)], w=[("R", m)])
            self.linear_fm(w_dn[li], q * 2048, 0, 16, hid, "hid", ncols, evac_dn, sw=512, big=True)

    def load_tile(self, tile, with_sample):
        S = self.S
        R = self.R
        xp = self.ins["xp"]
        for tb in range(TT // 128):
            st = self.stage_f[0]
            sk = ("stage", 0)
            t0 = tile * TT + tb * 128
            S.dma("sp", lambda e, st=st, t0=t0: e.dma_start(out=st[:, :], in_=xp[t0:t0 + 128, :]), self.ssem[0], w=[sk])
            for g4 in range(4):
                bk, psb, pskey = self.bank()
                fns = []
                for j in range(4):
                    c = g4 * 4 + j
                    fns.append(lambda e, psb=psb, st=st, c=c, j=j: e.transpose(psb[:, j * 128:(j + 1) * 128], st[:, c * 128:(c + 1) * 128], self.ident_f[:]))
                S.group("pe", fns, r=[sk, "const"], w=[pskey])
                eng = "act" if g4 % 2 == 0 else "dve"
                if eng == "act":
                    S.op("act", lambda e, psb=psb, g4=g4, tb=tb: e.activation(
                        R[:, g4 * 4:(g4 + 1) * 4, tb * 128:(tb + 1) * 128], psb[:, 0:512].rearrange("p (j n) -> p j n", j=4), AF.Copy),
                        r=[pskey], w=[("R", g4 * 4 + j) for j in range(4)])
                else:
                    S.op("dve", lambda e, psb=psb, g4=g4, tb=tb: e.tensor_copy(
                        R[:, g4 * 4:(g4 + 1) * 4, tb * 128:(tb + 1) * 128], psb[:, 0:512].rearrange("p (j n) -> p j n", j=4)),
                        r=[pskey], w=[("R", g4 * 4 + j) for j in range(4)])
        if with_sample:
            xs = self.ins["xs"]
            st = self.stage_f[0]
            sk = ("stage", 0)
            S.dma("sp", lambda e, st=st: e.dma_start(out=st[0:NS, :], in_=xs[:, :]), self.ssem[0], w=[sk])
            bk, psb, pskey = self.bank()
            fns = []
            for c in range(16):
                fns.append(lambda e, psb=psb, st=st, c=c: e.transpose(psb[:, c * NS:(c + 1) * NS], st[0:NS, c * 128:(c + 1) * 128], self.ident_f[0:NS, 0:NS]))
            S.group("pe", fns, r=[sk, "const"], w=[pskey])
            S.op("dve", lambda e, psb=psb: e.tensor_copy(R[:, :, TT:TT + NS], psb[:, 0:16 * NS].rearrange("p (c n) -> p c n", c=16)),
                 r=[pskey], w=[("R", c) for c in range(16)])

    def store_tile(self, tile, with_sample):
        S = self.S
        R = self.R
        yp = self.outs["yp"]
        for tb in range(TT // 128):
            st = self.stage_f[0]
            sk = ("stage", 0)
            for g4 in range(4):
                bk, psb, pskey = self.bank()
                fns = []
                for j in range(4):
                    c = g4 * 4 + j
                    fns.append(lambda e, psb=psb, c=c, j=j, tb=tb: e.transpose(psb[:, j * 128:(j + 1) * 128], R[:, c, tb * 128:(tb + 1) * 128], self.ident_f[:]))
                S.group("pe", fns, r=[("R", g4 * 4 + j) for j in range(4)] + ["const"], w=[pskey])
                if g4 % 2 == 0:
                    S.op("act", lambda e, psb=psb, st=st, g4=g4: e.activation(st[:, g4 * 512:(g4 + 1) * 512], psb[:, 0:512], AF.Copy), r=[pskey], w=[sk])
                else:
                    S.op("dve", lambda e, psb=psb, st=st, g4=g4: e.tensor_copy(st[:, g4 * 512:(g4 + 1) * 512], psb[:, 0:512]), r=[pskey], w=[sk])
            t0 = tile * TT + tb * 128
            S.dma("sp", lambda e, st=st, t0=t0: e.dma_start(out=yp[t0:t0 + 128, :], in_=st[:, :]), self.osem, r=[sk])
        if with_sample:
            ys = self.outs["ys"]
            st = self.stage_f[0]
            sk = ("stage", 0)
            for g4 in range(4):
                bk, psb, pskey = self.bank()
                fns = []
                for j in range(4):
                    c = g4 * 4 + j
                    fns.append(lambda e, psb=psb, c=c, j=j: e.transpose(psb[0:NS, j * 128:(j + 1) * 128], R[:, c, TT:TT + NS], self.ident_f[:]))
                S.group("pe", fns, r=[("R", g4 * 4 + j) for j in range(4)] + ["const"], w=[pskey])
                S.op("dve", lambda e, psb=psb, st=st, g4=g4: e.tensor_copy(st[0:NS, g4 * 512:(g4 + 1) * 512], psb[0:NS, 0:512]), r=[pskey], w=[sk])
            S.dma("sp", lambda e, st=st: e.dma_start(out=ys[:, :], in_=st[0:NS, :]), self.osem, r=[sk])


    def load_cols(self, src_rows, k, dst_fn, wkeys, q="sp"):
        S = self.S
        st = self.stage_f[0]
        sk = ("stage", 0)
        S.dma(q, lambda e: e.dma_start(out=st[0:k, :], in_=src_rows), self.ssem[0], w=[sk])
        bk, psb, pskey = self.bank()
        fns = []
        for c in range(16):
            fns.append(lambda e, c=c: e.transpose(psb[:, c * k:(c + 1) * k], st[0:k, c * 128:(c + 1) * 128], self.ident_f[0:k, 0:k]))
        S.group("pe", fns, r=[sk, "const"], w=[pskey])
        S.op("dve", lambda e: e.tensor_copy(dst_fn(), psb[:, 0:16 * k].rearrange("p (c n) -> p c n", c=16)), r=[pskey], w=wkeys)

    def store_cols(self, src_fn, k, dst_rows, rkeys, osem=None):
        S = self.S
        st = self.stage_f[0]
        sk = ("stage", 0)
        for g4 in range(4):
            bk, psb, pskey = self.bank()
            fns = []
            for j in range(4):
                c = g4 * 4 + j
                fns.append(lambda e, c=c, j=j, psb=psb: e.transpose(psb[0:k, j * 128:(j + 1) * 128], src_fn(c), self.ident_f[:]))
            S.group("pe", fns, r=list(rkeys) + ["const"], w=[pskey])
            S.op("dve", lambda e, g4=g4, psb=psb: e.tensor_copy(st[0:k, g4 * 512:(g4 + 1) * 512], psb[0:k, 0:512]), r=[pskey], w=[sk])
        S.dma("sp", lambda e: e.dma_start(out=dst_rows, in_=st[0:k, :]), osem or self.osem, r=[sk])

    def evac_add_R(self):
        S = self.S
        R = self.R

        def ev(m, n0, n, ps, pskey):
            S.op("dve", lambda e: e.tensor_tensor(R[:, m, n0:n0 + n], ps, R[:, m, n0:n0 + n], ALU.add),
                 r=[pskey, ("R", m)], w=[("R", m)])
        return ev

    def conv_mixer(self, tile, ncols, ws):
        S = self.S
        F1, xn, hid = self.F1, self.xn, self.hid
        w_in = self.ins["conv_w_in"][0]
        w_out = self.ins["conv_w_out"][0]
        allF = [("F1", m) for m in range(16)]

        def col(n0):
            return 2 + n0 if n0 < TT else n0 + 4
        S.op("pool", lambda e: e.tensor_copy(F1[:, :, 0:2], self.ccarry[:, :, :]), r=["ccarry"], w=allF)
        if ws:
            S.op("pool", lambda e: e.tensor_copy(F1[:, :, TT + 2:TT + 4], self.sconv[:, :, :]), r=["sconv"], w=allF)

        def evA(m, n0, n, ps, pskey):
            c0 = col(n0)
            S.op("act", lambda e: e.activation(F1[:, m, c0:c0 + n], ps, AF.Copy), r=[pskey], w=[("F1", m)])
        self.linear_fm(w_in, 0, 2048, 16, xn, "xn", ncols, evA)

        def evB(m, n0, n, ps, pskey):
            c0 = col(n0)
            S.op("dve", lambda e: e.tensor_tensor(F1[:, m, c0:c0 + n], ps, F1[:, m, c0:c0 + n], ALU.mult), r=[pskey, ("F1", m)], w=[("F1", m)])
        self.linear_fm(w_in, 0, 4096, 16, xn, "xn", ncols, evB)

        def evC(m, n0, n, ps, pskey):
            c0 = col(n0)
            t = self.tmpf[self.tmpf_i % 2]
            tk = ("tmpf", self.tmpf_i % 2)
            self.tmpf_i += 1
            S.op("dve", lambda e: e.tensor_scalar(t[:, 0:n], F1[:, m, c0 - 2:c0 - 2 + n], self.dwv[:, m:m + 1], None, ALU.mult), r=[("F1", m), "gv"], w=[tk])
            S.op("dve", lambda e: e.scalar_tensor_tensor(t[:, 0:n], F1[:, m, c0 - 1:c0 - 1 + n], self.dwv[:, 16 + m:17 + m], t[:, 0:n], ALU.mult, ALU.add), r=[("F1", m), tk, "gv"], w=[tk])
            S.op("dve", lambda e: e.scalar_tensor_tensor(t[:, 0:n], F1[:, m, c0:c0 + n], self.dwv[:, 32 + m:33 + m], t[:, 0:n], ALU.mult, ALU.add), r=[("F1", m), tk, "gv"], w=[tk])
            S.op("dve", lambda e: e.tensor_tensor(hid[:, m, n0:n0 + n], ps, t[:, 0:n], ALU.mult), r=[pskey, tk], w=[("hid", m)])
        self.linear_fm(w_in, 0, 0, 16, xn, "xn", ncols, evC)
        self.linear_fm(w_out, 0, 0, 16, hid, "hid", ncols, self.evac_add_R())
        S.op("pool", lambda e: e.tensor_copy(self.ccarry[:, :, :], F1[:, :, TT:TT + 2]), r=allF, w=["ccarry"])
        if ws:
            self.store_cols(lambda c: self.ccarry[:, c, :], 2, self.outs["conv_p"], ["ccarry"])
            self.store_cols(lambda c: F1[:, c, TT + 6:TT + 8], 2, self.outs["conv_s"], allF)

    def tm_proj(self, wap, c0, nslabs, tok_blocks, evac):
        S = self.S
        xn = self.xn
        nxt = self.load_slab(wap, 0, c0, SW)
        for sl in range(nslabs):
            buf, wkey = nxt
            if sl + 1 < nslabs:
                nxt = self.load_slab(wap, 0, c0 + (sl + 1) * SW, SW)
            for (t0, ntok) in tok_blocks:
                bk, psb, pskey = self.bank()
                fns = []
                for kc in range(16):
                    fns.append(lambda e, psb=psb, buf=buf, kc=kc, t0=t0, ntok=ntok: e.matmul(
                        psb[0:ntok, 0:SW], xn[:, kc, t0:t0 + ntok], buf[:, kc, 0:SW], start=(kc == 0), stop=(kc == 15)))
                S.group("pe", fns, r=[wkey] + [("xn", kc) for kc in range(16)], w=[pskey])
                evac(sl, t0, ntok, psb, pskey)

    def attn_block(self, zrhs, zk, N, kT_ap, kkeys, v_ap, vkeys, mask, bias_ap, carry, ckey, o_ps, okey, first, last, tri):
        S = self.S
        scale = 128.0 ** -0.5
        bk, zps, zkey = self.bank()
        S.op("pe", lambda e: e.matmul(zps[:, 0:N], kT_ap, zrhs, start=True, stop=True), r=list(kkeys) + list(zk), w=[zkey])
        i = self.ab_i % 2
        self.ab_i += 1
        E, L, Lb, T1, Ab = self.abE[0], self.abL[0], self.abLb[i], self.abT[0], self.abA[i]
        ek, lk, lbk, tk, ak = ("abE", 0), ("abL", 0), ("abLb", i), ("abT", 0), ("abA", i)
        S.op("act", lambda e: e.activation(E[:, 0:N], zps[:, 0:N], AF.Exp, bias=bias_ap, scale=scale), r=[zkey, "abias"], w=[ek])
        S.op("act", lambda e: e.activation(L[:, 0:N], E[:, 0:N], AF.Ln, bias=self.one_t[:, 0:1]), r=[ek, "const"], w=[lk])
        if mask is not None:
            S.op("dve", lambda e: e.tensor_tensor(Lb[:, 0:N], L[:, 0:N], mask, ALU.mult), r=[lk, "masks"], w=[lbk])
        else:
            S.op("dve", lambda e: e.tensor_copy(Lb[:, 0:N], L[:, 0:N]), r=[lk], w=[lbk])
        bk2, sps, skey = self.bank()
        S.op("pe", lambda e: e.matmul(sps[:, 0:N], tri, Lb[:, 0:N], start=True, stop=True), r=[lbk, "const"], w=[skey])
        bk3, cps, cpkey = self.bank()
        S.op("pe", lambda e: e.matmul(cps[:, 0:N], self.ones_b[:], Lb[:, 0:N], start=True, stop=True), r=[lbk, "const"], w=[cpkey])
        if first:
            S.op("dve", lambda e: e.tensor_scalar(T1[:, 0:N], zps[:, 0:N], scale, None, ALU.mult), r=[zkey], w=[tk])
        else:
            S.op("dve", lambda e: e.scalar_tensor_tensor(T1[:, 0:N], zps[:, 0:N], scale, carry, ALU.mult, ALU.subtract), r=[zkey, ckey], w=[tk])
        S.op("dve", lambda e: e.tensor_tensor(T1[:, 0:N], T1[:, 0:N], sps[:, 0:N], ALU.subtract), r=[tk, skey], w=[tk])
        if mask is not None:
            S.op("act", lambda e: e.activation(E[:, 0:N], T1[:, 0:N], AF.Exp, bias=bias_ap), r=[tk, "abias"], w=[ek])
            S.op("dve", lambda e: e.tensor_tensor(Ab[:, 0:N], E[:, 0:N], mask, ALU.mult), r=[ek, "masks"], w=[ak])
        else:
            S.op("act", lambda e: e.activation(Ab[:, 0:N], T1[:, 0:N], AF.Exp, bias=bias_ap), r=[tk, "abias"], w=[ak])
        if first:
            S.op("dve", lambda e: e.tensor_copy(carry, cps[:, 0:N]), r=[cpkey], w=[ckey])
        else:
            S.op("dve", lambda e: e.tensor_tensor(carry, carry, cps[:, 0:N], ALU.add), r=[cpkey, ckey], w=[ckey])
        S.op("pe", lambda e: e.matmul(o_ps, v_ap, Ab[:, 0:N], start=first, stop=last), r=[ak] + list(vkeys), w=[okey])

    def attention(self, tile, ncols, ws):
        S = self.S
        xn, hid, F1 = self.xn, self.hid, self.F1
        wqkv = self.ins["attn_w_qkv"][0]
        wo = self.ins["attn_w_o"][0]
        qT, kT, Vt = self.qT, self.kT, self.Vt
        tok_blocks = [(tb * 128, 128) for tb in range(TT // 128)] + ([(TT, NS)] if ws else [])
        kout, vout = self.outs["kp"], self.outs["vp"]

        import os
        def norm_evac(which):
            dst = qT if which == 0 else kT
            dkey = "qT" if which == 0 else "kT"
            gain = self.qgain if which == 0 else self.kgain

            def ev(sl, t0, ntok, psb, pskey):
                i = self.nq_i % 2
                self.nq_i += 1
                sqt, ssq, nrm = self.nsq[i], self.nss[i], self.nrm[i]
                NE = int(os.environ.get("NE", "9"))
                S.op("act", lambda e: e.activation(sqt[0:ntok, :], psb[0:ntok, 0:SW], AF.Square), r=[pskey], w=[("nsq", i)])
                if NE <= 1:
                    return
                S.op("dve", lambda e: e.tensor_reduce(ssq[0:ntok, :], sqt[0:ntok, :].rearrange("p (h d) -> p h d", h=2), AX.X, ALU.add), r=[("nsq", i)], w=[("nss", i)])
                S.op("act", lambda e: e.activation(ssq[0:ntok, :], ssq[0:ntok, :], AF.Sqrt, bias=self.eps_t[0:ntok, 0:1], scale=1.0 / 128), r=[("nss", i), "const"], w=[("nss", i)])
                S.op("dve", lambda e: e.reciprocal(ssq[0:ntok, :], ssq[0:ntok, :]), r=[("nss", i)], w=[("nss", i)])
                if NE <= 2:
                    return
                for h2 in range(2):
                    S.op("dve", lambda e, h2=h2: e.scalar_tensor_tensor(nrm[0:ntok, h2 * 128:(h2 + 1) * 128], psb[0:ntok, h2 * 128:(h2 + 1) * 128],
                                                                      ssq[0:ntok, h2:h2 + 1], gain[0:ntok, :], ALU.mult, ALU.mult),
                         r=[pskey, ("nss", i), "gains"], w=[("nrm", i)])
                if which == 1 and not os.environ.get("NOKDMA"):
                    dstk = self.outs["ks"][:, sl * SW:(sl + 1) * SW] if t0 >= TT else kout[tile * TT + t0:tile * TT + t0 + ntok, sl * SW:(sl + 1) * SW]
                    S.dma("sp", lambda e: e.dma_start(out=dstk, in_=nrm[0:ntok, :]), self.osem_n[i], r=[("nrm", i)])
                if NE <= 3:
                    return
                bk, tps, tkey = self.bank()
                fns = [lambda e, h2=h2: e.transpose(tps[:, h2 * 128:h2 * 128 + ntok], nrm[0:ntok, h2 * 128:(h2 + 1) * 128], self.ident_f[0:ntok, 0:ntok]) for h2 in range(2)]
                S.group("pe", fns, r=[("nrm", i), "const"], w=[tkey])
                if NE <= 4:
                    return
                for h2 in range(2):
                    h = sl * 2 + h2
                    S.op("dve", lambda e, h=h, h2=h2: e.tensor_copy(dst[:, h, t0:t0 + ntok], tps[:, h2 * 128:h2 * 128 + ntok]), r=[tkey], w=[(dkey, h)])
            return ev
        import os
        sub = int(os.environ.get("ATT_SUB", "9"))
        if sub <= 0:
            return
        self.tm_proj(wqkv, 0, 8, tok_blocks, norm_evac(0))
        if sub == 1 and os.environ.get("ATT_Q"):
            return
        self.tm_proj(wqkv, 2048, 8, tok_blocks, norm_evac(1))

        def v_evac(sl, t0, ntok, psb, pskey):
            i = self.nq_i % 2
            self.nq_i += 1
            nrm = self.nrm[i]
            tb = t0 // 128
            S.op("dve", lambda e: e.tensor_copy(nrm[0:ntok, :], psb[0:ntok, 0:SW]), r=[pskey], w=[("nrm", i)])
            S.op("dve", lambda e: e.tensor_copy(Vt[0:ntok, tb, sl * SW:(sl + 1) * SW], psb[0:ntok, 0:SW]), r=[pskey], w=[("Vt", tb)])
            dstv = self.outs["vs"][:, sl * SW:(sl + 1) * SW] if t0 >= TT else vout[tile * TT + t0:tile * TT + t0 + ntok, sl * SW:(sl + 1) * SW]
            S.dma("sp", lambda e: e.dma_start(out=dstv, in_=nrm[0:ntok, :]), self.osem_n[i], r=[("nrm", i)])
        self.tm_proj(wqkv, 4096, 8, tok_blocks, v_evac)
        import os
        sub = int(os.environ.get("ATT_SUB", "9"))
        if sub <= 1:
            return
        if tile < NTILES - 1:
            S.dma("sp", lambda e: e.dma_start(out=self.kT_scr[:, :, tile * TT:(tile + 1) * TT].rearrange("h d t -> d h t"), in_=kT[:, :, 0:TT]),
                  self.scrsem, r=[("kT", h) for h in range(16)], w=[("kscr", tile)])
            S.dma("sp", lambda e: e.dma_start(out=self.v_scr[tile * TT:(tile + 1) * TT, :].rearrange("(tb p) n -> p tb n", p=128), in_=Vt[:, 0:TT // 128, :]),
                  self.scrsem, r=[("Vt", tb) for tb in range(TT // 128)], w=[("vscr", tile)])
        if sub <= 2:
            return
        nprev = tile * TT
        for h in range(16 if sub > 3 else 1):
            pi = 0
            if nprev:
                S.dma("sp", lambda e, h=h: e.dma_start(out=self.kprev[pi][:, 0:nprev], in_=self.kT_scr[h, :, 0:nprev]), self.pvsem[pi],
                      r=[("kscr", t) for t in range(tile)], w=[("kprev", pi)])
                S.dma("sp", lambda e, h=h: e.dma_start(out=self.vprev[pi][:, 0:nprev // 128, :], in_=self.v_scr[0:nprev, h * 128:(h + 1) * 128].rearrange("(tb p) d -> p tb d", p=128)),
                      self.pvsem_v, r=[("vscr", t) for t in range(tile)], w=[("vprev", pi)])
            ci = self.ab_c % 2
            self.ab_c += 1
            carry = self.carry[ci]
            ckey = ("carry", ci)
            obk, ops_, okey = self.obank(ci)
            nblk = TT // 128 + nprev // 128
            bi = 0
            bias_ap = self.abias[:, h:h + 1]
            for kb in reversed(range(TT // 128)):
                self.attn_block(qT[:, h, 0:TT], [("qT", h)], TT, kT[:, h, kb * 128:(kb + 1) * 128], [("kT", h)],
                                Vt[:, kb, h * 128:(h + 1) * 128], [("Vt", kb)], self.masks[:, kb, :], bias_ap,
                                carry[:, 0:TT], ckey, ops_[:, 0:TT], okey, bi == 0, bi == nblk - 1, self.tri_b[:])
                bi += 1
            for pb in reversed(range(nprev // 128)):
                self.attn_block(qT[:, h, 0:TT], [("qT", h)], TT, self.kprev[pi][:, pb * 128:(pb + 1) * 128], [("kprev", pi)],
                                self.vprev[pi][:, pb, :], [("vprev", pi)], None, bias_ap,
                                carry[:, 0:TT], ckey, ops_[:, 0:TT], okey, bi == 0, bi == nblk - 1, self.tri_b[:])
                bi += 1
            S.op("dve", lambda e, h=h, ops_=ops_: e.tensor_copy(hid[:, h, 0:TT], ops_[:, 0:TT]), r=[okey], w=[("hid", h)])
        if ws and self.sample_on:
            self.sample_attention()
        self.linear_fm(wo, 0, 0, 16, hid, "hid", ncols, self.evac_add_R())

    def sample_attention(self):
        S = self.S
        qT, kT, Vt, hid = self.qT, self.kT, self.Vt, self.hid
        scale = 128.0 ** -0.5
        NQ = NS * 16
        pk = self.ins["cache_k"]
        pv = self.ins["cache_v"]
        rows_k = pk.rearrange("a r h d -> (a r) (h d)")
        rows_v = pv.rearrange("a r h d -> (a r) (h d)")
        carry = self.scarry
        obk, ops_, okey = self.obank(0)
        nblk = NPAGES + 1
        for bi in range(nblk):
            first = bi == 0
            last = bi == nblk - 1
            i = bi % 2
            KTp = self.KTp[0]
            kk = ("KTp", 0)
            if first:
                S.op("dve", lambda e: e.memset(KTp[:, :, :], 0.0), w=[kk])
                S.op("dve", lambda e: e.tensor_copy(KTp[:, :, 0:NS], kT[:, :, TT:TT + NS]), r=[("kT", h) for h in range(16)] + [kk], w=[kk])
                vsrc = lambda h: Vt[:, TT // 128, h * 128:(h + 1) * 128]
                vkeys = [("Vt", TT // 128)]
                mask = self.smask[:, :]
            else:
                page = NPAGES - bi
                pg = self.pgK[0]
                S.dma("pool", lambda e, pg=pg, page=page: e.indirect_dma_start(out=pg[:, :], out_offset=None, in_=rows_k,
                      in_offset=bass.IndirectOffsetOnAxis(ap=self.pidx[:, page:page + 1], axis=0)), self.pgsem[0], r=["pidx"], w=[("stage", 0)])
                pgv = self.pgV[i]
                S.dma("pool", lambda e, pgv=pgv, page=page: e.indirect_dma_start(out=pgv[:, :], out_offset=None, in_=rows_v,
                      in_offset=bass.IndirectOffsetOnAxis(ap=self.pidx[:, page:page + 1], axis=0)), self.pgsemV[i], r=["pidx"], w=[("pgV", i)])
                for g4 in range(4):
                    bk, tps, tkey = self.bank()
                    fns = [lambda e, j=j, tps=tps, pg=pg, g4=g4: e.transpose(tps[:, j * 128:(j + 1) * 128], pg[:, (g4 * 4 + j) * 128:(g4 * 4 + j + 1) * 128], self.ident_f[:]) for j in range(4)]
                    S.group("pe", fns, r=[("stage", 0), "const"], w=[tkey])
                    if False:
                        pass
                    else:
                        S.op("dve", lambda e, tps=tps, g4=g4: e.tensor_copy(KTp[:, g4 * 4:(g4 + 1) * 4, :], tps[:, 0:512].rearrange("p (j n) -> p j n", j=4)), r=[tkey], w=[kk])
                vsrc = lambda h, i=i: self.pgV[i][:, h * 128:(h + 1) * 128]
                vkeys = [("pgV", i)]
                mask = None
            bk, zps, zkey = self.bank()
            fns = [lambda e, h=h, zps=zps, KTp=KTp: e.matmul(zps[:, h * 16:h * 16 + NS], KTp[:, h, :], qT[:, h, TT:TT + NS], start=True, stop=True) for h in range(16)]
            S.group("pe", fns, r=[kk] + [("qT", h) for h in range(16)], w=[zkey])
            j = self.ab_i % 2
            self.ab_i += 1
            E, L, Lb, T1, Ab = self.abE[0], self.abL[0], self.abLb[j], self.abT[0], self.abA[j]
            ek, lk, lbk, tk, ak = ("abE", 0), ("abL", 0), ("abLb", j), ("abT", 0), ("abA", j)
            S.op("dve", lambda e, zps=zps, T1=T1: e.scalar_tensor_tensor(T1[:, 0:NQ].rearrange("p (h q) -> p h q", q=NS), zps[:, 0:256].rearrange("p (h c) -> p h c", c=16)[:, :, 0:NS], scale,
                                                                        self.sbiasrow[:, :].rearrange("p (h q) -> p h q", q=NS), ALU.mult, ALU.add), r=[zkey, "abias"], w=[tk])
            S.op("act", lambda e, E=E, T1=T1: e.activation(E[:, 0:NQ], T1[:, 0:NQ], AF.Exp), r=[tk], w=[ek])
            S.op("act", lambda e, E=E, L=L: e.activation(L[:, 0:NQ], E[:, 0:NQ], AF.Ln, bias=self.one_t[:, 0:1]), r=[ek, "const"], w=[lk])
            if mask is not None:
                S.op("dve", lambda e, Lb=Lb, L=L, mask=mask: e.tensor_tensor(Lb[:, 0:NQ], L[:, 0:NQ], mask, ALU.mult), r=[lk, "masks"], w=[lbk])
            else:
                S.op("dve", lambda e, Lb=Lb, L=L: e.tensor_copy(Lb[:, 0:NQ], L[:, 0:NQ]), r=[lk], w=[lbk])
            bk2, sps, skey = self.bank()
            S.op("pe", lambda e, sps=sps, Lb=Lb: e.matmul(sps[:, 0:NQ], self.tri_b[:], Lb[:, 0:NQ], start=True, stop=True), r=[lbk, "const"], w=[skey])
            bk3, cps, cpkey = self.bank()
            S.op("pe", lambda e, cps=cps, Lb=Lb: e.matmul(cps[:, 0:NQ], self.ones_b[:], Lb[:, 0:NQ], start=True, stop=True), r=[lbk, "const"], w=[cpkey])
            if not first:
                S.op("dve", lambda e, T1=T1: e.tensor_tensor(T1[:, 0:NQ], T1[:, 0:NQ], carry[:, 0:NQ], ALU.subtract), r=[tk, "scarry"], w=[tk])
            S.op("dve", lambda e, T1=T1, sps=sps: e.tensor_tensor(T1[:, 0:NQ], T1[:, 0:NQ], sps[:, 0:NQ], ALU.subtract), r=[tk, skey], w=[tk])
            if mask is not None:
                S.op("act", lambda e, E=E, T1=T1: e.activation(E[:, 0:NQ], T1[:, 0:NQ], AF.Exp), r=[tk], w=[ek])
                S.op("dve", lambda e, Ab=Ab, E=E, mask=mask: e.tensor_tensor(Ab[:, 0:NQ], E[:, 0:NQ], mask, ALU.mult), r=[ek, "masks"], w=[ak])
            else:
                S.op("act", lambda e, Ab=Ab, T1=T1: e.activation(Ab[:, 0:NQ], T1[:, 0:NQ], AF.Exp), r=[tk], w=[ak])
            if first:
                S.op("dve", lambda e, cps=cps: e.tensor_copy(carry[:, 0:NQ], cps[:, 0:NQ]), r=[cpkey], w=["scarry"])
            else:
                S.op("dve", lambda e, cps=cps: e.tensor_tensor(carry[:, 0:NQ], carry[:, 0:NQ], cps[:, 0:NQ], ALU.add), r=[cpkey, "scarry"], w=["scarry"])
            bk4, obp, obkey = self.bank()
            fns = [lambda e, h=h, Ab=Ab, vsrc=vsrc, obp=obp: e.matmul(obp[:, h * 16:h * 16 + NS], vsrc(h), Ab[:, h * NS:(h + 1) * NS], start=True, stop=True) for h in range(16)]
            S.group("pe", fns, r=[ak] + vkeys, w=[obkey])
            oview = obp[:, 0:256].rearrange("p (h c) -> p h c", c=16)[:, :, 0:NS]
            oacc = self.soacc[:, :].rearrange("p (h q) -> p h q", q=NS)
            if first:
                S.op("dve", lambda e, oview=oview: e.tensor_copy(oacc, oview), r=[obkey], w=["soacc"])
            else:
                S.op("dve", lambda e, oview=oview: e.tensor_tensor(oacc, oacc, oview, ALU.add), r=[obkey, "soacc"], w=["soacc"])
        ops_ = self.soacc
        okey = "soacc"
        S.op("dve", lambda e: e.tensor_copy(hid[:, :, TT:TT + NS], ops_[:, 0:NQ].rearrange("p (h n) -> p h n", h=16)), r=[okey], w=[("hid", h) for h in range(16)])
        import os
        if os.environ.get("DEBUG_SA"):
            dsm = self.osem
            S.dma("sp", lambda e: e.dma_start(out=self.outs["dbg_pgk"], in_=self.pgK[0][:, :]), dsm, r=[("stage", 0)])
            S.dma("sp", lambda e: e.dma_start(out=self.outs["dbg_pgv"], in_=self.pgV[0][:, :]), dsm, r=[("pgV", 0)])
            S.dma("sp", lambda e: e.dma_start(out=self.outs["dbg_carry"], in_=carry[:, 0:NQ]), dsm, r=["scarry"])
            t = self.tmpf[0]
            S.op("dve", lambda e: e.tensor_copy(t[:, 0:NQ], ops_[:, 0:NQ]), r=[okey], w=[("tmpf", 0)])
            S.dma("sp", lambda e: e.dma_start(out=self.outs["dbg_o"], in_=t[:, 0:NQ]), dsm, r=[("tmpf", 0)])
            S.dma("sp", lambda e: e.dma_start(out=self.outs["dbg_q"], in_=qT[:, :, TT:TT + NS]), dsm, r=[("qT", h) for h in range(16)])
            S.dma("sp", lambda e: e.dma_start(out=self.outs["dbg_ktp"], in_=self.KTp[0][:, :, :]), dsm, r=[("KTp", 0)])


    def s5_prep(self, j):
        S = self.S
        ins = self.ins
        A = {}
        o = [0]

        def f32(name, shape):
            ap = self.av(o[0], shape, F32)
            n = 1
            for d in shape[1:]:
                n *= d
            o[0] += n * 4
            A[name] = ap
            return ap
        for nm in ("lr", "li", "xr", "th", "c", "s", "t0", "t1", "t2", "t3", "nr", "cr", "ci"):
            f32(nm, [128, 64])
        f32("dt", [128, 1])
        uc = f32("uc", [128, 9, 64])
        us = f32("us", [128, 9, 64])
        mk = f32("mk", [128, 16, 64])
        pr = f32("pr", [128, 16, 64])
        pi = f32("pi", [128, 16, 64])
        br = f32("br", [128, 64, 16])
        bi = f32("bi", [128, 64, 16])
        Br = f32("Br", [128, 64, 16])
        Bi = f32("Bi", [128, 64, 16])
        Cr = f32("Cr", [128, 16, 64])
        Ci = f32("Ci", [128, 16, 64])
        tb = f32("tb", [128, 64, 16])
        tb2 = f32("tb2", [128, 64, 16])
        big = f32("big", [128, 64, 128])
        nat = f32("nat", [128, 128])
        f32("nat2", [128, 128])
        eg = [f32("eg%d" % i, [128, 3, 128]) for i in range(2)]
        wo = [self.av(o[0] + i * 1024, [128, 4, 128], BF16) for i in range(2)]
        o[0] += 2048
        K = "s5p"
        sem = self.s5sem
        q = "sp"
        S.dma(q, lambda e: e.dma_start(out=A["lr"], in_=ins["ssm_lambda_re"][j]), sem, w=[K])
        S.dma(q, lambda e: e.dma_start(out=A["li"], in_=ins["ssm_lambda_im"][j]), sem, w=[K])
        S.dma(q, lambda e: e.dma_start(out=A["dt"], in_=ins["ssm_log_dt"][j].rearrange("(g o) -> g o", o=1)), sem, w=[K])
        S.dma(q, lambda e: e.dma_start(out=br, in_=ins["ssm_b_re"][j]), sem, w=[K])
        S.dma(q, lambda e: e.dma_start(out=bi, in_=ins["ssm_b_im"][j]), sem, w=[K])
        S.dma(q, lambda e: e.dma_start(out=Cr, in_=ins["ssm_c_re"][j]), sem, w=[K])
        S.dma(q, lambda e: e.dma_start(out=Ci, in_=ins["ssm_c_im"][j]), sem, w=[K])

        def dv(fn):
            S.op("dve", fn, r=[K], w=[K])

        def ac(fn):
            S.op("act", fn, r=[K, "const"], w=[K])
        ac(lambda e: e.activation(A["dt"], A["dt"], AF.Exp))
        dv(lambda e: e.tensor_scalar(A["xr"], A["lr"], A["dt"][:, 0:1], None, ALU.mult))
        dv(lambda e: e.tensor_scalar(A["th"], A["li"], A["dt"][:, 0:1], None, ALU.mult))
        ac(lambda e: e.activation(A["s"], A["th"], AF.Sin, scale=1.0 / 32))
        ac(lambda e: e.activation(A["c"], A["th"], AF.Sin, bias=self.hpi_t[:, 0:1], scale=1.0 / 32))
        for _ in range(5):
            dv(lambda e: e.tensor_tensor(A["t0"], A["c"], A["c"], ALU.mult))
            dv(lambda e: e.tensor_tensor(A["t1"], A["s"], A["s"], ALU.mult))
            dv(lambda e: e.tensor_tensor(A["t2"], A["c"], A["s"], ALU.mult))
            dv(lambda e: e.tensor_tensor(A["c"], A["t0"], A["t1"], ALU.subtract))
            dv(lambda e: e.tensor_scalar(A["s"], A["t2"], 2.0, None, ALU.mult))
        dv(lambda e: e.memset(uc[:, 0, :], 1.0))
        dv(lambda e: e.memset(us[:, 0, :], 0.0))
        dv(lambda e: e.tensor_copy(uc[:, 1, :], A["c"]))
        dv(lambda e: e.tensor_copy(us[:, 1, :], A["s"]))
        for k in range(2, 9):
            dv(lambda e, k=k: e.tensor_tensor(A["t0"], uc[:, k - 1, :], A["c"], ALU.mult))
            dv(lambda e, k=k: e.tensor_tensor(A["t1"], us[:, k - 1, :], A["s"], ALU.mult))
            dv(lambda e, k=k: e.tensor_tensor(uc[:, k, :], A["t0"], A["t1"], ALU.subtract))
            dv(lambda e, k=k: e.tensor_tensor(A["t0"], uc[:, k - 1, :], A["s"], ALU.mult))
            dv(lambda e, k=k: e.tensor_tensor(A["t1"], us[:, k - 1, :], A["c"], ALU.mult))
            dv(lambda e, k=k: e.tensor_tensor(us[:, k, :], A["t0"], A["t1"], ALU.add))
        for ki in range(16):
            k = ki - 7
            ac(lambda e, ki=ki, k=k: e.activation(mk[:, ki, :], A["xr"], AF.Exp, scale=float(k)))
            dv(lambda e, ki=ki, k=k: e.tensor_tensor(pr[:, ki, :], mk[:, ki, :], uc[:, abs(k), :], ALU.mult))
            dv(lambda e, ki=ki, k=k: e.tensor_tensor(pi[:, ki, :], mk[:, ki, :], us[:, abs(k), :], ALU.mult))
            if k < 0:
                dv(lambda e, ki=ki: e.tensor_scalar(pi[:, ki, :], pi[:, ki, :], -1.0, None, ALU.mult))
        dv(lambda e: e.tensor_scalar(A["nr"], pr[:, 8, :], -1.0, None, ALU.add))
        dv(lambda e: e.tensor_tensor(A["t0"], A["lr"], A["lr"], ALU.mult))
        dv(lambda e: e.tensor_tensor(A["t1"], A["li"], A["li"], ALU.mult))
        dv(lambda e: e.tensor_tensor(A["t0"], A["t0"], A["t1"], ALU.add))
        dv(lambda e: e.reciprocal(A["t0"], A["t0"]))
        dv(lambda e: e.tensor_tensor(A["t1"], A["nr"], A["lr"], ALU.mult))
        dv(lambda e: e.tensor_tensor(A["t2"], pi[:, 8, :], A["li"], ALU.mult))
        dv(lambda e: e.tensor_tensor(A["t1"], A["t1"], A["t2"], ALU.add))
        dv(lambda e: e.tensor_tensor(A["cr"], A["t1"], A["t0"], ALU.mult))
        dv(lambda e: e.tensor_tensor(A["t1"], pi[:, 8, :], A["lr"], ALU.mult))
        dv(lambda e: e.tensor_tensor(A["t2"], A["nr"], A["li"], ALU.mult))
        dv(lambda e: e.tensor_tensor(A["t1"], A["t1"], A["t2"], ALU.subtract))
        dv(lambda e: e.tensor_tensor(A["ci"], A["t1"], A["t0"], ALU.mult))

        def bc_c(x):
            return x.unsqueeze(2).to_broadcast([128, 64, 16])
        dv(lambda e: e.tensor_tensor(tb, br, bc_c(A["cr"]), ALU.mult))
        dv(lambda e: e.tensor_tensor(tb2, bi, bc_c(A["ci"]), ALU.mult))
        dv(lambda e: e.tensor_tensor(Br, tb, tb2, ALU.subtract))
        dv(lambda e: e.tensor_tensor(tb, bi, bc_c(A["cr"]), ALU.mult))
        dv(lambda e: e.tensor_tensor(tb2, br, bc_c(A["ci"]), ALU.mult))
        dv(lambda e: e.tensor_tensor(Bi, tb, tb2, ALU.add))
        CrT = Cr.rearrange("g c p -> g p c")
        CiT = Ci.rearrange("g c p -> g p c")
        for kind in range(3):
            for ri in range(2):
                for i in range(8):
                    if kind == 0:
                        ki = (7 - i) + 7
                        X, Y = Br, Bi
                    else:
                        ki = (i - 7) + 7 if kind == 1 else (i + 1) + 7
                        X, Y = CrT, CiT
                    prk = bc_c(pr[:, ki, :])
                    pik = bc_c(pi[:, ki, :])
                    dst = big[:, :, i * 16:(i + 1) * 16]
                    if ri == 0:
                        dv(lambda e, X=X, prk=prk: e.tensor_tensor(tb, X, prk, ALU.mult))
                        dv(lambda e, Y=Y, pik=pik: e.tensor_tensor(tb2, Y, pik, ALU.mult))
                        dv(lambda e, dst=dst: e.tensor_tensor(dst, tb, tb2, ALU.subtract))
                    else:
                        dv(lambda e, Y=Y, prk=prk: e.tensor_tensor(tb, Y, prk, ALU.mult))
                        dv(lambda e, X=X, pik=pik: e.tensor_tensor(tb2, X, pik, ALU.mult))
                        dv(lambda e, dst=dst: e.tensor_tensor(dst, tb, tb2, ALU.add))
                        if kind > 0:
                            dv(lambda e, dst=dst: e.tensor_scalar(dst, dst, -1.0, None, ALU.mult))
                S.dma("sp", lambda e, kind=kind, ri=ri: e.dma_start(out=self.efg[:, kind, ri * 64:(ri + 1) * 64, :], in_=big), sem, r=[K], w=[K, "efg"])
        for name, ki in (("A8", 15), ("Am4", 3)):
            for vi, (sa, sb_) in enumerate(((1.0, 1.0), (-1.0, 1.0), (1.0, -1.0))):
                src = pr if vi == 0 else pi
                dv(lambda e, src=src, ki=ki, sa=sa: e.tensor_scalar(nat[:, 0:64], src[:, ki, :], sa, None, ALU.mult))
                dv(lambda e, src=src, ki=ki, sb_=sb_: e.tensor_scalar(nat[:, 64:128], src[:, ki, :], sb_, None, ALU.mult))
                bk, psb, pskey = self.bank()
                S.op("pe", lambda e, psb=psb: e.transpose(psb[:, 0:128], nat, self.ident_f[:]), r=[K, "const"], w=[pskey])
                idx = (0 if name == "A8" else 3) + vi
                S.op("dve", lambda e, psb=psb: e.tensor_copy(A["nat2"], psb[:, 0:128]), r=[pskey, K], w=[K])
                S.dma("sp", lambda e, idx=idx: e.dma_start(out=self.s5A_scr[j, idx], in_=A["nat2"]), sem, r=[K], w=[K, ("s5A", j)])
        for g in range(128):
            bi_ = g % 2
            et = eg[bi_]
            ek = ("s5eg", bi_)
            S.dma("sp", lambda e, g=g, et=et: e.dma_start(out=et, in_=self.efg[g].rearrange("k r n -> r k n")), self.s5sem2[bi_], r=["efg"], w=[ek])
            bk, p1, k1 = self.bank()
            S.op("pe", lambda e, p1=p1, et=et: e.transpose(p1[:, 0:128], et[:, 0, :], self.ident_f[:]), r=[ek, "const"], w=[k1])
            bk, p2, k2 = self.bank()
            S.op("pe", lambda e, p2=p2, et=et: e.matmul(p2[:, 0:128], et[:, 0, :], et[:, 1, :], start=True, stop=True), r=[ek], w=[k2])
            wt = wo[bi_]
            wk = ("s5wo", bi_)
            S.op("dve", lambda e, wt=wt, p1=p1: e.tensor_copy(wt[:, 0, :], p1[:, 0:128]), r=[k1], w=[wk])
            S.op("dve", lambda e, wt=wt, p1=p1: e.tensor_copy(wt[:, 1, 0:64], p1[:, 64:128]), r=[k1], w=[wk])
            S.op("dve", lambda e, wt=wt, p1=p1: e.tensor_copy(wt[:, 1, 64:128], p1[:, 0:64]), r=[k1], w=[wk])
            S.op("dve", lambda e, wt=wt, p2=p2: e.tensor_tensor(wt[:, 2, :], p2[:, 0:128], self.blkmask[:, :], ALU.mult), r=[k2, "const"], w=[wk])
            S.op("dve", lambda e, wt=wt, et=et: e.tensor_copy(wt[:, 3, :], et[:, 2, :]), r=[ek], w=[wk])
            S.dma("sp", lambda e, g=g, wt=wt: e.dma_start(out=self.s5w[j, g], in_=wt), self.s5sem3, r=[wk], w=[("s5w", j)])

    def s5_apply(self, j, li, tile, ncols, ws):
        S = self.S
        R, xn, hid = self.R, self.xn, self.hid
        ncol = NCH + 1 if ws else NCH
        G7 = 7
        V = self.av(0, [128, 128, NCH + 1], F32)
        Vs = self.av(128 * (NCH + 1) * 4, [128, 128, NCH + 1], F32)
        o = 2 * 128 * (NCH + 1) * 4
        wb = [self.av(o + i * 8192, [128, 8, 512], BF16) for i in range(2)]
        o += 16384
        Yg = self.av(0, [128, 128, NCH + 1], BF16)
        F2 = self.av(128 * (NCH + 1) * 2, [128, 16, NT], F32)
        XG = hid.rearrange("p a b -> p (a b)")[:, 0:128 * (NCH + 1)].rearrange("p (g n) -> p g n", g=128)
        Hall = xn.rearrange("p a b -> p (a b)")[:, 0:128 * (NCH + 1)].rearrange("p (g n) -> p g n", g=128)
        allhid = [("hid", c) for c in range(16)]
        allxn = [("xn", c) for c in range(16)]
        Atl = [self.av(o + i * 512, [128, 128], F32) for i in range(6)]
        o += 6 * 512
        A8, Am4 = Atl[0:3], Atl[3:6]
        st = dict(self.s5state[j])
        for nm in ("t1", "t2", "t3", "t4", "h0", "h0s", "hs0", "hs0s", "hsf", "so", "natA"):
            st[nm] = self.av(o, [128, 128], F32)
            o += 512
        self.s5u = self.av(2 * 128 * (NCH + 1) * 4, [128, NT], F32)
        self.s5y = self.av(2 * 128 * (NCH + 1) * 4 + NT * 4, [128, NT], F32)
        self.s5t = self.av(2 * 128 * (NCH + 1) * 4 + 2 * NT * 4, [128, NT], F32)
        for i in range(6):
            S.dma("sp", lambda e, i=i: e.dma_start(out=Atl[i], in_=self.s5A_scr[j, i]), self.s5sem, r=[("s5A", j)], w=["s5A"])
        if ws:
            for nm, first, second in (("h0", "sst_re", "sst_im"), ("h0s", "sst_im", "sst_re")):
                S.dma("sp", lambda e, first=first: e.dma_start(out=st["natA"][:, 0:64], in_=self.ins[first][j]), self.s5sem_n, w=["s5nat"])
                S.dma("sp", lambda e, second=second: e.dma_start(out=st["natA"][:, 64:128], in_=self.ins[second][j]), self.s5sem_n, w=["s5nat"])
                bk, psb, pskey = self.bank()
                S.op("pe", lambda e, psb=psb: e.transpose(psb[:, 0:128], st["natA"], self.ident_f[:]), r=["s5nat", "const"], w=[pskey])
                S.op("dve", lambda e, psb=psb, nm=nm: e.tensor_copy(st[nm], psb[:, 0:128]), r=[pskey], w=["s5h0"])
        for g0 in range(0, 128, G7):
            gs = list(range(g0, min(128, g0 + G7)))
            bk, psb, pskey = self.bank()
            fns = []
            for si, g in enumerate(gs):
                dc, gl = g // 8, g % 8
                for i in range(8):
                    fns.append(lambda e, si=si, dc=dc, gl=gl, i=i, psb=psb: e.matmul(
                        psb[:, si * 65:si * 65 + NCH], self.Zb[gl][:, 112 - 16 * i:240 - 16 * i], xn[:, dc, i:TT:8], start=(i == 0), stop=(i == 7)))
                if ws:
                    for k in range(NS):
                        fns.append(lambda e, si=si, dc=dc, gl=gl, k=k, psb=psb: e.matmul(
                            psb[:, si * 65 + NCH:si * 65 + NCH + 1], self.Zb[gl][:, 112 - 16 * (4 + k):240 - 16 * (4 + k)], xn[:, dc, TT + k:TT + k + 1],
                            start=(k == 0), stop=(k == NS - 1)))
            S.group("pe", fns, r=allxn + ["const", "Zb"], w=[pskey])
            n = len(gs)
            S.op("dve", lambda e, g0=g0, n=n, psb=psb: e.tensor_copy(XG[:, g0:g0 + n, 0:ncol], psb[:, 0:n * 65].rearrange("p (g c) -> p g c", c=65)[:, :, 0:ncol]),
                 r=[pskey], w=allhid)
        def load_wb(gb):
            bi_ = gb % 2
            S.dma("pool", lambda e, gb=gb, bi_=bi_: e.dma_start(out=wb[bi_], in_=self.s5w[j, gb * 8:(gb + 1) * 8].rearrange("g p k n -> p g (k n)")),
                  self.s5wsem[bi_], r=[("s5w", j)], w=[("s5wb", bi_)])
            return wb[bi_], ("s5wb", bi_)
        for gb in range(16):
            wt, wk = load_wb(gb)
            for half in range(2):
                gs = list(range(gb * 8 + half * 4, gb * 8 + half * 4 + 4))
                for which, dst in ((0, V), (1, Vs)):
                    bk, psb, pskey = self.bank()
                    fns = [lambda e, si=si, g=g, psb=psb, which=which, wt=wt: e.matmul(
                        psb[:, si * 65:si * 65 + ncol], wt[:, g % 8, which * 128:(which + 1) * 128], XG[:, g, 0:ncol], start=True, stop=True) for si, g in enumerate(gs)]
                    S.group("pe", fns, r=allhid + [wk], w=[pskey])
                    S.op("dve", lambda e, g0=gs[0], psb=psb, dst=dst: e.tensor_copy(dst[:, g0:g0 + 4, 0:ncol], psb[:, 0:4 * 65].rearrange("p (g c) -> p g c", c=65)[:, :, 0:ncol]),
                         r=[pskey], w=["s5V"])
        H = [h_[:, :] for h_ in st["H"]]
        Hs = [h_[:, :] for h_ in st["Hs"]]
        cur = st["cur"]
        for jj in range(NCH):
            nxt = 1 - cur
            S.op("dve", lambda e, jj=jj, cur=cur: e.tensor_copy(Hall[:, :, jj], H[cur][:, :]), r=[("H", cur)], w=allxn)
            S.op("dve", lambda e, cur=cur: e.tensor_tensor(st["t1"], A8[0], H[cur], ALU.mult), r=[("H", cur), "s5A"], w=["s5t1"])
            S.op("dve", lambda e, cur=cur: e.tensor_tensor(st["t2"], A8[1], Hs[cur], ALU.mult), r=[("Hs", cur), "s5A"], w=["s5t2"])
            S.op("dve", lambda e: e.tensor_tensor(st["t1"], st["t1"], st["t2"], ALU.add), r=["s5t1", "s5t2"], w=["s5t1"])
            S.op("dve", lambda e, jj=jj, nxt=nxt: e.tensor_tensor(H[nxt], st["t1"], V[:, :, jj], ALU.add), r=["s5t1", "s5V"], w=[("H", nxt)])
            S.op("pool", lambda e, cur=cur: e.tensor_tensor(st["t3"], A8[0], Hs[cur], ALU.mult), r=[("Hs", cur), "s5A"], w=["s5t3"])
            S.op("pool", lambda e, cur=cur: e.tensor_tensor(st["t4"], A8[2], H[cur], ALU.mult), r=[("H", cur), "s5A"], w=["s5t4"])
            S.op("pool", lambda e: e.tensor_tensor(st["t3"], st["t3"], st["t4"], ALU.add), r=["s5t3", "s5t4"], w=["s5t3"])
            S.op("pool", lambda e, jj=jj, nxt=nxt: e.tensor_tensor(Hs[nxt], st["t3"], Vs[:, :, jj], ALU.add), r=["s5t3", "s5V"], w=[("Hs", nxt)])
            cur = nxt
        self.s5state[j]["cur"] = cur
        if ws:
            h0, h0s = st["h0"], st["h0s"]
            hs0, hs0s = st["hs0"], st["hs0s"]
            dv = lambda fn, r, w: S.op("dve", fn, r=r, w=w)
            dv(lambda e: e.tensor_tensor(st["t1"], Am4[0], h0, ALU.mult), ["s5h0", "s5A"], ["s5t1"])
            dv(lambda e: e.tensor_tensor(st["t2"], Am4[1], h0s, ALU.mult), ["s5h0", "s5A"], ["s5t2"])
            dv(lambda e: e.tensor_tensor(hs0, st["t1"], st["t2"], ALU.add), ["s5t1", "s5t2"], ["s5hs0"])
            dv(lambda e: e.tensor_tensor(st["t1"], Am4[0], h0s, ALU.mult), ["s5h0", "s5A"], ["s5t1"])
            dv(lambda e: e.tensor_tensor(st["t2"], Am4[2], h0, ALU.mult), ["s5h0", "s5A"], ["s5t2"])
            dv(lambda e: e.tensor_tensor(hs0s, st["t1"], st["t2"], ALU.add), ["s5t1", "s5t2"], ["s5hs0"])
            dv(lambda e: e.tensor_copy(Hall[:, :, NCH], hs0), ["s5hs0"], allxn)
            dv(lambda e: e.tensor_tensor(st["t1"], A8[0], hs0, ALU.mult), ["s5hs0", "s5A"], ["s5t1"])
            dv(lambda e: e.tensor_tensor(st["t2"], A8[1], hs0s, ALU.mult), ["s5hs0", "s5A"], ["s5t2"])
            dv(lambda e: e.tensor_tensor(st["t1"], st["t1"], st["t2"], ALU.add), ["s5t1", "s5t2"], ["s5t1"])
            dv(lambda e: e.tensor_tensor(st["hsf"], st["t1"], V[:, :, NCH], ALU.add), ["s5t1", "s5V"], ["s5hsf"])
            for src, skey, o_re, o_im in ((H[cur], ("H", cur), "ssm_re_p", "ssm_im_p"), (st["hsf"], "s5hsf", "ssm_re_s", "ssm_im_s")):
                bk, psb, pskey = self.bank()
                S.op("pe", lambda e, psb=psb, src=src: e.transpose(psb[:, 0:128], src, self.ident_f[:]), r=[skey, "const"], w=[pskey])
                so = st["so"]
                S.op("dve", lambda e, psb=psb, so=so: e.tensor_copy(so[:, :], psb[:, 0:128]), r=[pskey], w=["s5so"])
                S.dma("sp", lambda e, so=so, o_re=o_re: e.dma_start(out=self.outs[o_re][j], in_=so[:, 0:64]), self.osem, r=["s5so"])
                S.dma("sp", lambda e, so=so, o_im=o_im: e.dma_start(out=self.outs[o_im][j], in_=so[:, 64:128]), self.osem, r=["s5so"])
        for gb in range(16):
            wt, wk = load_wb(gb)
            for half in range(2):
                gs = list(range(gb * 8 + half * 4, gb * 8 + half * 4 + 4))
                bk, psb, pskey = self.bank()
                fns = []
                for si, g in enumerate(gs):
                    fns.append(lambda e, si=si, g=g, psb=psb, wt=wt: e.matmul(psb[:, si * 65:si * 65 + ncol], wt[:, g % 8, 256:384], XG[:, g, 0:ncol], start=True, stop=False))
                    fns.append(lambda e, si=si, g=g, psb=psb, wt=wt: e.matmul(psb[:, si * 65:si * 65 + ncol], wt[:, g % 8, 384:512], Hall[:, g, 0:ncol], start=False, stop=True))
                S.group("pe", fns, r=allhid + allxn + [wk], w=[pskey])
                S.op("dve", lambda e, g0=gs[0], psb=psb: e.tensor_copy(Yg[:, g0:g0 + 4, 0:ncol], psb[:, 0:4 * 65].rearrange("p (g c) -> p g c", c=65)[:, :, 0:ncol]),
                     r=[pskey, "s5V"], w=["s5Y"])
        gcol = (li * 2) * 16
        for dc in range(16):
            bk, psb, pskey = self.bank()
            fns = []
            for i in range(8):
                for gl in range(8):
                    fns.append(lambda e, i=i, gl=gl, dc=dc, psb=psb: e.matmul(psb[:, i * NCH:(i + 1) * NCH], self.Zb[i][:, 112 - 16 * gl:240 - 16 * gl], Yg[:, dc * 8 + gl, 0:NCH],
                                                                         start=(gl == 0), stop=(gl == 7)))
            S.group("pe", fns, r=["s5Y", "Zb"], w=[pskey])
            if ws:
                bk2, psb2, pskey2 = self.bank()
                fns = []
                for k in range(NS):
                    for gl in range(8):
                        fns.append(lambda e, k=k, gl=gl, dc=dc, psb2=psb2: e.matmul(psb2[:, k:k + 1], self.Zb[4 + k][:, 112 - 16 * gl:240 - 16 * gl], Yg[:, dc * 8 + gl, NCH:NCH + 1],
                                                                               start=(gl == 0), stop=(gl == 7)))
                S.group("pe", fns, r=["s5Y", "Zb"], w=[pskey2])
            u = self.s5u
            y = self.s5y
            t = self.s5t
            S.op("dve", lambda e, dc=dc: e.scalar_tensor_tensor(u[:, 0:ncols], R[:, dc, 0:ncols], self.gv[:, gcol + dc:gcol + dc + 1], self.rstd[:, 0:ncols], ALU.mult, ALU.mult),
                 r=[("R", dc), "rstd", "gv", ("s5wb", 0), ("s5wb", 1)], w=["s5u", ("s5wb", 0)])
            S.op("dve", lambda e, dc=dc, psb=psb: e.scalar_tensor_tensor(y[:, 0:TT].rearrange("p (j i) -> p j i", i=8), u[:, 0:TT].rearrange("p (j i) -> p j i", i=8),
                                                                    self.dvv[:, j * 16 + dc:j * 16 + dc + 1], psb[:, 0:TT].rearrange("p (i j) -> p j i", i=8), ALU.mult, ALU.add),
                 r=["s5u", pskey, "gv"], w=["s5y"])
            if ws:
                S.op("dve", lambda e, dc=dc, psb2=psb2: e.scalar_tensor_tensor(y[:, TT:TT + NS], u[:, TT:TT + NS], self.dvv[:, j * 16 + dc:j * 16 + dc + 1], psb2[:, 0:NS], ALU.mult, ALU.add),
                     r=["s5u", pskey2, "gv"], w=["s5y"])
            S.op("pool", lambda e: e.tensor_tensor(t[:, 0:ncols], y[:, 0:ncols], y[:, 0:ncols], ALU.mult), r=["s5y"], w=["s5t"])
            S.op("dve", lambda e: e.tensor_scalar(t[:, 0:ncols], t[:, 0:ncols], 0.044715, 1.0, ALU.mult, ALU.add), r=["s5t"], w=["s5t"])
            S.op("dve", lambda e: e.tensor_tensor(t[:, 0:ncols], t[:, 0:ncols], y[:, 0:ncols], ALU.mult), r=["s5t", "s5y"], w=["s5t"])
            S.op("act", lambda e: e.activation(t[:, 0:ncols], t[:, 0:ncols], AF.Sigmoid, scale=1.5957691216057308), r=["s5t"], w=["s5t"])
            S.op("dve", lambda e, dc=dc: e.tensor_tensor(hid[:, dc, 0:ncols], y[:, 0:ncols], t[:, 0:ncols], ALU.mult), r=["s5t", "s5y", "s5Y"], w=[("hid", dc)])
        wg = self.ins["ssm_w_glu"][j]

        def ev_g(m, n0, n, ps, pskey):
            S.op("act", lambda e: e.activation(F2[:, m, n0:n0 + n], ps, AF.Sigmoid), r=[pskey, "s5Y"], w=[("F2", m)])
        self.linear_fm(wg, 0, 2048, 16, hid, "hid", ncols, ev_g)

        def ev_a(m, n0, n, ps, pskey):
            tt = self.tmpf[self.tmpf_i % 2]
            tk = ("tmpf", self.tmpf_i % 2)
            self.tmpf_i += 1
            S.op("dve", lambda e: e.tensor_tensor(tt[:, 0:n], ps, F2[:, m, n0:n0 + n], ALU.mult), r=[pskey, ("F2", m)], w=[tk])
            S.op("dve", lambda e: e.tensor_tensor(R[:, m, n0:n0 + n], R[:, m, n0:n0 + n], tt[:, 0:n], ALU.add), r=[tk, ("R", m)], w=[("R", m)])
        self.linear_fm(wg, 0, 0, 16, hid, "hid", ncols, ev_a)

    def obank(self, i):
        return i, self.obanks[i], ("ops", i)

    def av(self, off, shape, dt):
        n = 1
        for d in shape[1:]:
            n *= d
        if dt == BF16:
            ap = self.arena[:, off // 2: off // 2 + n]
        else:
            ap = self.arena[:, off // 2: off // 2 + 2 * n].bitcast(dt)
        if len(shape) == 3:
            ap = ap.rearrange("p (a b) -> p a b", a=shape[1])
        self._av_end = max(getattr(self, "_av_end", 0), off + n * (2 if dt == BF16 else 4))
        assert self._av_end <= self.ARENA_BYTES, self._av_end
        return ap

    def barrier(self):
        S = self.S
        snap = [(S.sem[e], S.cnt[e]) for e in S.ENG if S.cnt[e]]
        dsn = [(d.h, d.count) for d in self.all_dsems if d.count]
        for eng in S.ENG:
            waits = []
            for s_, v in snap + dsn:
                if S.waited[eng].get(s_.name, 0) < v:
                    S.waited[eng][s_.name] = v
                    waits.append((s_, v))

            def emit(e, waits=waits):
                for s_, v in waits:
                    e.wait_ge(s_, v)
            S.ops[eng].append(emit)

    def build(self):
        nc = self.nc
        es = self.es
        S = self.S = Sched(nc, es)
        self.all_dsems = []
        _ds = S.dsem

        def dsem(name=None):
            d = _ds(name)
            self.all_dsems.append(d)
            return d
        S.dsem = dsem
        self.din("xp", [SEQ, D])
        self.din("xs", [NS, D])
        self.din("norm_mix", [DEPTH, D])
        self.din("norm_mlp", [DEPTH, D])
        self.din("mlp_w_up", [DEPTH, D, FF])
        self.din("mlp_w_down", [DEPTH, FF, D])
        self.din("conv_w_in", [1, D, 3 * D])
        self.din("conv_w_dw", [1, 3, D])
        self.din("conv_w_out", [1, D, D])
        self.din("sconv_in", [2, D])
        self.din("attn_w_qkv", [1, D, 3 * D])
        self.din("attn_q_norm", [1, 128])
        self.din("attn_k_norm", [1, 128])
        self.din("attn_sb_bias", [1, 16])
        self.din("attn_w_o", [1, D, D])
        import os
        self.sample_on = self.stage >= 4 and not os.environ.get("NOSAMPLE")
        if self.sample_on:
            self.din("cache_k", [1280, 128, 16, 128])
            self.din("cache_v", [1280, 128, 16, 128])
        self.din("pt", [1, NPAGES], I32)
        for nm, shp in (("ssm_lambda_re", [2, 128, 64]), ("ssm_lambda_im", [2, 128, 64]), ("ssm_log_dt", [2, 128]),
                        ("ssm_b_re", [2, 128, 64, 16]), ("ssm_b_im", [2, 128, 64, 16]), ("ssm_c_re", [2, 128, 16, 64]),
                        ("ssm_c_im", [2, 128, 16, 64]), ("ssm_d", [2, D]), ("ssm_w_glu", [2, D, 2 * D]),
                        ("sst_re", [2, 128, 64]), ("sst_im", [2, 128, 64]), ("c_blkmask", [128, 128])):
            self.din(nm, shp)
        self.din("c_ident", [128, 128])
        self.din("c_tri", [128, 128])
        self.din("c_masks", [128, TT // 128, TT])
        self.din("c_smask", [128, 16 * NS])
        self.dout("yp", [SEQ, D])
        self.dout("ys", [NS, D])
        self.dout("conv_p", [2, D])
        self.dout("conv_s", [2, D])
        for nm in ("ssm_re_p", "ssm_im_p", "ssm_re_s", "ssm_im_s"):
            self.dout(nm, [2, 128, 64])
        import os
        if os.environ.get("DEBUG_SA"):
            self.dout("dbg_pgk", [128, D])
            self.dout("dbg_pgv", [128, D], BF16)
            self.dout("dbg_carry", [128, 16 * NS])
            self.dout("dbg_o", [128, 16 * NS])
            self.dout("dbg_q", [128, 16, NS], BF16)
            self.dout("dbg_ktp", [128, 16, 128], BF16)
        self.dout("kp", [SEQ, D])
        self.dout("vp", [SEQ, D])
        self.dout("ks", [NS, D])
        self.dout("vs", [NS, D])
        self.kT_scr = nc.dram_tensor("kT_scr", [16, 128, SEQ], BF16).ap()
        self.v_scr = nc.dram_tensor("v_scr", [SEQ, D], BF16).ap()
        self.efg = nc.dram_tensor("efg_scr", [128, 3, 128, 128], F32).ap()
        self.s5w = nc.dram_tensor("s5w_scr", [2, 128, 128, 4, 128], BF16).ap()
        self.s5A_scr = nc.dram_tensor("s5A_scr", [2, 6, 128, 128], F32).ap()
        self.R = self.sb("R", [128, 16, NT], F32)
        self.xn = self.sb("xn", [128, 16, NT + 4], BF16)
        self.hid = self.sb("hid", [128, 16, NT + 4], BF16)
        self.wbuf = [self.sb("w%d" % i, [128, 16, SW], BF16) for i in range(2)]
        self.wsem = [S.dsem("ws%d" % i) for i in range(2)]
        self.w_i = 0
        self.rstd = self.sb("rstd", [128, NT], F32)
        self.sq = [self.sb("sq0", [128, NT], F32)]
        self.tmpb = [self.sb("tmpb%d" % i, [128, 512], BF16) for i in range(2)]
        self.tmp_i = 0
        self.tmpf = [self.sb("tmpf%d" % i, [128, 512], F32) for i in range(2)]
        self.tmpf_i = 0
        self.stage_f = [self.sb("stage0", [128, D], F32)] * 2
        self.ssem = [S.dsem("ss%d" % i) for i in range(2)]
        self.osem = S.dsem("osem")
        self.scrsem = S.dsem("scrsem")
        self.pvsem = [S.dsem("pv%d" % i) for i in range(2)]
        self.pgsem = [S.dsem("pg%d" % i) for i in range(2)]
        self.pgsemV = [S.dsem("pgv%d" % i) for i in range(2)]
        self.osem_n = [S.dsem("on%d" % i) for i in range(2)]
        self.pvsem_v = S.dsem("pvv")
        self.s5sem_n = S.dsem("s5n")
        self.gv = self.sb("gv", [128, 16 * 8], F32)
        self.dwv = self.sb("dwv", [128, 48], F32)
        self.ccarry = self.sb("ccarry", [128, 16, 2], F32)
        self.sconv = self.sb("sconv", [128, 16, 2], F32)
        self.ident_f = self.sb("ident_f", [128, 128], F32)
        self.ones_f = self.sb("ones_f", [128, 128], F32)
        self.ones_b = self.sb("ones_b", [128, 128], BF16)
        self.tri_f = self.sb("tri_f", [128, 128], F32)
        self.tri_b = self.sb("tri_b", [128, 128], BF16)
        self.eps_t = self.sb("eps_t", [128, 1], F32)
        self.one_t = self.sb("one_t", [128, 1], F32)
        self.qgain = self.sb("qgain", [128, 128], F32)
        self.kgain = self.sb("kgain", [128, 128], F32)
        self.abias = self.sb("abias", [128, 16], F32)
        self.sbiasrow = self.sb("sbiasrow", [128, 16 * NS], F32)
        self.ptsb = self.sb("ptsb", [128, NPAGES], I32)
        self.pidx = self.sb("pidx", [128, NPAGES], I32)
        self.iota_p = self.sb("iota_p", [128, 1], I32)
        self.s5sem = S.dsem("s5sem")
        self.s5sem2 = [S.dsem("s5e%d" % i) for i in range(2)]
        self.s5sem3 = S.dsem("s5sem3")
        self.s5wsem = [S.dsem("s5w%d" % i) for i in range(2)]
        self.Zb = [self.sb("Zb%d" % a, [128, 240], BF16) for a in range(8)]
        self.blkmask = self.sb("blkmask", [128, 128], F32)
        self.hpi_t = self.sb("hpi_t", [128, 1], F32)
        self.dvv = self.sb("dvv", [128, 32], F32)
        self.s5state = []
        for jj in range(2):
            self.s5state.append({"H": [self.sb("H%d_%d" % (jj, i), [128, 128], F32) for i in range(2)],
                                 "Hs": [self.sb("Hs%d_%d" % (jj, i), [128, 128], F32) for i in range(2)], "cur": 0})
        self.ARENA_BYTES = 93 * 1024
        self.arena = self.sb("arena", [128, self.ARENA_BYTES // 2], BF16)
        allb = [self.ps("bank%d" % i, [128, 512], F32) for i in range(8)]
        self.banks = allb[0:6]
        self.obanks = allb[6:8]
        self.bank_i = 0
        self.ab_i = 0
        self.ab_c = 0
        self.nq_i = 0
        self.wbig = [self.av(i * 16384, [128, 16, 512], BF16) for i in range(5)]
        self.wbigsem = [S.dsem("wbg%d" % i) for i in range(5)]
        self.wb_i = 0
        self.F1 = self.av(0, [128, 16, TT + 8], F32)
        o = 0
        self.qT = self.av(o, [128, 16, NT], BF16); o += 16 * NT * 2
        self.kT = self.av(o, [128, 16, NT], BF16); o += 16 * NT * 2
        self.Vt = self.av(o, [128, TT // 128 + 1, D], BF16); o += (TT // 128 + 1) * D * 2
        npv = SEQ - TT
        self.kprev = [self.av(o, [128, npv], BF16)] * 2; o += npv * 2
        self.vprev = [self.av(o, [128, npv // 128, 128], BF16)] * 2; o += npv * 2
        self.abLb = [self.av(o + i * 1024, [128, 512], BF16) for i in range(2)]; o += 2048
        self.abA = [self.av(o + i * 1024, [128, 512], BF16) for i in range(2)]; o += 2048
        self.masks = self.av(o, [128, TT // 128, TT], BF16); o += (TT // 128) * TT * 2
        self.smask = self.av(o, [128, 16 * NS], BF16); o += 16 * NS * 2
        self.pgV = [self.av(o + i * 4096, [128, D], BF16) for i in range(2)]; o += 8192
        self.pgVb = self.pgV
        self.KTp = [self.av(o, [128, 16, 128], BF16)] * 2; o += 4096
        self.abE = [self.av(o, [128, 512], F32)] * 2; o += 2048
        self.abL = [self.av(o, [128, 512], F32)] * 2; o += 2048
        self.abT = [self.av(o, [128, 512], F32)] * 2; o += 2048
        self.carry = [self.av(o + i * 2048, [128, 512], F32) for i in range(2)]; o += 4096
        self.scarry = self.av(o, [128, 16 * NS], F32); o += 16 * NS * 4
        self.soacc = self.av(o, [128, 16 * NS], F32); o += 16 * NS * 4
        self.nsq = [self.av(o + i * 1024, [128, SW], F32) for i in range(2)]; o += 2048
        self.nrm = [self.av(o + i * 1024, [128, SW], F32) for i in range(2)]; o += 2048
        self.nss = [self.av(o + i * 8, [128, 2], F32) for i in range(2)]; o += 16
        self.pgK = self.stage_f

        self.csem = None
        S.dma("sp", lambda e: e.dma_start(out=self.ident_f[:], in_=self.ins["c_ident"]), S.dsem(), w=["const0"])
        S.dma("sp", lambda e: e.dma_start(out=self.tri_f[:], in_=self.ins["c_tri"]), S.dsem(), w=["const2"])
        S.op("dve", lambda e: e.memset(self.ones_f[:], 1.0), w=["const1"])
        S.op("dve", lambda e: e.memset(self.ones_b[:], 1.0), w=["const3"])
        S.op("dve", lambda e: e.memset(self.one_t[:], 1.0), w=["const4"])
        S.op("dve", lambda e: e.tensor_copy(self.tri_b[:], self.tri_f[:]), r=["const2"], w=["const5"])
        S.op("dve", lambda e: e.memset(self.eps_t[:], EPS), r=["const0", "const1", "const3", "const4", "const5"], w=["const"])
        S.op("dve", lambda e: e.tensor_copy(self.eps_t[:], self.eps_t[:]), r=["const"], w=["const"])
        for i in range(DEPTH):
            for k, nm in enumerate(("norm_mix", "norm_mlp")):
                col = (i * 2 + k) * 16
                src = self.ins[nm][i].rearrange("(c p) -> p c", p=128)
                S.dma("sp", lambda e, col=col, src=src: e.dma_start(out=self.gv[:, col:col + 16], in_=src, allow_slow_non_contiguous=True),
                      S.dsem(), w=["gv"])
        for k in range(3):
            src = self.ins["conv_w_dw"][0, k].rearrange("(c p) -> p c", p=128)
            S.dma("sp", lambda e, k=k, src=src: e.dma_start(out=self.dwv[:, k * 16:(k + 1) * 16], in_=src, allow_slow_non_contiguous=True), S.dsem(), w=["gv"])
        S.dma("sp", lambda e: e.dma_start(out=self.qgain[:], in_=self.ins["attn_q_norm"][0].partition_broadcast(128)), S.dsem(), w=["gains"])
        S.dma("sp", lambda e: e.dma_start(out=self.kgain[:], in_=self.ins["attn_k_norm"][0].partition_broadcast(128)), S.dsem(), w=["gains"])
        S.dma("sp", lambda e: e.dma_start(out=self.abias[:], in_=self.ins["attn_sb_bias"][0].partition_broadcast(128)), S.dsem(), w=["abias0"])
        for h in range(16):
            S.op("dve", lambda e, h=h: e.tensor_scalar(self.sbiasrow[:, h * NS:(h + 1) * NS], self.ones_f[:, 0:NS], self.abias[:, h:h + 1], None, ALU.mult),
                 r=["abias0", "const"], w=["abias"])
        S.dma("pool", lambda e: e.dma_start(out=self.ptsb[:], in_=self.ins["pt"][0].partition_broadcast(128)), S.dsem(), w=["pt"])
        S.op("pool", lambda e: e.iota(self.iota_p[:], pattern=[[0, 1]], base=0, channel_multiplier=1), w=["iota"])
        S.op("pool", lambda e: e.tensor_scalar(self.pidx[:], self.ptsb[:], 128, self.iota_p[:, 0:1], ALU.mult, ALU.add), r=["pt", "iota"], w=["pidx"])
        S.op("pool", lambda e: e.memset(self.ccarry[:, :, :], 0.0), w=["ccarry"])
        S.dma("sp", lambda e: e.dma_start(out=self.blkmask[:], in_=self.ins["c_blkmask"]), S.dsem(), w=["const6"])
        S.op("dve", lambda e: e.memset(self.hpi_t[:], float(np.pi / 2)), r=["const6"], w=["const7"])
        for a in range(8):
            S.op("dve", lambda e, a=a: e.memset(self.Zb[a][:], 0.0), w=["Zb"])
            S.op("dve", lambda e, a=a: e.tensor_copy(self.Zb[a][:, 112:128], self.ident_f[:, a * 16:(a + 1) * 16]), r=["const", "Zb"], w=["Zb"])
        for jj in range(2):
            src = self.ins["ssm_d"][jj].rearrange("(c p) -> p c", p=128)
            S.dma("sp", lambda e, jj=jj, src=src: e.dma_start(out=self.dvv[:, jj * 16:(jj + 1) * 16], in_=src, allow_slow_non_contiguous=True), S.dsem(), w=["gv"])
            for i in range(2):
                S.op("dve", lambda e, jj=jj, i=i: e.memset(self.s5state[jj]["H"][i][:], 0.0), w=[("H", i)])
                S.op("pool", lambda e, jj=jj, i=i: e.memset(self.s5state[jj]["Hs"][i][:], 0.0), w=[("Hs", i)])
        self.barrier()
        if self.stage >= 5:
            for jj in range(2):
                self.barrier()
                self.s5_prep(jj)
            self.barrier()
        self.load_cols(self.ins["sconv_in"], 2, lambda: self.sconv[:, :, :], ["sconv"])

        for tile in range(NTILES):
            ws = (tile == NTILES - 1)
            ncols = NT if ws else TT
            self.load_tile(tile, ws)
            for li in range(DEPTH):
                kind = li % 3
                if kind == 0 and self.stage >= 5:
                    self.rmsnorm((li * 2) * 16, ncols)
                    self.barrier()
                    self.s5_apply(li // 3, li, tile, ncols, ws)
                    self.barrier()
                if (kind == 1 and self.stage >= 2) or (kind == 2 and self.stage >= 3):
                    self.rmsnorm((li * 2) * 16, ncols)
                    self.barrier()
                    if kind == 1:
                        self.conv_mixer(tile, ncols, ws)
                    else:
                        if tile == 0:
                            S.dma("sp", lambda e: e.dma_start(out=self.masks[:, :, :], in_=self.ins["c_masks"]), csem, w=["masks"]) if False else None
                        self.attn_consts()
                        self.attention(tile, ncols, ws)
                    self.barrier()
                self.rmsnorm((li * 2 + 1) * 16, ncols)
                self.barrier()
                self.mlp(li, ncols)
                self.barrier()
                if self.stage == 0:
                    break
            self.store_tile(tile, ws)
        S.final_wait("sp", self.all_dsems)
        S.flush()
        return nc

    def attn_consts(self):
        S = self.S
        st = self.stage_f[0]
        for kb in range(TT // 128):
            S.dma("sp", lambda e, kb=kb: e.dma_start(out=st[:, 0:TT], in_=self.ins["c_masks"][:, kb, :]), self.ssem[0], w=[("stage", 0)])
            S.op("dve", lambda e, kb=kb: e.tensor_copy(self.masks[:, kb, :], st[:, 0:TT]), r=[("stage", 0)], w=["masks"])
        S.dma("sp", lambda e: e.dma_start(out=st[:, 0:16 * NS], in_=self.ins["c_smask"]), self.ssem[0], w=[("stage", 0)])
        S.op("dve", lambda e: e.tensor_copy(self.smask[:, :], st[:, 0:16 * NS]), r=[("stage", 0)], w=["masks"])
        S.op("pool", lambda e: e.memset(self.Vt[:, TT // 128, :], 0.0), w=[("Vt", TT // 128)])


_CONST = {}


def _consts():
    if not _CONST:
        _CONST["c_ident"] = np.eye(128, dtype=np.float32)
        jj = np.arange(128)
        _CONST["c_tri"] = (jj[:, None] >= jj[None, :]).astype(np.float32)
        t = np.arange(TT)
        m = np.zeros((128, TT // 128, TT), np.float32)
        for kb in range(TT // 128):
            m[:, kb, :] = ((kb * 128 + jj)[:, None] < t[None, :])
        _CONST["c_masks"] = m
        sm = np.zeros((128, 16 * NS), np.float32)
        for h in range(16):
            for q in range(NS):
                sm[:q, h * NS + q] = 1.0
        _CONST["c_smask"] = sm
        ii = jj // 16
        _CONST["c_blkmask"] = (ii[None, :] >= ii[:, None]).astype(np.float32)

    return _CONST


_SHARED = ("ssm_lambda_re", "ssm_lambda_im", "ssm_log_dt", "ssm_b_re", "ssm_b_im", "ssm_c_re", "ssm_c_im", "ssm_d", "ssm_w_glu",
           "norm_mix", "norm_mlp", "mlp_w_up", "mlp_w_down", "conv_w_in", "conv_w_dw", "conv_w_out",
           "attn_w_qkv", "attn_q_norm", "attn_k_norm", "attn_sb_bias", "attn_w_o")


def make_in_maps(inputs, cores, with_cache=True):
    cst = _consts()
    ck = np.ascontiguousarray(inputs["cache_k"][0]) if with_cache else None
    cv = np.ascontiguousarray(inputs["cache_v"][0]) if with_cache else None
    in_maps = []
    for c in cores:
        m = {
            "xp": np.ascontiguousarray(inputs["x_prompt"][c % 4]),
            "xs": np.ascontiguousarray(inputs["x_sample"][c]),
            "sconv_in": np.ascontiguousarray(inputs["state_conv"][0, c]),
            "pt": np.ascontiguousarray(inputs["page_table"][c:c + 1]).astype(np.int32),
            "sst_re": np.ascontiguousarray(inputs["state_ssm_re"][:, c]),
            "sst_im": np.ascontiguousarray(inputs["state_ssm_im"][:, c]),
        }
        if with_cache:
            m["cache_k"] = ck
            m["cache_v"] = cv
        for k in _SHARED:
            m[k] = inputs[k]
        m.update(cst)
        in_maps.append(m)
    return in_maps


def kernel(**inputs):
    inputs = {k: np.asarray(v) for k, v in inputs.items()}
    b = Builder()
    nc = b.build()
    in_maps = make_in_maps(inputs, list(range(8)))
    res = run_bass_kernel_spmd(nc, in_maps, core_ids=list(range(8)))
    rs = res.results
    f = np.float32
    y_p = np.stack([rs[c]["yp"] for c in range(4)]).astype(f)
    y_s = np.stack([rs[c]["ys"] for c in range(8)]).astype(f)
    ssm_re_p = np.stack([rs[c]["ssm_re_p"] for c in range(4)], axis=1).astype(f)
    ssm_im_p = np.stack([rs[c]["ssm_im_p"] for c in range(4)], axis=1).astype(f)
    ssm_re_s = np.stack([rs[c]["ssm_re_s"] for c in range(8)], axis=1).astype(f)
    ssm_im_s = np.stack([rs[c]["ssm_im_s"] for c in range(8)], axis=1).astype(f)
    conv_p = np.stack([rs[c]["conv_p"] for c in range(4)])[None].astype(f)
    conv_s = np.stack([rs[c]["conv_s"] for c in range(8)])[None].astype(f)
    k_p = np.stack([rs[c]["kp"] for c in range(4)]).reshape(1, 4, SEQ, NH, 128).astype(f)
    v_p = np.stack([rs[c]["vp"] for c in range(4)]).reshape(1, 4, SEQ, NH, 128).astype(f)
    k_s = np.stack([rs[c]["ks"] for c in range(8)]).reshape(1, 8, NS, NH, 128).astype(f)
    v_s = np.stack([rs[c]["vs"] for c in range(8)]).reshape(1, 8, NS, NH, 128).astype(f)
    return (y_p, y_s, ssm_re_p, ssm_im_p, ssm_re_s, ssm_im_s, conv_p, conv_s, k_p, v_p, k_s, v_s)
```
